# Optimizing a Trainium2 kernel written in Bass

```python
import math
import jax, jax.numpy as jnp
from jax import lax
import numpy as np

D_MODEL = 1024
BATCH = 8
SEQ = 2048
DEPTH = 1
DEC_BATCH = 32
DEC_SEQ = 8
PAST_LEN = 8192
PAGE_SIZE = 128

HEAD_DIM = 64
N_HEADS_MOBA = 8
N_HEADS_SB = 8
W_MOBA = N_HEADS_MOBA * HEAD_DIM
W_SB = N_HEADS_SB * HEAD_DIM
MOBA_BLOCK = 256
MOBA_TOPK = 3
Q_BLOCK = 128
ROPE_THETA = 10000.0
D_FF = ((8 * D_MODEL // 3 + 255) // 256) * 256
RMS_EPS = 1e-6
D_IN = 3 * W_MOBA + 3 * W_SB + 2 * D_MODEL

kernel_name = 'hybrid_moba_stickbreaking_decode_step'


def rms_norm(x, g):
    xf = x.astype(jnp.float32)
    y = xf * lax.rsqrt(jnp.mean(xf * xf, axis=-1, keepdims=True) + RMS_EPS)
    return (y * g.astype(jnp.float32)).astype(x.dtype)


def rope(x, pos):
    half = HEAD_DIM // 2
    inv = ROPE_THETA ** (-jnp.arange(half, dtype=jnp.float32) / half)
    ang = pos.astype(jnp.float32)[:, None] * inv[None, :]
    cos = jnp.cos(ang)[None, :, None, :]
    sin = jnp.sin(ang)[None, :, None, :]
    xf = x.astype(jnp.float32)
    x1, x2 = xf[..., :half], xf[..., half:]
    out = jnp.concatenate([x1 * cos - x2 * sin, x2 * cos + x1 * sin], axis=-1)
    return out.astype(x.dtype)


def moba_block(q, k, v, q_pos):
    H, L, d = k.shape
    qc = q.shape[1]
    nb = L // MOBA_BLOCK
    kb = k.reshape(H, nb, MOBA_BLOCK, d)
    vb = v.reshape(H, nb, MOBA_BLOCK, d)
    k_mean = jnp.mean(kb.astype(jnp.float32), axis=2)
    cur = q_pos // MOBA_BLOCK
    gate = jnp.einsum('hqd,hnd->hqn', q.astype(jnp.float32), k_mean)
    full_past = jnp.arange(nb, dtype=jnp.int32)[None, :] < cur[:, None]
    gate = jnp.where(full_past[None], gate, -jnp.inf)
    n_sel = min(MOBA_TOPK, nb)
    _, top = lax.top_k(gate, n_sel)
    top_ok = top < cur[None, :, None]
    own = jnp.broadcast_to(cur[None, :, None], (H, qc, 1))
    sel = jnp.concatenate([top, own], axis=-1)
    sel_ok = jnp.concatenate([top_ok, jnp.ones((H, qc, 1), dtype=bool)], axis=-1)
    h_idx = jnp.arange(H)[:, None, None]
    k_sel = kb[h_idx, sel]
    v_sel = vb[h_idx, sel]
    s = jnp.einsum('hqd,hqnkd->hqnk', q, k_sel).astype(jnp.float32) / math.sqrt(d)
    k_pos = sel[..., None] * MOBA_BLOCK + jnp.arange(MOBA_BLOCK, dtype=jnp.int32)
    mask = sel_ok[..., None] & (k_pos <= q_pos[None, :, None, None])
    s = jnp.where(mask, s, -jnp.inf)
    p = jax.nn.softmax(s.reshape(H, qc, -1), axis=-1).reshape(s.shape)
    return jnp.einsum('hqnk,hqnkd->hqd', p.astype(v.dtype), v_sel)


def stick_breaking_block(q, k, v, q_pos):
    L = k.shape[1]
    z = jnp.einsum('hqd,hkd->hqk', q, k).astype(jnp.float32) / math.sqrt(HEAD_DIM)
    past = jnp.arange(L, dtype=jnp.int32)[None, :] < q_pos[:, None]
    log_keep = jnp.where(past[None], jax.nn.log_sigmoid(-z), 0.0)
    tail = lax.cumsum(log_keep, axis=2, reverse=True) - log_keep
    a = jnp.where(past[None], jnp.exp(jax.nn.log_sigmoid(z) + tail), 0.0)
    return jnp.einsum('hqk,hkd->hqd', a.astype(v.dtype), v)


def sweep_query_blocks(fn, q, k, v, q_start):
    B, H, T, d = q.shape
    qc = math.gcd(T, Q_BLOCK)
    nc = T // qc
    qb = q.reshape(B, H, nc, qc, d).transpose(0, 2, 1, 3, 4).reshape(B * nc, H, qc, d)
    idx = jnp.arange(B * nc, dtype=jnp.int32)

    def step(args):
        q_blk, i = args
        b, c = i // nc, i % nc
        q_pos = q_start + c * qc + jnp.arange(qc, dtype=jnp.int32)
        return fn(q_blk, k[b], v[b], q_pos)

    out = lax.map(step, (qb, idx))
    return out.reshape(B, nc, H, qc, d).transpose(0, 2, 1, 3, 4).reshape(B, H, T, d)


def gather_pages(cache, page_table):
    g = cache[page_table]
    nb, npg, ps, h, d = g.shape
    return g.reshape(nb, npg * ps, h, d)


def decoder_layer(x, pos_start, past, w):
    (g_mix, w_in, w_branch_moba, w_branch_sb, w_out,
     g_ffn, w_ffn_gate, w_ffn_up, w_ffn_down) = w
    B, T, _ = x.shape
    h = rms_norm(x, g_mix)
    proj = jnp.einsum('btd,de->bte', h, w_in)
    cuts = [W_MOBA, 2 * W_MOBA, 3 * W_MOBA, 3 * W_MOBA + W_SB,
            3 * W_MOBA + 2 * W_SB, 3 * W_MOBA + 3 * W_SB, 3 * W_MOBA + 3 * W_SB + D_MODEL]
    qa, ka, va, qs, ks, vs, ga, gs = jnp.split(proj, cuts, axis=-1)
    pos = pos_start + jnp.arange(T, dtype=jnp.int32)
    qa = rope(qa.reshape(B, T, N_HEADS_MOBA, HEAD_DIM), pos)
    ka = rope(ka.reshape(B, T, N_HEADS_MOBA, HEAD_DIM), pos)
    va = va.reshape(B, T, N_HEADS_MOBA, HEAD_DIM)
    qs = qs.reshape(B, T, N_HEADS_SB, HEAD_DIM)
    ks = ks.reshape(B, T, N_HEADS_SB, HEAD_DIM)
    vs = vs.reshape(B, T, N_HEADS_SB, HEAD_DIM)
    new_rows = (ka, va, ks, vs)
    if past is None:
        full = new_rows
    else:
        full = tuple(jnp.concatenate([p.astype(n.dtype), n], axis=1) for p, n in zip(past, new_rows))
    bhtd = lambda t: t.transpose(0, 2, 1, 3)
    ka_f, va_f, ks_f, vs_f = (bhtd(t) for t in full)
    L = ka_f.shape[2]
    pad = (-L) % MOBA_BLOCK
    ka_f = jnp.pad(ka_f, ((0, 0), (0, 0), (0, pad), (0, 0)))
    va_f = jnp.pad(va_f, ((0, 0), (0, 0), (0, pad), (0, 0)))
    o_a = sweep_query_blocks(moba_block, bhtd(qa), ka_f, va_f, pos_start)
    o_s = sweep_query_blocks(stick_breaking_block, bhtd(qs), ks_f, vs_f, pos_start)
    o_a = o_a.transpose(0, 2, 1, 3).reshape(B, T, W_MOBA)
    o_s = o_s.transpose(0, 2, 1, 3).reshape(B, T, W_SB)
    merged = (jax.nn.sigmoid(ga) * jnp.einsum('btw,wd->btd', o_a, w_branch_moba)
              + jax.nn.sigmoid(gs) * jnp.einsum('btw,wd->btd', o_s, w_branch_sb))
    x = x + jnp.einsum('btd,de->bte', merged, w_out)
    h2 = rms_norm(x, g_ffn)
    ff = jax.nn.silu(jnp.einsum('btd,df->btf', h2, w_ffn_gate)) * jnp.einsum('btd,df->btf', h2, w_ffn_up)
    x = x + jnp.einsum('btf,fd->btd', ff, w_ffn_down)
    return x, new_rows


def setup_inputs(seed: int = 0) -> dict:
    key = jax.random.key(seed)
    ks = jax.random.split(key, 20)
    n_pages = PAST_LEN // PAGE_SIZE
    n_used = DEC_BATCH * n_pages
    n_pool = n_used + n_used // 4
    f32 = jnp.float32

    def nrm(k, shape, scale):
        return jax.random.normal(k, shape, f32) * scale

    cshape_a = (DEPTH, n_pool, PAGE_SIZE, N_HEADS_MOBA, HEAD_DIM)
    cshape_s = (DEPTH, n_pool, PAGE_SIZE, N_HEADS_SB, HEAD_DIM)
    page_table = jax.random.permutation(ks[6], n_pool)[:n_used].reshape(DEC_BATCH, n_pages).astype(jnp.int32)
    return {
        'x_prompt': nrm(ks[0], (BATCH, SEQ, D_MODEL), 1.0),
        'x_sample': nrm(ks[1], (DEC_BATCH, DEC_SEQ, D_MODEL), 1.0),
        'cache_moba_k': nrm(ks[2], cshape_a, 1.0),
        'cache_moba_v': nrm(ks[3], cshape_a, 1.0),
        'cache_sb_k': nrm(ks[4], cshape_s, 1.0),
        'cache_sb_v': nrm(ks[5], cshape_s, 1.0),
        'page_table': page_table,
        'g_mix': 1.0 + nrm(ks[7], (DEPTH, D_MODEL), 0.02),
        'w_in': nrm(ks[8], (DEPTH, D_MODEL, D_IN), D_MODEL ** -0.5),
        'w_branch_moba': nrm(ks[9], (DEPTH, W_MOBA, D_MODEL), W_MOBA ** -0.5),
        'w_branch_sb': nrm(ks[10], (DEPTH, W_SB, D_MODEL), W_SB ** -0.5),
        'w_out': nrm(ks[11], (DEPTH, D_MODEL, D_MODEL), D_MODEL ** -0.5),
        'g_ffn': 1.0 + nrm(ks[12], (DEPTH, D_MODEL), 0.02),
        'w_ffn_gate': nrm(ks[13], (DEPTH, D_MODEL, D_FF), D_MODEL ** -0.5),
        'w_ffn_up': nrm(ks[14], (DEPTH, D_MODEL, D_FF), D_MODEL ** -0.5),
        'w_ffn_down': nrm(ks[15], (DEPTH, D_FF, D_MODEL), D_FF ** -0.5),
        'g_final': 1.0 + nrm(ks[16], (D_MODEL,), 0.02),
    }


def reference(x_prompt, x_sample, cache_moba_k, cache_moba_v, cache_sb_k, cache_sb_v,
              page_table, g_mix, w_in, w_branch_moba, w_branch_sb, w_out,
              g_ffn, w_ffn_gate, w_ffn_up, w_ffn_down, g_final):
    past_len = page_table.shape[1] * PAGE_SIZE
    xp, xs = x_prompt, x_sample
    rows_p = ([], [], [], [])
    rows_s = ([], [], [], [])
    for l in range(DEPTH):
        w = (g_mix[l], w_in[l], w_branch_moba[l], w_branch_sb[l], w_out[l],
             g_ffn[l], w_ffn_gate[l], w_ffn_up[l], w_ffn_down[l])
        xp, new_p = decoder_layer(xp, 0, None, w)
        past = (gather_pages(cache_moba_k[l], page_table), gather_pages(cache_moba_v[l], page_table),
                gather_pages(cache_sb_k[l], page_table), gather_pages(cache_sb_v[l], page_table))
        xs, new_s = decoder_layer(xs, past_len, past, w)
        for acc, r in zip(rows_p, new_p):
            acc.append(r)
        for acc, r in zip(rows_s, new_s):
            acc.append(r)
    y_prompt = rms_norm(xp, g_final)
    y_sample = rms_norm(xs, g_final)
    return (y_prompt, y_sample,
            jnp.stack(rows_p[0]), jnp.stack(rows_p[1]), jnp.stack(rows_p[2]), jnp.stack(rows_p[3]),
            jnp.stack(rows_s[0]), jnp.stack(rows_s[1]), jnp.stack(rows_s[2]), jnp.stack(rows_s[3]))
```

```python
import numpy as np
import ml_dtypes
from contextlib import ExitStack
import concourse.bass as bass
import concourse.mybir as mybir
from concourse.bass_utils import run_bass_kernel_spmd

F32 = mybir.dt.float32
BF16 = mybir.dt.bfloat16
I32 = mybir.dt.int32
AF = mybir.ActivationFunctionType
ALU = mybir.AluOpType
AX = mybir.AxisListType

ENGS = ("pe", "act", "dve", "pool", "sp")
NCORES = 8
D = 1024
SEQ = 2048
NT = 16
ST = 32
NTOK = SEQ + ST
DFF = 2816
NFC = 22
NPOOL = 2560
NPAGES = 64
NEG = -30000.0
EPS = 1e-6


class Op:
    __slots__ = ("eng", "fn", "reads", "writes", "dma", "key", "deps", "needed", "token", "idx")

    def __init__(self, eng, fn, reads, writes, dma, key):
        self.eng, self.fn, self.reads, self.writes, self.dma, self.key = eng, fn, reads, writes, dma, key
        self.deps = []
        self.needed = False
        self.token = None


class _Rec:
    call = None

    def __getattr__(self, name):
        def f(*a, **k):
            self.call = (name, a, k)
            return None
        return f


class Ctx:
    def __init__(self, nc, es, n_dma_sems=40):
        self.nc = nc
        self.esem = {e: es.enter_context(nc.semaphore("e_" + e)) for e in ENGS if e != "sp"}
        self.cnt = {e: 0 for e in ENGS}
        self.dpool = [es.enter_context(nc.semaphore("d%d" % i)) for i in range(n_dma_sems)]
        self.dcnt = [0] * n_dma_sems
        self.out_waits = {}


class Prog:
    def __init__(self, ctx, paranoid=True):
        self.ctx = ctx
        self.ops = []
        self.paranoid = paranoid

    def op(self, eng, fn, reads=(), writes=(), dma=False, key=None):
        rec = _Rec()
        fn(rec)
        name, a, k = rec.call
        fn = lambda e: getattr(e, name)(*a, **k)
        o = Op(eng, fn, tuple(reads), tuple(writes), dma, key)
        o.idx = len(self.ops)
        self.ops.append(o)
        return o

    def i(self, eng, name, reads, writes, *args, **kwargs):
        return self.op(eng, lambda e: getattr(e, name)(*args, **kwargs), reads, writes)

    def dma(self, q, out, in_, reads=(), writes=(), key=None):
        return self.op(q, lambda e: e.dma_start(out=out, in_=in_), reads, writes, dma=True, key=key)

    def analyze(self):
        last_w, readers = {}, {}
        for o in self.ops:
            deps = {}
            for r in o.reads:
                w = last_w.get(r)
                if w is not None:
                    deps[w.idx] = (w, "raw")
            for r in o.writes:
                w = last_w.get(r)
                if w is not None and w.idx not in deps:
                    if not (w.dma and o.dma and w.key is not None and w.key == o.key):
                        deps[w.idx] = (w, "waw")
                for rd in readers.get(r, ()):
                    if rd.idx not in deps:
                        deps[rd.idx] = (rd, "war")
            for r in o.reads:
                readers.setdefault(r, []).append(o)
            for r in o.writes:
                last_w[r] = o
                readers[r] = []
            out = []
            for d, kind in deps.values():
                if d is o:
                    continue
                if (not d.dma) and d.eng == o.eng:
                    if d.eng in ("pe", "sp"):
                        continue
                    if kind == "war" or not self.paranoid:
                        continue
                out.append(d)
                d.needed = True
            o.deps = out
        for e in ENGS:
            if e == "sp":
                continue
            for o in reversed(self.ops):
                if o.eng == e and not o.dma:
                    o.needed = True
                    break

    def emit(self):
        ctx = self.ctx
        nc = ctx.nc
        self.analyze()
        dkey = {}
        for o in self.ops:
            if o.dma:
                k = o.key if o.key is not None else (o.writes[0] if o.writes else ("dma", o.idx))
                if k not in dkey:
                    dkey[k] = len(dkey)
                    assert len(dkey) <= len(ctx.dpool), "too many DMA semaphores"
                i = dkey[k]
                ctx.dcnt[i] += 16
                o.token = (ctx.dpool[i], ctx.dcnt[i])
            elif o.needed:
                ctx.cnt[o.eng] += 1
                o.token = (ctx.esem[o.eng], ctx.cnt[o.eng])
        per_eng = {e: [o for o in self.ops if o.eng == e] for e in ENGS}
        fin_e = dict(ctx.cnt)
        fin_d = list(ctx.dcnt)

        def run(engobj, ename):
            know = {}
            for o in per_eng[ename]:
                for d in o.deps:
                    s, v = d.token
                    if know.get(id(s), 0) < v:
                        engobj.wait_ge(s, v)
                        know[id(s)] = v
                ins = o.fn(engobj)
                if o.token is not None:
                    ins.then_inc(o.token[0], 16 if o.dma else 1)
            for e2 in ENGS:
                if e2 == "sp" or e2 == ename:
                    continue
                if fin_e[e2] > 0 and know.get(id(ctx.esem[e2]), 0) < fin_e[e2]:
                    engobj.wait_ge(ctx.esem[e2], fin_e[e2])
            for i in range(len(dkey)):
                if fin_d[i] > 0 and know.get(id(ctx.dpool[i]), 0) < fin_d[i]:
                    engobj.wait_ge(ctx.dpool[i], fin_d[i])

        with nc.Block() as block:
            @block.tensor
            def _(e):
                run(e, "pe")

            @block.scalar
            def _(e):
                run(e, "act")

            @block.vector
            def _(e):
                run(e, "dve")

            @block.gpsimd
            def _(e):
                run(e, "pool")

            @block.sync
            def _(e):
                run(e, "sp")


def build_nc():
    nc = bass.Bass("TRN2", target_bir_lowering=False)

    def din(name, shape, dt=F32):
        return nc.dram_tensor(name, list(shape), dt, kind="ExternalInput").ap()

    def dout(name, shape, dt=F32):
        return nc.dram_tensor(name, list(shape), dt, kind="ExternalOutput").ap()

    xp = din("xp", [SEQ, D])
    xs = din("xs", [ST, D])
    cmk = din("cmk", [NPOOL * 128, 512])
    cmv = din("cmv", [NPOOL * 128, 512])
    csk = din("csk", [NPOOL * 128, 512])
    csv = din("csv", [NPOOL * 128, 512])
    ptab = din("pt", [4, NPAGES], I32)
    w_in = din("w_in", [D, 5120])
    w_bm = din("w_bm", [512, D])
    w_bs = din("w_bs", [512, D])
    w_out = din("w_out", [D, D])
    w_g = din("w_g", [D, DFF])
    w_u = din("w_u", [D, DFF])
    w_d = din("w_d", [DFF, D])
    gmixT = din("gmixT", [128, 8])
    gffnT = din("gffnT", [128, 8])
    gfin = din("gfin", [1, D])
    cs_all = din("cs_all", [NTOK, 64])
    cbf = din("cbf", [128, 8, 128], BF16)
    ebk = din("ebk", [8, 8, 128], BF16)
    cm8_d = din("cm8", [8, 2, 64], BF16)
    oh4_d = din("oh4", [128, 4])
    pmod_d = din("pmod", [128, 1])
    zsel_d = din("zsel2", [128, 16, 32], BF16)
    eg_d = din("eg", [32, 16, 128], BF16)

    yp = dout("yp", [SEQ, D])
    ys = dout("ys", [ST, D])
    o_kv = {}
    for nm in ("mk", "mv", "sk", "sv"):
        o_kv[nm] = (dout(nm + "p", [SEQ, 512]), dout(nm + "s", [ST, 512]))

    def wscr(name, shape):
        return nc.dram_tensor(name, list(shape), BF16, kind="Internal").ap()

    w_in_b = wscr("w_in_b", [D, 5120])
    w_bm_b = wscr("w_bm_b", [512, D])
    w_bs_b = wscr("w_bs_b", [512, D])
    w_out_b = wscr("w_out_b", [D, D])
    w_g_b = wscr("w_g_b", [D, DFF])
    w_u_b = wscr("w_u_b", [D, DFF])
    w_d_b = wscr("w_d_b", [DFF, D])
    hT_scr = nc.dram_tensor("hT_scr", [128, 8, NTOK], BF16, kind="Internal").ap()
    oa_scr = nc.dram_tensor("oa_scr", [NTOK, 512], BF16, kind="Internal").ap()
    os_scr = nc.dram_tensor("os_scr", [NTOK, 512], BF16, kind="Internal").ap()

    def tok_rows(t):
        return (t * 128, 128) if t < NT else (SEQ, ST)

    def x_src(t):
        return xp[t * 128:(t + 1) * 128, :] if t < NT else xs[:, :]

    with ExitStack() as es:
        def sb(name, shape, dt, stack=None):
            return (stack or es).enter_context(nc.sbuf_tensor(name, list(shape), dt))

        def ps(name, shape, dt):
            return es.enter_context(nc.psum_tensor(name, list(shape), dt))

        ctx = Ctx(nc, es, n_dma_sems=48)
        pb = [ps("pb%d" % i, [128, 512], F32) for i in range(6)]
        pt16 = [ps("pt16_%d" % i, [128, 1024], BF16) for i in range(2)]

        consts = sb("consts", [128, 8, 128], BF16)
        ident = consts[:, 0, :]
        mask_incl = consts[:, 1, :]
        mask_strict = consts[:, 2, :]
        Uneg = consts[:, 3, :]
        negones = consts[:, 4, :]
        ones = consts[:, 5, :]
        negI = consts[:, 6, :]
        Usneg = consts[:, 7, :]
        eb = sb("eb", [8, 8, 128], BF16)
        gmix_sb = sb("gmix_sb", [128, 8], F32)
        gffn_sb = sb("gffn_sb", [128, 8], F32)
        gfin_sb = sb("gfin_sb", [128, D], F32)
        zeros16 = sb("zeros16", [128, 512], BF16)
        QTa_s = sb("QTa_s", [128, 4, ST], BF16)
        KTa_s = sb("KTa_s", [128, 4, ST], BF16)
        QTs_s = sb("QTs_s", [128, 4, ST], BF16)
        KTs_s = sb("KTs_s", [128, 4, ST], BF16)
        Vna = sb("Vna", [ST, 512], BF16)
        Vns = sb("Vns", [ST, 512], BF16)

        with ExitStack() as s1:
            hT = sb("hT", [128, 8, NTOK], BF16, s1)
            QTa = sb("QTa", [128, 4, SEQ], BF16, s1)
            KTa = sb("KTa", [128, 4, SEQ], BF16, s1)
            QTs = sb("QTs", [128, 4, SEQ], BF16, s1)
            KTs = sb("KTs", [128, 4, SEQ], BF16, s1)
            Va = sb("Va", [128, NT, 8, 65], BF16, s1)
            Vs = sb("Vs", [128, NT, 512], BF16, s1)

            with ExitStack() as s2:
                P = Prog(ctx)
                P.dma("sp", consts[:], cbf, writes=["consts"])
                P.dma("sp", eb[:], ebk, writes=["eb"])
                P.dma("sp", gmix_sb[:], gmixT, writes=["gmix"])
                P.dma("sp", gffn_sb[:], gffnT, writes=["gffn"])
                P.dma("sp", gfin_sb[:], gfin.partition_broadcast(128).rearrange("p a d -> p (a d)"), writes=["gfin"])
                for nm, src, dst, a in (("w_in", w_in, w_in_b, 4),):
                    P.dma("pool", dst.rearrange("k (a n) -> k a n", a=a), src.rearrange("k (a n) -> k a n", a=a),
                          writes=[("wscr", nm)], key=("wscr", nm))
                P.op("pool", lambda e: e.memset(zeros16[:], 0.0), writes=["zeros16"])
                P.op("pool", lambda e: e.memset(Va[:, :, :, 64:65], 1.0), writes=["Va_ones"])

                xt = [sb("xt%d" % i, [128, D], F32, s2) for i in range(2)]
                sq = sb("sq", [128, D], BF16, s2)
                ssq = [sb("ssq%d" % i, [128, 4], F32, s2) for i in range(2)]
                hb = [sb("hb%d" % i, [128, D], BF16, s2) for i in range(2)]
                cs_sb = sb("cs_sb", [128, NT + 1, 64], F32, s2)
                for t in range(NT + 1):
                    r0, nr = tok_rows(t)
                    P.dma("sp", cs_sb[0:nr, t, :], cs_all[r0:r0 + nr, :], writes=[("cs", t)])

                for t in range(NT + 1):
                    r0, nr = tok_rows(t)
                    b = t % 2
                    X, SS, HB = xt[b], ssq[b], hb[b]
                    P.dma("sp", X[0:nr, :], x_src(t), writes=[("xt", b)])
                    P.op("act", lambda e, X=X, SS=SS, nr=nr: e.activation(
                        out=sq[0:nr, :], in_=X[0:nr, :], func=AF.Square, accum_out=SS[0:nr, 0:1]),
                        reads=[("xt", b)], writes=["sq", ("ssq", b)])
                    P.op("act", lambda e, SS=SS, nr=nr: e.activation(
                        out=SS[0:nr, 1:2], in_=SS[0:nr, 0:1], func=AF.Ln, scale=1.0 / D, bias=EPS),
                        reads=[("ssq", b)], writes=[("ssq", b)])
                    P.op("act", lambda e, SS=SS, nr=nr: e.activation(
                        out=SS[0:nr, 2:3], in_=SS[0:nr, 1:2], func=AF.Exp, scale=-0.5),
                        reads=[("ssq", b)], writes=[("ssq", b)])
                    P.op("dve", lambda e, X=X, SS=SS, HB=HB, nr=nr: e.tensor_scalar(
                        out=HB[0:nr, :], in0=X[0:nr, :], scalar1=SS[0:nr, 2:3], scalar2=None, op0=ALU.mult),
                        reads=[("xt", b), ("ssq", b)], writes=[("hb", b)])
                    pT = pt16[b]
                    for kc in range(8):
                        P.op("pe", lambda e, pT=pT, HB=HB, kc=kc, nr=nr: e.transpose(
                            out=pT[:, kc * 128:kc * 128 + nr], in_=HB[0:nr, kc * 128:(kc + 1) * 128],
                            identity=ident[0:nr, 0:nr]),
                            reads=[("hb", b), "consts"], writes=[("pt16", b)])
                    P.op("dve", lambda e, pT=pT, r0=r0, nr=nr: e.tensor_tensor(
                        out=hT[:, :, r0:r0 + nr],
                        in0=pT[:, :].rearrange("p (k t) -> p k t", t=128)[:, :, 0:nr],
                        in1=gmix_sb[:, :].unsqueeze(2).to_broadcast([128, 8, nr]), op=ALU.mult),
                        reads=[("pt16", b), "gmix"], writes=[("hT", t)])
                    P.dma("sp", hT_scr[:, :, r0:r0 + nr], hT[:, :, r0:r0 + nr], reads=[("hT", t)], key=("o", "hTs"))

                Wb = [sb("Wb%d" % i, [128, 8, 512], BF16, s2) for i in range(2)]
                raw = [sb("raw%d" % i, [128, 512], F32, s2) for i in range(2)]
                stg = [sb("stg%d" % i, [128, 512], F32, s2) for i in range(3)]
                tmpv = sb("tmpv", [128, 256], F32, s2)
                tmpg = sb("tmpg", [128, 256], F32, s2)
                qb = [sb("qb%d" % i, [128, 512], BF16, s2) for i in range(2)]
                w_in_v = w_in_b.rearrange("(kc p) n -> p kc n", p=128)
                it = 0
                for cb in range(6):
                    wslot = cb % 2
                    W = Wb[wslot]
                    for half in range(2):
                        P.dma("sp", W[:, half * 4:(half + 1) * 4, :],
                              w_in_v[:, half * 4:(half + 1) * 4, cb * 512:(cb + 1) * 512],
                              reads=[("wscr", "w_in")], writes=[("Wb", wslot, half)])
                    kind = ("qa", "ka", "va", "qs", "ks", "vs")[cb]
                    for t in range(NT + 1):
                        r0, nr = tok_rows(t)
                        acc = pb[it % 2]
                        akey = ("pb", it % 2)
                        for kc in range(8):
                            P.op("pe", lambda e, acc=acc, W=W, kc=kc, r0=r0, nr=nr: e.matmul(
                                acc[0:nr, :], lhsT=hT[:, kc, r0:r0 + nr], rhs=W[:, kc, :],
                                start=(kc == 0), stop=(kc == 7)),
                                reads=[("hT", t), ("Wb", wslot, kc // 4)], writes=[akey])
                        rslot = it % 2
                        R = raw[rslot]
                        sslot = it % 3
                        S = stg[sslot]
                        qslot = it % 2
                        Q = qb[qslot]
                        it += 1
                        if kind in ("qa", "ka"):
                            P.op("act", lambda e, R=R, acc=acc, nr=nr: e.activation(out=R[0:nr, :], in_=acc[0:nr, :], func=AF.Copy),
                                 reads=[akey], writes=[("raw", rslot)])
                            Rv = R[:, :].rearrange("p (h t d) -> p h t d", t=2, d=32)
                            Sv = S[:, :].rearrange("p (h t d) -> p h t d", t=2, d=32)
                            cosb = cs_sb[0:nr, t, 0:32].unsqueeze(1).to_broadcast([nr, 8, 32])
                            sinb = cs_sb[0:nr, t, 32:64].unsqueeze(1).to_broadcast([nr, 8, 32])
                            tv = tmpv[:, :].rearrange("p (h d) -> p h d", d=32)
                            tg = tmpg[:, :].rearrange("p (h d) -> p h d", d=32)
                            P.op("dve", lambda e, Sv=Sv, Rv=Rv, cosb=cosb, nr=nr: e.tensor_tensor(
                                out=Sv[0:nr, :, 0, :], in0=Rv[0:nr, :, 0, :], in1=cosb, op=ALU.mult),
                                reads=[("raw", rslot), ("cs", t)], writes=[("stg", sslot, 0)])
                            P.op("dve", lambda e, tv=tv, Rv=Rv, sinb=sinb, nr=nr: e.tensor_tensor(
                                out=tv[0:nr], in0=Rv[0:nr, :, 1, :], in1=sinb, op=ALU.mult),
                                reads=[("raw", rslot), ("cs", t)], writes=["tmpv"])
                            P.op("dve", lambda e, Sv=Sv, tv=tv, nr=nr: e.tensor_tensor(
                                out=Sv[0:nr, :, 0, :], in0=Sv[0:nr, :, 0, :], in1=tv[0:nr], op=ALU.subtract),
                                reads=["tmpv", ("stg", sslot, 0)], writes=[("stg", sslot, 0)])
                            P.op("pool", lambda e, Sv=Sv, Rv=Rv, cosb=cosb, nr=nr: e.tensor_tensor(
                                out=Sv[0:nr, :, 1, :], in0=Rv[0:nr, :, 1, :], in1=cosb, op=ALU.mult),
                                reads=[("raw", rslot), ("cs", t)], writes=[("stg", sslot, 1)])
                            P.op("pool", lambda e, tg=tg, Rv=Rv, sinb=sinb, nr=nr: e.tensor_tensor(
                                out=tg[0:nr], in0=Rv[0:nr, :, 0, :], in1=sinb, op=ALU.mult),
                                reads=[("raw", rslot), ("cs", t)], writes=["tmpg"])
                            P.op("pool", lambda e, Sv=Sv, tg=tg, nr=nr: e.tensor_tensor(
                                out=Sv[0:nr, :, 1, :], in0=Sv[0:nr, :, 1, :], in1=tg[0:nr], op=ALU.add),
                                reads=["tmpg", ("stg", sslot, 1)], writes=[("stg", sslot, 1)])
                            sreads = [("stg", sslot, 0), ("stg", sslot, 1)]
                        else:
                            P.op("act", lambda e, S=S, acc=acc, nr=nr: e.activation(out=S[0:nr, :], in_=acc[0:nr, :], func=AF.Copy),
                                 reads=[akey], writes=[("stg", sslot, 0), ("stg", sslot, 1)])
                            sreads = [("stg", sslot, 0), ("stg", sslot, 1)]
                        if kind in ("ka", "va", "ks", "vs"):
                            dst = o_kv[{"ka": "mk", "va": "mv", "ks": "sk", "vs": "sv"}[kind]]
                            dap = dst[0][r0:r0 + nr, :] if t < NT else dst[1][:, :]
                            P.dma("sp", dap, S[0:nr, :], reads=sreads, key=("o", "stg", sslot))
                        if kind in ("va", "vs"):
                            if t < NT:
                                if kind == "va":
                                    P.op("pool", lambda e, S=S, t=t: e.tensor_copy(
                                        out=Va[:, t, :, 0:64], in_=S[:, :].rearrange("p (h d) -> p h d", d=64)),
                                        reads=sreads, writes=[("Va", t)])
                                else:
                                    P.op("pool", lambda e, S=S, t=t: e.tensor_copy(out=Vs[:, t, :], in_=S[:, :]),
                                         reads=sreads, writes=[("Vs", t)])
                            else:
                                Vn = Vna if kind == "va" else Vns
                                P.op("pool", lambda e, S=S, Vn=Vn: e.tensor_copy(out=Vn[:, :], in_=S[0:ST, :]),
                                     reads=sreads, writes=["Vn" + kind])
                        else:
                            scale = 0.125 if kind in ("qa", "qs") else 1.0
                            P.op("dve", lambda e, Q=Q, S=S, nr=nr, scale=scale: e.tensor_scalar(
                                out=Q[0:nr, :], in0=S[0:nr, :], scalar1=scale, scalar2=None, op0=ALU.mult),
                                reads=sreads, writes=[("qb", qslot)])
                            pT = pt16[it % 2]
                            for j in range(4):
                                P.op("pe", lambda e, pT=pT, Q=Q, j=j, nr=nr: e.transpose(
                                    out=pT[:, j * 128:j * 128 + nr], in_=Q[0:nr, j * 128:(j + 1) * 128],
                                    identity=ident[0:nr, 0:nr]),
                                    reads=[("qb", qslot), "consts"], writes=[("pt16", it % 2)])
                            if t < NT:
                                dstT = {"qa": QTa, "ka": KTa, "qs": QTs, "ks": KTs}[kind]
                                dsl = dstT[:, :, r0:r0 + nr]
                            else:
                                dstT = {"qa": QTa_s, "ka": KTa_s, "qs": QTs_s, "ks": KTs_s}[kind]
                                dsl = dstT[:, :, :]
                            P.op("act", lambda e, pT=pT, dsl=dsl, nr=nr: e.activation(
                                out=dsl, in_=pT[:, 0:512].rearrange("p (j t) -> p j t", t=128)[:, :, 0:nr], func=AF.Copy),
                                reads=[("pt16", it % 2)], writes=[(kind + "T", t)])
                P.emit()

            with ExitStack() as s2:
                P = Prog(ctx)
                otok_a = sb("otok_a", [128, NT, 512], BF16, s2)
                otok_s = sb("otok_s", [128, NT, 512], BF16, s2)
                kmf = sb("kmf", [128, 4, 8], F32, s2)
                kmT = sb("kmT", [128, 4, 8], BF16, s2)
                Gm = sb("Gm", [128, 8, 8], F32, s2)
                top8 = sb("top8", [128, 8, 8], F32, s2)
                selt = sb("selt", [128, 8, 8], F32, s2)
                Mtok = sb("Mtok", [128, 8, 64], BF16, s2)
                MTs = [sb("MTs%d" % i, [8, 512], BF16, s2) for i in range(2)]
                Pt = [sb("Pt%d" % i, [128, 512], BF16, s2) for i in range(4)]
                Ef = [sb("Ef%d" % i, [128, 512], F32, s2) for i in range(3)]
                SP = [sb("SP%d" % i, [128, 512], BF16, s2) for i in range(3)]
                Rr = [sb("Rr%d" % i, [128, 512], BF16, s2) for i in range(4)]
                rden = sb("rden", [128, 4], F32, s2)
                for nm, src, dst, a in (("w_bm", w_bm, w_bm_b, 1), ("w_bs", w_bs, w_bs_b, 1),
                                        ("w_out", w_out, w_out_b, 1), ("w_g", w_g, w_g_b, 2), ("w_u", w_u, w_u_b, 2),
                                        ("w_d", w_d, w_d_b, 1)):
                    P.dma("pool", dst.rearrange("k (a n) -> k a n", a=a), src.rearrange("k (a n) -> k a n", a=a),
                          writes=[("wscr", nm)], key=("wscr", nm))

                for hp in range(4):
                    P.op("dve", lambda e, hp=hp: e.reduce_sum(
                        out=kmf[:, hp, :], in_=KTa[:, hp, :].rearrange("p (n k) -> p n k", k=256), axis=AX.X),
                        writes=[("kmf", hp)])
                P.op("dve", lambda e: e.tensor_scalar(out=kmT[:], in0=kmf[:], scalar1=1.0 / 256, scalar2=None, op0=ALU.mult),
                     reads=[("kmf", hp) for hp in range(4)], writes=["kmT"])
                for c in range(8, NT):
                    cur = c // 2
                    G = pb[5]
                    for h in range(8):
                        hp, hb_ = h // 2, (h % 2) * 64
                        P.op("pe", lambda e, G=G, h=h, hp=hp, hb_=hb_, c=c: e.matmul(
                            G[:, h * 8:(h + 1) * 8], lhsT=QTa[hb_:hb_ + 64, hp, c * 128:(c + 1) * 128],
                            rhs=kmT[hb_:hb_ + 64, hp, :], start=True, stop=True),
                            reads=["kmT"], writes=[("pb", 5)])
                    P.op("dve", lambda e, G=G: e.tensor_copy(out=Gm[:], in_=G[:, 0:64].rearrange("p (h n) -> p h n", n=8)),
                         reads=[("pb", 5)], writes=["Gm"])
                    P.op("dve", lambda e, cur=cur: e.memset(Gm[:, :, cur:8], -1e30), reads=["Gm"], writes=["Gm"])
                    for h in range(8):
                        P.op("dve", lambda e, h=h: e.max(out=top8[:, h, :], in_=Gm[:, h, :]), reads=["Gm"], writes=[("top8", h)])
                    P.op("dve", lambda e: e.tensor_tensor(
                        out=selt[:], in0=Gm[:], in1=top8[:, :, 2:3].to_broadcast([128, 8, 8]), op=ALU.is_ge),
                        reads=["Gm"] + [("top8", h) for h in range(8)], writes=["selt"])
                    Mv = Mtok[:, c - 8, :].rearrange("p (h n) -> p h n", n=8)
                    P.op("dve", lambda e, Mv=Mv: e.tensor_scalar(
                        out=Mv, in0=selt[:], scalar1=1.0, scalar2=-NEG, op0=ALU.subtract, op1=ALU.mult),
                        reads=["selt"], writes=[("Mtok", c)])
                    P.op("dve", lambda e, Mv=Mv, cur=cur: e.memset(Mv[:, :, cur:cur + 1], 0.0),
                         reads=[("Mtok", c)], writes=[("Mtok", c)])

                sidx = [0]
                gidx = [0]

                def attn_head_group(h, qg, moba):
                    hp, hb_ = h // 2, (h % 2) * 64
                    KT, QT = (KTa, QTa) if moba else (KTs, QTs)
                    c_lo, c_hi = qg * 4, qg * 4 + 3
                    oi = 4 + gidx[0] % 2
                    mslot = rslot = gidx[0] % 2
                    gidx[0] += 1
                    O = pb[oi]
                    okey = ("pb", oi)
                    ncol = 4 * 65 if moba else 4 * 64
                    ow = 65 if moba else 64
                    P.i("pe", "matmul", ["zeros16"], [okey], O[:, 0:ncol], lhsT=zeros16[:, 0:128], rhs=zeros16[:, 0:ncol],
                        start=True, stop=False)
                    mts = None
                    if moba and qg >= 2:
                        mts = MTs[mslot]
                        pT = pt16[0]
                        for i in range(4):
                            c = c_lo + i
                            P.i("pe", "transpose", [("Mtok", c), "consts"], [("pt16", 0)], out=pT[0:8, i * 128:(i + 1) * 128],
                                in_=Mtok[:, c - 8, h * 8:(h + 1) * 8], identity=ident)
                        P.i("dve", "tensor_copy", [("pt16", 0)], [("MTs", mslot)], out=mts[:, :], in_=pT[0:8, 0:512])
                    Rpp = None
                    if not moba:
                        Rpp = (Rr[2 * rslot], Rr[2 * rslot + 1])
                        for q_ in range(2):
                            P.i("pool", "memset", [], [("Rr", 2 * rslot + q_)], Rpp[q_][:], 0.0)
                    rstep = [0]
                    kts = list(range(0, c_hi + 1)) if moba else list(range(c_hi, -1, -1))

                    def stage1(kt):
                        c0 = max(kt, c_lo)
                        N = (c_hi + 1 - c0) * 128
                        q0 = c0 * 128
                        diag = kt >= c_lo
                        n = sidx[0]
                        sidx[0] += 1
                        st = dict(kt=kt, c0=c0, N=N, coff=(c0 - c_lo) * 128, sb_i=n % 4, eslot=n % 3, pslot=n % 4)
                        S1 = pb[st["sb_i"]]
                        k1 = ("pb", st["sb_i"])
                        P.i("pe", "matmul", [], [k1], S1[:, 0:N], lhsT=KT[hb_:hb_ + 64, hp, kt * 128:(kt + 1) * 128],
                            rhs=QT[hb_:hb_ + 64, hp, q0:q0 + N], start=True, stop=False)
                        if diag:
                            P.i("pe", "matmul", ["consts"], [k1], S1[:, 0:128], lhsT=ident,
                                rhs=(mask_incl if moba else mask_strict), start=False, stop=False)
                        if moba:
                            n_blk = kt // 2
                            cm = max(c0, 2 * n_blk + 2)
                            if qg >= 2 and cm <= c_hi:
                                o1 = (cm - c0) * 128
                                m1 = (cm - c_lo) * 128
                                P.i("pe", "matmul", ["eb", ("MTs", mslot)], [k1], S1[:, o1:N], lhsT=eb[0:8, n_blk, :],
                                    rhs=mts[0:8, m1:512], start=False, stop=True)
                            A = Pt[st["pslot"]]
                            P.i("act", "activation", [k1], [("Pt", st["pslot"])], out=A[:, 0:N], in_=S1[:, 0:N], func=AF.Exp)
                        else:
                            E, SPt = Ef[st["eslot"]], SP[st["eslot"]]
                            P.i("act", "activation", [k1], [("Ef", st["eslot"])], out=E[:, 0:N], in_=S1[:, 0:N], func=AF.Exp)
                            P.i("act", "activation", [("Ef", st["eslot"])], [("SP", st["eslot"])], out=SPt[:, 0:N],
                                in_=E[:, 0:N], func=AF.Ln, bias=1.0)
                        return st

                    def stage2(st, first):
                        kt, c0, N, coff = st["kt"], st["c0"], st["N"], st["coff"]
                        S1 = pb[st["sb_i"]]
                        k1 = ("pb", st["sb_i"])
                        A = Pt[st["pslot"]]
                        if not moba:
                            SPt = SP[st["eslot"]]
                            P.i("pe", "matmul", ["consts", ("SP", st["eslot"])], [k1], S1[:, 0:N], lhsT=Uneg, rhs=SPt[:, 0:N],
                                start=False, stop=first)
                            ra = rstep[0] % 2
                            rstep[0] += 1
                            Rcur, Rnxt = Rpp[ra], Rpp[1 - ra]
                            kcur, knxt = ("Rr", 2 * rslot + ra), ("Rr", 2 * rslot + 1 - ra)
                            if kt > 0:
                                P.i("pool", "tensor_tensor", [("SP", st["eslot"]), kcur], [knxt],
                                    out=Rnxt[:, coff:coff + N], in0=Rcur[:, coff:coff + N], in1=SPt[:, 0:N], op=ALU.add)
                            if not first:
                                P.i("pe", "matmul", ["consts", kcur], [k1], S1[:, 0:N], lhsT=negones,
                                    rhs=Rcur[:, coff:coff + N], start=False, stop=True)
                            P.i("act", "activation", [k1], [("Pt", st["pslot"])], out=A[:, 0:N], in_=S1[:, 0:N], func=AF.Exp)
                        for c in range(c0, c_hi + 1):
                            j = c - c0
                            i = c - c_lo
                            rhs = Va[:, kt, h, :] if moba else Vs[:, kt, h * 64:(h + 1) * 64]
                            P.i("pe", "matmul", [("Pt", st["pslot"])], [okey], O[:, i * ow:(i + 1) * ow],
                                lhsT=A[:, j * 128:(j + 1) * 128], rhs=rhs, start=False, stop=False)

                    prev = stage1(kts[0])
                    for idx_, kt in enumerate(kts[1:]):
                        cur_ = stage1(kt)
                        stage2(prev, idx_ == 0)
                        prev = cur_
                    stage2(prev, len(kts) == 1)
                    if moba:
                        Ov = O[:, 0:ncol].rearrange("p (i w) -> p i w", w=65)
                        P.i("dve", "reciprocal", [okey], ["rden"], out=rden[:, :], in_=Ov[:, :, 64])
                        P.i("dve", "tensor_tensor", [okey, "rden"], [("otok_a", qg, h)],
                            out=otok_a[:, c_lo:c_hi + 1, h * 64:(h + 1) * 64], in0=Ov[:, :, 0:64],
                            in1=rden[:, :].unsqueeze(2).to_broadcast([128, 4, 64]), op=ALU.mult)
                    else:
                        Ov = O[:, 0:ncol].rearrange("p (i w) -> p i w", w=64)
                        P.i("dve", "tensor_copy", [okey], [("otok_s", qg, h)],
                            out=otok_s[:, c_lo:c_hi + 1, h * 64:(h + 1) * 64], in_=Ov)

                for h in range(8):
                    for qg in range(4):
                        attn_head_group(h, qg, True)
                for h in range(8):
                    for qg in range(4):
                        attn_head_group(h, qg, False)
                for qg in range(4):
                    for nm, ot, scr in (("otok_a", otok_a, oa_scr), ("otok_s", otok_s, os_scr)):
                        P.dma("sp", scr[qg * 512:(qg + 1) * 512, :].rearrange("(i p) f -> p i f", p=128),
                              ot[:, qg * 4:(qg + 1) * 4, :], reads=[(nm, qg, h) for h in range(8)], key=("o", nm))
                P.emit()


        with ExitStack() as s2:
            P = Prog(ctx)
            pt_sb = sb("pt_sb", [128, 4, NPAGES], I32, s2)
            ptf = sb("ptf", [128, 4, NPAGES], F32, s2)
            ptsel = sb("ptsel", [128, 64, 4], F32, s2)
            pgrp = sb("pgrp", [128, 64], F32, s2)
            idxf = sb("idxf", [128, 64], F32, s2)
            idx = sb("idx", [128, 64], I32, s2)
            oh4 = sb("oh4_sb", [128, 4], F32, s2)
            pmod = sb("pmod_sb", [128, 1], F32, s2)
            cm8 = sb("cm8_sb", [8, 2, 64], BF16, s2)
            zsel = sb("zsel_sb", [128, 16, 32], BF16, s2)
            egm = sb("eg_sb", [32, 16, 128], BF16, s2)
            Vnq = sb("Vnq", [8, 4, 2, 512], BF16, s2)
            Qbd = sb("Qbd", [128, 2, 4, 16], BF16, s2)
            Ka_seq = sb("Ka_seq", [128, NPAGES, 512], BF16, s2)
            NSL = 3
            Ksb = [sb("Ksb%d" % i, [128, 4, 512], BF16, s2) for i in range(NSL)]
            Vab = [sb("Vab%d" % i, [128, 4, 512], BF16, s2) for i in range(NSL)]
            Vsb = [sb("Vsb%d" % i, [128, 4, 512], BF16, s2) for i in range(NSL)]
            KTg = [[sb("KTg%d_%d" % (a, i), [128, 4, 4, 128], BF16, s2) for i in range(NSL)] for a in range(2)]
            KMb = sb("KMb", [32, 512], BF16, s2)
            KMT = sb("KMT", [128, 4, 32], BF16, s2)
            Gs = sb("Gs", [16, 4, 32], F32, s2)
            top8s = sb("top8s", [16, 4, 8], F32, s2)
            sels = sb("sels", [16, 4, 32], F32, s2)
            Msel = sb("Msel", [16, 4, 32], BF16, s2)
            MTs2 = sb("MTs2", [32, 64], BF16, s2)
            Pn = sb("Pn", [8, 64], BF16, s2)
            En = sb("En", [8, 64], F32, s2)
            SPn = sb("SPn", [8, 64], BF16, s2)
            An = sb("An", [8, 64], BF16, s2)
            Pm = [sb("Pm%d" % i, [128, 4, 64], BF16, s2) for i in range(NSL)]
            Eg = [sb("Eg%d" % i, [128, 256], F32, s2) for i in range(NSL)]
            SPg = [sb("SPg%d" % i, [128, 4, 64], BF16, s2) for i in range(NSL)]
            Ag = [sb("Ag%d" % i, [128, 4, 64], BF16, s2) for i in range(NSL)]
            Wc = [sb("Wc%d" % i, [128, 4, 64], BF16, s2) for i in range(NSL)]
            carry = sb("carry", [128, 64], BF16, s2)
            rdn = sb("rdn", [64, 1], F32, s2)
            oa_sb = sb("oa_sb", [64, 512], BF16, s2)
            os_sb = sb("os_sb", [64, 512], BF16, s2)

            P.dma("sp", pt_sb[:], ptab.partition_broadcast(128), writes=["pt_sb"])
            P.dma("sp", cm8[:], cm8_d, writes=["cm8"])
            P.dma("sp", oh4[:], oh4_d, writes=["oh4"])
            P.dma("sp", pmod[:], pmod_d, writes=["pmod"])
            P.dma("sp", zsel[:], zsel_d, writes=["zsel"])
            P.dma("sp", egm[:], eg_d, writes=["egm"])
            P.i("dve", "tensor_copy", ["pt_sb"], ["ptf"], out=ptf[:], in_=pt_sb[:])
            P.i("dve", "tensor_tensor", ["ptf", "oh4"], ["ptsel"], out=ptsel[:],
                in0=ptf[:, :, :].rearrange("p s (g l) -> p (s g) l", l=4),
                in1=oh4[:, :].unsqueeze(1).to_broadcast([128, 64, 4]), op=ALU.mult)
            P.i("dve", "reduce_sum", ["ptsel"], ["pgrp"], out=pgrp[:], in_=ptsel[:], axis=AX.X)
            P.i("dve", "tensor_scalar", ["pgrp", "pmod"], ["idxf"], out=idxf[:], in0=pgrp[:], scalar1=32.0,
                scalar2=pmod[:, 0:1], op0=ALU.mult, op1=ALU.add)
            P.i("dve", "tensor_copy", ["idxf"], ["idx"], out=idx[:], in_=idxf[:])
            for s in range(4):
                P.dma("sp", Vnq[0:8, s, 0, :], Vna[s * 8:(s + 1) * 8, :], writes=[("Vnq", s)])
                P.dma("sp", Vnq[0:8, s, 1, :], Vns[s * 8:(s + 1) * 8, :], writes=[("Vnq", s)])

            def gather(dst, cache, s, g, reads, writes, key):
                col = s * 16 + g
                P.op("pool", lambda e: e.indirect_dma_start(
                    out=dst.rearrange("p t f -> p (t f)"), out_offset=None,
                    in_=cache.rearrange("(r t) f -> r (t f)", t=4),
                    in_offset=bass.IndirectOffsetOnAxis(ap=idx[:, col:col + 1], axis=0)),
                    reads=reads, writes=writes, dma=True, key=key)

            gcount = [0]
            for s in range(4):
                sc = slice(s * 8, (s + 1) * 8)
                P.i("dve", "memset", [], ["Qbd"], Qbd[:], 0.0)
                for a, QTx in ((0, QTa_s), (1, QTs_s)):
                    P.i("dve", "tensor_copy", ["Qbd"], ["Qbd"], out=Qbd[0:64, a, :, 0:8], in_=QTx[0:64, :, sc])
                    P.i("dve", "tensor_copy", ["Qbd"], ["Qbd"], out=Qbd[64:128, a, :, 8:16], in_=QTx[64:128, :, sc])
                for g in range(16):
                    gather(Ka_seq[:, g * 4:(g + 1) * 4, :], cmk, s, g, ["idx"], [("Ka", g // 2)], ("Ka", g // 2))
                KM = pb[2]
                for j in range(NPAGES):
                    P.i("pe", "matmul", [("Ka", j // 8), "zsel"], [("pb", 2)], KM[0:32, :],
                        lhsT=zsel[:, j // 4, :], rhs=Ka_seq[:, j, :], start=(j == 0), stop=(j == NPAGES - 1))
                P.i("act", "activation", [("pb", 2)], ["KMb"], out=KMb[:, :], in_=KM[0:32, :], func=AF.Copy)
                pT = pt16[0]
                for hp in range(4):
                    P.i("pe", "transpose", ["KMb"], [("pt16", 0)], out=pT[:, hp * 32:(hp + 1) * 32],
                        in_=KMb[0:32, hp * 128:(hp + 1) * 128], identity=ident[0:32, 0:32])
                P.i("dve", "tensor_copy", [("pt16", 0)], ["KMT"], out=KMT[:, :, :],
                    in_=pT[:, 0:128].rearrange("p (h n) -> p h n", n=32))
                Gp = pb[5]
                for hp in range(4):
                    P.i("pe", "matmul", ["KMT", "Qbd"], [("pb", 5)], Gp[0:16, hp * 32:(hp + 1) * 32],
                        lhsT=Qbd[:, 0, hp, :], rhs=KMT[:, hp, :], start=True, stop=True)
                P.i("dve", "tensor_copy", [("pb", 5)], ["Gs"], out=Gs[:, :, :],
                    in_=Gp[0:16, 0:128].rearrange("p (h n) -> p h n", n=32))
                for hp in range(4):
                    P.i("dve", "max", ["Gs"], [("top8s", hp)], out=top8s[:, hp, :], in_=Gs[:, hp, :])
                P.i("dve", "tensor_tensor", ["Gs"] + [("top8s", hp) for hp in range(4)], ["sels"], out=sels[:],
                    in0=Gs[:], in1=top8s[:, :, 2:3].to_broadcast([16, 4, 32]), op=ALU.is_ge)
                P.i("dve", "tensor_scalar", ["sels"], ["Msel"], out=Msel[:], in0=sels[:], scalar1=1.0, scalar2=-NEG,
                    op0=ALU.subtract, op1=ALU.mult)
                pT = pt16[1]
                for hp in range(4):
                    P.i("pe", "transpose", ["Msel"], [("pt16", 1)], out=pT[0:32, hp * 16:(hp + 1) * 16],
                        in_=Msel[0:16, hp, :], identity=ident[0:16, 0:16])
                P.i("dve", "tensor_copy", [("pt16", 1)], ["MTs2"], out=MTs2[:, :], in_=pT[0:32, 0:64])
                Oa, Os, Dn = pb[3], pb[4], pb[5]
                P.i("pe", "matmul", ["zeros16"], [("pb", 3)], Oa[0:64, :], lhsT=zeros16[:, 0:64], rhs=zeros16[:, 0:512],
                    start=True, stop=False)
                P.i("pe", "matmul", ["zeros16"], [("pb", 4)], Os[0:64, :], lhsT=zeros16[:, 0:64], rhs=zeros16[:, 0:512],
                    start=True, stop=False)
                P.i("pe", "matmul", ["zeros16", "Gs"], [("pb", 5)], Dn[0:64, 0:8], lhsT=zeros16[:, 0:64], rhs=zeros16[:, 0:8],
                    start=True, stop=False)
                Sm, S1, S2 = pb[2], pb[0], pb[1]
                P.i("pe", "matmul", ["cm8"], [("pb", 2)], Sm[0:8, 0:64], lhsT=ident[0:8, 0:8], rhs=cm8[0:8, 0, :],
                    start=True, stop=False)
                for hp in range(4):
                    P.i("pe", "matmul", ["Qbd"], [("pb", 2)], Sm[0:8, hp * 16:(hp + 1) * 16],
                        lhsT=KTa_s[:, hp, sc], rhs=Qbd[:, 0, hp, :], start=False, stop=(hp == 3))
                P.i("act", "activation", [("pb", 2)], ["Pn"], out=Pn[:, :], in_=Sm[0:8, 0:64], func=AF.Exp)
                P.i("pe", "matmul", ["Pn", ("Vnq", s)], [("pb", 3)], Oa[0:64, :], lhsT=Pn[0:8, :], rhs=Vnq[0:8, s, 0, :],
                    start=False, stop=False)
                P.i("pe", "matmul", ["Pn"], [("pb", 5)], Dn[0:64, 0:1], lhsT=Pn[0:8, :], rhs=ones[0:8, 0:1],
                    start=False, stop=False)
                P.i("pe", "matmul", ["cm8"], [("pb", 0)], S1[0:8, 0:64], lhsT=ident[0:8, 0:8], rhs=cm8[0:8, 1, :],
                    start=True, stop=False)
                for hp in range(4):
                    P.i("pe", "matmul", ["Qbd"], [("pb", 0)], S1[0:8, hp * 16:(hp + 1) * 16],
                        lhsT=KTs_s[:, hp, sc], rhs=Qbd[:, 1, hp, :], start=False, stop=(hp == 3))
                P.i("act", "activation", [("pb", 0)], ["En"], out=En[:, :], in_=S1[0:8, 0:64], func=AF.Exp)
                P.i("act", "activation", ["En"], ["SPn"], out=SPn[:, :], in_=En[:, :], func=AF.Ln, bias=1.0)
                P.i("pe", "matmul", ["SPn"], [("pb", 1)], S2[0:8, 0:64], lhsT=Uneg[0:8, 0:8], rhs=SPn[0:8, :],
                    start=True, stop=False)
                P.i("pe", "matmul", ["cm8"], [("pb", 1)], S2[0:8, 0:64], lhsT=ident[0:8, 0:8], rhs=cm8[0:8, 1, :],
                    start=False, stop=False)
                for hp in range(4):
                    P.i("pe", "matmul", ["Qbd"], [("pb", 1)], S2[0:8, hp * 16:(hp + 1) * 16],
                        lhsT=KTs_s[:, hp, sc], rhs=Qbd[:, 1, hp, :], start=False, stop=(hp == 3))
                P.i("act", "activation", [("pb", 1)], ["An"], out=An[:, :], in_=S2[0:8, 0:64], func=AF.Exp)
                P.i("pe", "matmul", ["An", ("Vnq", s)], [("pb", 4)], Os[0:64, :], lhsT=An[0:8, :], rhs=Vnq[0:8, s, 1, :],
                    start=False, stop=False)
                for g in range(15, -1, -1):
                    slot = gcount[0] % NSL
                    gcount[0] += 1
                    NC_ = 256
                    gather(Ksb[slot][:, :, :], csk, s, g, ["idx"], [("Ksb", slot)], ("Ksb", slot))
                    gather(Vab[slot][:, :, :], cmv, s, g, ["idx"], [("Vab", slot)], ("Vab", slot))
                    gather(Vsb[slot][:, :, :], csv, s, g, ["idx"], [("Vsb", slot)], ("Vsb", slot))
                    ev = 0
                    for a in range(2):
                        for pp in range(0, 4, 2):
                            bank = (a * 2 + pp // 2) % 2
                            pT = pt16[bank]
                            for q in range(2):
                                pl = pp + q
                                src = Ka_seq[:, g * 4 + pl, :] if a == 0 else Ksb[slot][:, pl, :]
                                rk = ("Ka", g // 2) if a == 0 else ("Ksb", slot)
                                for hp in range(4):
                                    P.i("pe", "transpose", [rk], [("pt16", bank)],
                                        out=pT[:, q * 512 + hp * 128:q * 512 + (hp + 1) * 128],
                                        in_=src[:, hp * 128:(hp + 1) * 128], identity=ident)
                            dst = KTg[a][slot][:, pp:pp + 2, :, :].rearrange("p a h t -> p (a h t)")
                            if ev % 2 == 0:
                                P.i("act", "activation", [("pt16", bank)], [("KTg", a, slot, pp)], out=dst, in_=pT[:, :], func=AF.Copy)
                            else:
                                P.i("dve", "tensor_copy", [("pt16", bank)], [("KTg", a, slot, pp)], out=dst, in_=pT[:, :])
                            ev += 1
                    P.i("pe", "matmul", ["MTs2", "egm"], [("pb", 2)], Sm[:, 0:NC_], lhsT=egm[:, g, :],
                        rhs=MTs2[:, :].unsqueeze(1).to_broadcast([32, 4, 64]), start=True, stop=False)
                    for pl in range(4):
                        for hp in range(4):
                            P.i("pe", "matmul", [("KTg", 0, slot, (pl // 2) * 2), "Qbd"], [("pb", 2)],
                                Sm[:, pl * 64 + hp * 16:pl * 64 + (hp + 1) * 16],
                                lhsT=KTg[0][slot][:, pl, hp, :], rhs=Qbd[:, 0, hp, :], start=False, stop=False)
                    PM = Pm[slot]
                    P.i("act", "activation", [("pb", 2)], [("Pm", slot)], out=PM[:, :, :].rearrange("p a c -> p (a c)"),
                        in_=Sm[:, 0:NC_], func=AF.Exp)
                    for pl in range(4):
                        P.i("pe", "matmul", [("Pm", slot), ("Vab", slot)], [("pb", 3)], Oa[0:64, :], lhsT=PM[:, pl, :],
                            rhs=Vab[slot][:, pl, :], start=False, stop=False)
                        P.i("pe", "matmul", [("Pm", slot)], [("pb", 5)], Dn[0:64, 0:1], lhsT=PM[:, pl, :],
                            rhs=ones[:, 0:1], start=False, stop=False)
                    for pl in range(4):
                        for hp in range(4):
                            P.i("pe", "matmul", [("KTg", 1, slot, (pl // 2) * 2), "Qbd"], [("pb", 0)],
                                S1[:, pl * 64 + hp * 16:pl * 64 + (hp + 1) * 16],
                                lhsT=KTg[1][slot][:, pl, hp, :], rhs=Qbd[:, 1, hp, :], start=True, stop=True)
                    EG, SPG, AG, WC = Eg[slot], SPg[slot], Ag[slot], Wc[slot]
                    P.i("act", "activation", [("pb", 0)], [("Eg", slot)], out=EG[:, :], in_=S1[:, 0:NC_], func=AF.Exp)
                    P.i("act", "activation", [("Eg", slot)], [("SPg", slot)], out=SPG[:, :, :].rearrange("p a c -> p (a c)"),
                        in_=EG[:, :], func=AF.Ln, bias=1.0)
                    P.i("dve", "tensor_copy", [("SPg", slot)], [("Wc", slot)], out=WC[:, 3, :], in_=SPG[:, 3, :])
                    for pl in range(2, -1, -1):
                        P.i("dve", "tensor_tensor", [("Wc", slot), ("SPg", slot)], [("Wc", slot)], out=WC[:, pl, :],
                            in0=WC[:, pl + 1, :], in1=SPG[:, pl, :], op=ALU.add)
                    P.i("pe", "matmul", [("Wc", slot)], [("pb", 1)], S2[:, 0:NC_], lhsT=negI,
                        rhs=WC[:, :, :].rearrange("p a c -> p (a c)"), start=True, stop=False)
                    P.i("pe", "matmul", [("Wc", slot)], [("pb", 1)], S2[:, 0:NC_], lhsT=Usneg,
                        rhs=WC[:, 0, :].unsqueeze(1).to_broadcast([128, 4, 64]), start=False, stop=False)
                    if g < 15:
                        P.i("pe", "matmul", ["carry"], [("pb", 1)], S2[:, 0:NC_], lhsT=negones,
                            rhs=carry[:, :].unsqueeze(1).to_broadcast([128, 4, 64]), start=False, stop=False)
                    if g > 0:
                        if g == 15:
                            P.i("dve", "tensor_copy", [("Wc", slot)], ["carry"], out=carry[:, :], in_=WC[:, 0, :])
                        else:
                            P.i("dve", "tensor_tensor", [("Wc", slot), "carry"], ["carry"], out=carry[:, :],
                                in0=carry[:, :], in1=WC[:, 0, :], op=ALU.add)
                    P.i("pe", "matmul", ["SPn"], [("pb", 1)], S2[:, 0:NC_], lhsT=negones[0:8, :],
                        rhs=SPn[0:8, :].unsqueeze(1).to_broadcast([8, 4, 64]), start=False, stop=False)
                    for pl in range(4):
                        for hp in range(4):
                            P.i("pe", "matmul", [("KTg", 1, slot, (pl // 2) * 2), "Qbd"], [("pb", 1)],
                                S2[:, pl * 64 + hp * 16:pl * 64 + (hp + 1) * 16],
                                lhsT=KTg[1][slot][:, pl, hp, :], rhs=Qbd[:, 1, hp, :], start=False, stop=False)
                    P.i("act", "activation", [("pb", 1)], [("Ag", slot)], out=AG[:, :, :].rearrange("p a c -> p (a c)"),
                        in_=S2[:, 0:NC_], func=AF.Exp)
                    for pl in range(4):
                        P.i("pe", "matmul", [("Ag", slot), ("Vsb", slot)], [("pb", 4)], Os[0:64, :], lhsT=AG[:, pl, :],
                            rhs=Vsb[slot][:, pl, :], start=False, stop=False)
                P.i("dve", "reciprocal", [("pb", 5)], ["rdn"], out=rdn[:, :], in_=Dn[0:64, 0:1])
                P.i("dve", "tensor_scalar", [("pb", 3), "rdn"], ["oa_sb"], out=oa_sb[:, :], in0=Oa[0:64, :],
                    scalar1=rdn[:, 0:1], scalar2=None, op0=ALU.mult)
                P.i("act", "activation", [("pb", 4)], ["os_sb"], out=os_sb[:, :], in_=Os[0:64, :], func=AF.Copy)
                r0 = SEQ + s * 8
                for h in range(8):
                    P.dma("sp", oa_scr[r0:r0 + 8, h * 64:(h + 1) * 64], oa_sb[h * 8:(h + 1) * 8, h * 64:(h + 1) * 64],
                          reads=["oa_sb"], key=("o", "osc_a"))
                    P.dma("sp", os_scr[r0:r0 + 8, h * 64:(h + 1) * 64], os_sb[h * 8:(h + 1) * 8, h * 64:(h + 1) * 64],
                          reads=["os_sb"], key=("o", "osc_s"))
            P.emit()

        with ExitStack() as s2:
            P = Prog(ctx)
            Wout = sb("Wout", [128, 8, D], BF16, s2)
            Wd = sb("Wd", [128, NFC, D], BF16, s2)
            w_out_v = w_out_b.rearrange("(kc p) n -> p kc n", p=128)
            w_d_v = w_d_b.rearrange("(fc p) n -> p fc n", p=128)
            for kc in range(0, 8, 2):
                P.dma("sp", Wout[:, kc:kc + 2, :], w_out_v[:, kc:kc + 2, :], writes=[("Wout", kc)])
            for fc in range(0, NFC, 2):
                P.dma("sp", Wd[:, fc:fc + 2, :], w_d_v[:, fc:fc + 2, :], writes=[("Wd", fc)])
            w_bm_v = w_bm_b.rearrange("(kc p) n -> p kc n", p=128)
            w_bs_v = w_bs_b.rearrange("(kc p) n -> p kc n", p=128)
            w_in_v = w_in_b.rearrange("(kc p) n -> p kc n", p=128)
            w_g_v = w_g_b.rearrange("(kc p) n -> p kc n", p=128)
            w_u_v = w_u_b.rearrange("(kc p) n -> p kc n", p=128)
            Wbr = [sb("Wbr%d" % i, [128, 2, 4, 128], BF16, s2) for i in range(2)]
            Wgt = [sb("Wgt%d" % i, [128, 2, 8, 128], BF16, s2) for i in range(2)]
            Wgu = [sb("Wgu%d" % i, [128, 2, 8, 128], BF16, s2) for i in range(4)]
            hTg = sb("hTg", [128, 8, 512], BF16, s2)
            otg = sb("otg", [128, 2, 4, 512], BF16, s2)
            oT = sb("oT", [128, 2, 4, 512], BF16, s2)
            mergedT = sb("mergedT", [128, 8, 512], BF16, s2)
            x1 = sb("x1", [128, 4, D], F32, s2)
            xin = sb("xin", [128, D], F32, s2)
            h2 = [sb("h2_%d" % i, [128, D], BF16, s2) for i in range(2)]
            h2T = sb("h2T", [128, 8, 512], BF16, s2)
            ffT = sb("ffT", [128, NFC, 512], BF16, s2)
            ea = sb("ea", [128, 512], F32, s2)
            ebb = sb("ebb", [128, 512], F32, s2)
            ma = sb("ma", [128, 512], F32, s2)
            ssd = sb("ssd", [128, 8], F32, s2)
            sqd = sb("sqd", [128, D], BF16, s2)

            NG = 5
            for g in range(NG):
                if g < 4:
                    t0, ntile, ntok, r0 = g * 4, 4, 512, g * 512
                    tiles = [(i, 128) for i in range(4)]
                else:
                    t0, ntile, ntok, r0 = NT, 1, ST, SEQ
                    tiles = [(0, ST)]
                gk = ("g", g)
                P.dma("sp", hTg[:, :, 0:ntok], hT_scr[:, :, r0:r0 + ntok], writes=["hTg"])
                for a, scr in ((0, oa_scr), (1, os_scr)):
                    for (i, nr) in tiles:
                        P.dma("sp", otg[0:nr, a, i, :], scr[r0 + i * 128:r0 + i * 128 + nr, :], writes=[("otg", a, i)])
                for a in range(2):
                    for (i, nr) in tiles:
                        pT = pt16[(a * 4 + i) % 2]
                        pk = ("pt16", (a * 4 + i) % 2)
                        for kc in range(4):
                            P.op("pe", lambda e, pT=pT, a=a, i=i, kc=kc, nr=nr: e.transpose(
                                out=pT[:, kc * 128:kc * 128 + nr], in_=otg[0:nr, a, i, kc * 128:(kc + 1) * 128],
                                identity=ident[0:nr, 0:nr]),
                                reads=[("otg", a, i), "consts"], writes=[pk])
                        P.op("act", lambda e, pT=pT, a=a, i=i, nr=nr: e.activation(
                            out=oT[:, a, :, i * 128:i * 128 + nr],
                            in_=pT[:, 0:512].rearrange("p (k t) -> p k t", t=128)[:, :, 0:nr], func=AF.Copy),
                            reads=[pk], writes=[("oT", a, i)])
                oT_reads = [("oT", a, i) for a in range(2) for (i, _) in tiles]
                for dc in range(8):
                    ws = (g * 8 + dc) % 2
                    WB, WG = Wbr[ws], Wgt[ws]
                    P.dma("sp", WB[:, 0, :, :], w_bm_v[:, :, dc * 128:(dc + 1) * 128], writes=[("Wbr", ws, 0)])
                    P.dma("sp", WB[:, 1, :, :], w_bs_v[:, :, dc * 128:(dc + 1) * 128], writes=[("Wbr", ws, 1)])
                    P.dma("sp", WG[:, 0, :, :], w_in_v[:, :, 3072 + dc * 128:3072 + (dc + 1) * 128], writes=[("Wgt", ws, 0)])
                    P.dma("sp", WG[:, 1, :, :], w_in_v[:, :, 4096 + dc * 128:4096 + (dc + 1) * 128], writes=[("Wgt", ws, 1)])
                    Ba, Bs, Ga, Gs = pb[0], pb[1], pb[2], pb[3]
                    for a, Bx in ((0, Ba), (1, Bs)):
                        for kc in range(4):
                            P.op("pe", lambda e, Bx=Bx, WB=WB, a=a, kc=kc: e.matmul(
                                Bx[:, 0:ntok], lhsT=WB[:, a, kc, :], rhs=oT[:, a, kc, 0:ntok], start=(kc == 0), stop=(kc == 3)),
                                reads=[("Wbr", ws, a)] + oT_reads, writes=[("pb", a)])
                    for a, Gx in ((0, Ga), (1, Gs)):
                        for kc in range(8):
                            P.op("pe", lambda e, Gx=Gx, WG=WG, a=a, kc=kc: e.matmul(
                                Gx[:, 0:ntok], lhsT=WG[:, a, kc, :], rhs=hTg[:, kc, 0:ntok], start=(kc == 0), stop=(kc == 7)),
                                reads=[("Wgt", ws, a), "hTg"], writes=[("pb", 2 + a)])
                    P.op("act", lambda e, Ga=Ga: e.activation(out=ea[:, 0:ntok], in_=Ga[:, 0:ntok], func=AF.Exp, scale=-1.0),
                         reads=[("pb", 2)], writes=["ea"])
                    P.op("act", lambda e, Gs=Gs: e.activation(out=ebb[:, 0:ntok], in_=Gs[:, 0:ntok], func=AF.Exp, scale=-1.0),
                         reads=[("pb", 3)], writes=["ebb"])
                    for nm_, t_ in (("ea", ea), ("ebb", ebb)):
                        P.i("act", "activation", [nm_], [nm_], out=t_[:, 0:ntok], in_=t_[:, 0:ntok], func=AF.Ln, bias=1.0)
                        P.i("act", "activation", [nm_], [nm_], out=t_[:, 0:ntok], in_=t_[:, 0:ntok], func=AF.Exp, scale=-1.0)
                    P.op("dve", lambda e, Ba=Ba: e.tensor_tensor(out=ma[:, 0:ntok], in0=Ba[:, 0:ntok], in1=ea[:, 0:ntok], op=ALU.mult),
                         reads=[("pb", 0), "ea"], writes=["ma"])
                    P.op("dve", lambda e, Bs=Bs: e.tensor_tensor(out=ebb[:, 0:ntok], in0=Bs[:, 0:ntok], in1=ebb[:, 0:ntok], op=ALU.mult),
                         reads=[("pb", 1), "ebb"], writes=["ebb"])
                    P.op("dve", lambda e, dc=dc: e.tensor_tensor(out=mergedT[:, dc, 0:ntok], in0=ma[:, 0:ntok], in1=ebb[:, 0:ntok], op=ALU.add),
                         reads=["ma", "ebb"], writes=[("mergedT", dc)])
                mreads = [("mergedT", dc) for dc in range(8)]
                for (i, nr) in tiles:
                    P.dma("sp", xin[0:nr, :], (xp[r0 + i * 128:r0 + i * 128 + nr, :] if g < 4 else xs[:, :]), writes=["xin"])
                    for half in range(2):
                        acc = pb[4 + half]
                        for dc in range(8):
                            P.op("pe", lambda e, acc=acc, dc=dc, i=i, nr=nr, half=half: e.matmul(
                                acc[0:nr, :], lhsT=mergedT[:, dc, i * 128:i * 128 + nr], rhs=Wout[:, dc, half * 512:(half + 1) * 512],
                                start=(dc == 0), stop=(dc == 7)),
                                reads=mreads + [("Wout", (dc // 2) * 2)], writes=[("pb", 4 + half)])
                        P.op("dve", lambda e, acc=acc, i=i, nr=nr, half=half: e.tensor_tensor(
                            out=x1[0:nr, i, half * 512:(half + 1) * 512], in0=acc[0:nr, :], in1=xin[0:nr, half * 512:(half + 1) * 512], op=ALU.add),
                            reads=[("pb", 4 + half), "xin"], writes=[("x1", i, half)])
                    x1r = [("x1", i, 0), ("x1", i, 1)]
                    P.op("act", lambda e, i=i, nr=nr: e.activation(out=sqd[0:nr, :], in_=x1[0:nr, i, :], func=AF.Square, accum_out=ssd[0:nr, 0:1]),
                         reads=x1r, writes=["sqd", "ssd"])
                    P.op("act", lambda e, nr=nr: e.activation(out=ssd[0:nr, 1:2], in_=ssd[0:nr, 0:1], func=AF.Ln, scale=1.0 / D, bias=EPS),
                         reads=["ssd"], writes=["ssd"])
                    P.op("act", lambda e, nr=nr: e.activation(out=ssd[0:nr, 2:3], in_=ssd[0:nr, 1:2], func=AF.Exp, scale=-0.5),
                         reads=["ssd"], writes=["ssd"])
                    hs = i % 2
                    H2 = h2[hs]
                    P.op("dve", lambda e, H2=H2, i=i, nr=nr: e.tensor_scalar(
                        out=H2[0:nr, :], in0=x1[0:nr, i, :], scalar1=ssd[0:nr, 2:3], scalar2=None, op0=ALU.mult),
                        reads=x1r + ["ssd"], writes=[("h2", hs)])
                    pT = pt16[hs]
                    for kc in range(8):
                        P.op("pe", lambda e, pT=pT, H2=H2, kc=kc, nr=nr: e.transpose(
                            out=pT[:, kc * 128:kc * 128 + nr], in_=H2[0:nr, kc * 128:(kc + 1) * 128], identity=ident[0:nr, 0:nr]),
                            reads=[("h2", hs), "consts"], writes=[("pt16", hs)])
                    P.op("dve", lambda e, pT=pT, i=i, nr=nr: e.tensor_tensor(
                        out=h2T[:, :, i * 128:i * 128 + nr],
                        in0=pT[:, :].rearrange("p (k t) -> p k t", t=128)[:, :, 0:nr],
                        in1=gffn_sb[:, :].unsqueeze(2).to_broadcast([128, 8, nr]), op=ALU.mult),
                        reads=[("pt16", hs), "gffn"], writes=[("h2T", i)])
                h2r = [("h2T", i) for (i, _) in tiles]
                for fc in range(NFC):
                    ws = (g * NFC + fc) % 4
                    WGU = Wgu[ws]
                    P.dma("sp", WGU[:, 0, :, :], w_g_v[:, :, fc * 128:(fc + 1) * 128], writes=[("Wgu", ws, 0)])
                    P.dma("sp", WGU[:, 1, :, :], w_u_v[:, :, fc * 128:(fc + 1) * 128], writes=[("Wgu", ws, 1)])
                    pg, pu = pb[(fc % 2) * 2], pb[(fc % 2) * 2 + 1]
                    kg, ku = ("pb", (fc % 2) * 2), ("pb", (fc % 2) * 2 + 1)
                    for a, px, kx in ((0, pg, kg), (1, pu, ku)):
                        for kc in range(8):
                            P.op("pe", lambda e, px=px, WGU=WGU, a=a, kc=kc: e.matmul(
                                px[:, 0:ntok], lhsT=WGU[:, a, kc, :], rhs=h2T[:, kc, 0:ntok], start=(kc == 0), stop=(kc == 7)),
                                reads=[("Wgu", ws, a)] + h2r, writes=[kx])
                    P.op("act", lambda e, pg=pg: e.activation(out=ea[:, 0:ntok], in_=pg[:, 0:ntok], func=AF.Exp, scale=-1.0),
                         reads=[kg], writes=["ea"])
                    P.i("act", "activation", ["ea"], ["ea"], out=ea[:, 0:ntok], in_=ea[:, 0:ntok], func=AF.Ln, bias=1.0)
                    P.i("act", "activation", ["ea"], ["ea"], out=ea[:, 0:ntok], in_=ea[:, 0:ntok], func=AF.Exp, scale=-1.0)
                    P.op("dve", lambda e, pg=pg: e.tensor_tensor(out=ma[:, 0:ntok], in0=pg[:, 0:ntok], in1=ea[:, 0:ntok], op=ALU.mult),
                         reads=[kg, "ea"], writes=["ma"])
                    P.op("dve", lambda e, pu=pu, fc=fc: e.tensor_tensor(out=ffT[:, fc, 0:ntok], in0=pu[:, 0:ntok], in1=ma[:, 0:ntok], op=ALU.mult),
                         reads=[ku, "ma"], writes=[("ffT", fc)])
                ffr = [("ffT", fc) for fc in range(NFC)]
                for (i, nr) in tiles:
                    for half in range(2):
                        acc = pb[4 + half]
                        for fc in range(NFC):
                            P.op("pe", lambda e, acc=acc, fc=fc, i=i, nr=nr, half=half: e.matmul(
                                acc[0:nr, :], lhsT=ffT[:, fc, i * 128:i * 128 + nr], rhs=Wd[:, fc, half * 512:(half + 1) * 512],
                                start=(fc == 0), stop=(fc == NFC - 1)),
                                reads=ffr + [("Wd", (fc // 2) * 2)], writes=[("pb", 4 + half)])
                        P.op("dve", lambda e, acc=acc, i=i, nr=nr, half=half: e.tensor_tensor(
                            out=x1[0:nr, i, half * 512:(half + 1) * 512], in0=acc[0:nr, :], in1=x1[0:nr, i, half * 512:(half + 1) * 512], op=ALU.add),
                            reads=[("pb", 4 + half), ("x1", i, half)], writes=[("x1", i, half)])
                    x1r = [("x1", i, 0), ("x1", i, 1)]
                    P.op("act", lambda e, i=i, nr=nr: e.activation(out=sqd[0:nr, :], in_=x1[0:nr, i, :], func=AF.Square, accum_out=ssd[0:nr, 4:5]),
                         reads=x1r, writes=["sqd", "ssd2"])
                    P.op("act", lambda e, nr=nr: e.activation(out=ssd[0:nr, 5:6], in_=ssd[0:nr, 4:5], func=AF.Ln, scale=1.0 / D, bias=EPS),
                         reads=["ssd2"], writes=["ssd2"])
                    P.op("act", lambda e, nr=nr: e.activation(out=ssd[0:nr, 6:7], in_=ssd[0:nr, 5:6], func=AF.Exp, scale=-0.5),
                         reads=["ssd2"], writes=["ssd2"])
                    P.op("dve", lambda e, i=i, nr=nr: e.scalar_tensor_tensor(
                        out=x1[0:nr, i, :], in0=x1[0:nr, i, :], scalar=ssd[0:nr, 6:7], in1=gfin_sb[0:nr, :], op0=ALU.mult, op1=ALU.mult),
                        reads=x1r + ["ssd2", "gfin"], writes=x1r)
                    ydst = yp[r0 + i * 128:r0 + i * 128 + nr, :] if g < 4 else ys[:, :]
                    P.dma("pool", ydst, x1[0:nr, i, :], reads=x1r, key=("o", "y", i))
            P.emit()
    return nc


_NC_CACHE = {}


def _consts():
    k = np.arange(128)[:, None]
    q = np.arange(128)[None, :]
    c = np.zeros((128, 8, 128), np.float32)
    c[:, 0, :] = np.eye(128)
    c[:, 1, :] = np.where(k <= q, 0.0, NEG)
    c[:, 2, :] = np.where(k < q, 0.0, NEG)
    c[:, 3, :] = np.where(k >= q, -1.0, 0.0)
    c[:, 4, :] = -1.0
    c[:, 5, :] = 1.0
    c[:, 6, :] = -np.eye(128)
    c[:, 7, :] = np.where(k > q, -1.0, 0.0)
    eb = np.zeros((8, 8, 128), np.float32)
    for n in range(8):
        eb[n, n, :] = 1.0
    kk = np.arange(8)[:, None, None]
    qq = np.arange(8)[None, None, :]
    cm8 = np.zeros((8, 2, 8, 8), np.float32)
    cm8[:, 0] = np.where(kk <= qq, 0.0, NEG)
    cm8[:, 1] = np.where(kk < qq, 0.0, NEG)
    cm8 = cm8.reshape(8, 2, 64)
    pp = np.arange(128)
    oh4 = (pp[:, None] // 32 == np.arange(4)[None, :]).astype(np.float32)
    pmod = (pp % 32).astype(np.float32).reshape(128, 1)
    blk = 2 * np.arange(16)[None, :] + pp[:, None] // 64
    zsel2 = (blk[:, :, None] == np.arange(32)[None, None, :]).astype(np.float32) / 256.0
    eg = (np.arange(32)[:, None, None] == blk.T[None, :, :]).astype(np.float32)
    bf = ml_dtypes.bfloat16
    return c.astype(bf), eb.astype(bf), cm8.astype(bf), oh4, pmod, zsel2.astype(bf), eg.astype(bf)


def _rope_table(pos):
    half = 32
    inv = (10000.0 ** (-np.arange(half, dtype=np.float32) / half)).astype(np.float32)
    ang = pos.astype(np.float32)[:, None] * inv[None, :]
    return np.concatenate([np.cos(ang), np.sin(ang)], axis=1).astype(np.float32)


def kernel(x_prompt, x_sample, cache_moba_k, cache_moba_v, cache_sb_k, cache_sb_v,
           page_table, g_mix, w_in, w_branch_moba, w_branch_sb, w_out,
           g_ffn, w_ffn_gate, w_ffn_up, w_ffn_down, g_final):
    f = lambda a: np.ascontiguousarray(np.asarray(a))
    if "nc" not in _NC_CACHE:
        _NC_CACHE["nc"] = build_nc()
    nc = _NC_CACHE["nc"]
    cbf, ebk, cm8, oh4, pmod, zsel2, eg = _consts()
    past_len = page_table.shape[1] * 128
    pos = np.concatenate([np.arange(SEQ), np.tile(past_len + np.arange(8), 4)])
    cs_all = _rope_table(pos)
    caches = [f(c).reshape(NPOOL * 128, 512) for c in (cache_moba_k, cache_moba_v, cache_sb_k, cache_sb_v)]
    shared = {
        "cmk": caches[0], "cmv": caches[1], "csk": caches[2], "csv": caches[3],
        "w_in": f(w_in)[0], "w_bm": f(w_branch_moba)[0], "w_bs": f(w_branch_sb)[0], "w_out": f(w_out)[0],
        "w_g": f(w_ffn_gate)[0], "w_u": f(w_ffn_up)[0], "w_d": f(w_ffn_down)[0],
        "gmixT": f(f(g_mix)[0].reshape(8, 128).T), "gffnT": f(f(g_ffn)[0].reshape(8, 128).T),
        "gfin": f(g_final).reshape(1, D), "cs_all": cs_all, "cbf": cbf, "ebk": ebk,
        "cm8": cm8, "oh4": oh4, "pmod": pmod, "zsel2": zsel2, "eg": eg,
    }
    xpn, xsn, ptn = f(x_prompt), f(x_sample), f(page_table).astype(np.int32)
    in_maps = []
    for c in range(NCORES):
        m = dict(shared)
        m["xp"] = xpn[c]
        m["xs"] = xsn[4 * c:4 * c + 4].reshape(ST, D)
        m["pt"] = ptn[4 * c:4 * c + 4]
        in_maps.append(m)
    res = run_bass_kernel_spmd(nc, in_maps, core_ids=list(range(NCORES)))
    R = res.results
    y_prompt = np.stack([R[c]["yp"] for c in range(NCORES)]).astype(np.float32)
    y_sample = np.concatenate([R[c]["ys"].reshape(4, 8, D) for c in range(NCORES)]).astype(np.float32)
    outs = [y_prompt, y_sample]
    for nm in ("mk", "mv", "sk", "sv"):
        outs.append(np.stack([R[c][nm + "p"].reshape(SEQ, 8, 64) for c in range(NCORES)])[None].astype(np.float32))
    for nm in ("mk", "mv", "sk", "sv"):
        outs.append(np.concatenate([R[c][nm + "s"].reshape(4, 8, 8, 64) for c in range(NCORES)])[None].astype(np.float32))
    return tuple(outs)
```

```python
import numpy as np
import ml_dtypes
from contextlib import ExitStack
import concourse.bass as bass
import concourse.mybir as mybir
from concourse.bass_utils import run_bass_kernel_spmd

F32 = mybir.dt.float32
BF16 = mybir.dt.bfloat16
I32 = mybir.dt.int32
AF = mybir.ActivationFunctionType
ALU = mybir.AluOpType
AX = mybir.AxisListType

ENGS = ("pe", "act", "dve", "pool", "sp")
NCORES = 8
D = 1024
SEQ = 2048
NT = 16
ST = 32
NTOK = SEQ + ST
DFF = 2816
NFC = 22
NPOOL = 2560
NPAGES = 64
NEG = -30000.0
EPS = 1e-6


class Op:
    __slots__ = ("eng", "fn", "reads", "writes", "dma", "key", "deps", "needed", "token", "idx")

    def __init__(self, eng, fn, reads, writes, dma, key):
        self.eng, self.fn, self.reads, self.writes, self.dma, self.key = eng, fn, reads, writes, dma, key
        self.deps = []
        self.needed = False
        self.token = None


class _Rec:
    call = None

    def __getattr__(self, name):
        def f(*a, **k):
            self.call = (name, a, k)
            return None
        return f


class Ctx:
    def __init__(self, nc, es, n_dma_sems=40):
        self.nc = nc
        self.esem = {e: es.enter_context(nc.semaphore("e_" + e)) for e in ENGS if e != "sp"}
        self.cnt = {e: 0 for e in ENGS}
        self.dpool = [es.enter_context(nc.semaphore("d%d" % i)) for i in range(n_dma_sems)]
        self.dcnt = [0] * n_dma_sems
        self.out_waits = {}


class Prog:
    def __init__(self, ctx, paranoid=True):
        self.ctx = ctx
        self.ops = []
        self.paranoid = paranoid

    def op(self, eng, fn, reads=(), writes=(), dma=False, key=None):
        rec = _Rec()
        fn(rec)
        name, a, k = rec.call
        fn = lambda e: getattr(e, name)(*a, **k)
        o = Op(eng, fn, tuple(reads), tuple(writes), dma, key)
        o.idx = len(self.ops)
        self.ops.append(o)
        return o

    def i(self, eng, name, reads, writes, *args, **kwargs):
        return self.op(eng, lambda e: getattr(e, name)(*args, **kwargs), reads, writes)

    def dma(self, q, out, in_, reads=(), writes=(), key=None):
        return self.op(q, lambda e: e.dma_start(out=out, in_=in_), reads, writes, dma=True, key=key)

    def analyze(self):
        last_w, readers = {}, {}
        for o in self.ops:
            deps = {}
            for r in o.reads:
                w = last_w.get(r)
                if w is not None:
                    deps[w.idx] = (w, "raw")
            for r in o.writes:
                w = last_w.get(r)
                if w is not None and w.idx not in deps:
                    if not (w.dma and o.dma and w.key is not None and w.key == o.key):
                        deps[w.idx] = (w, "waw")
                for rd in readers.get(r, ()):
                    if rd.idx not in deps:
                        deps[rd.idx] = (rd, "war")
            for r in o.reads:
                readers.setdefault(r, []).append(o)
            for r in o.writes:
                last_w[r] = o
                readers[r] = []
            out = []
            for d, kind in deps.values():
                if d is o:
                    continue
                if (not d.dma) and d.eng == o.eng:
                    if d.eng in ("pe", "sp"):
                        continue
                    if kind == "war" or not self.paranoid:
                        continue
                out.append(d)
                d.needed = True
            o.deps = out
        for e in ENGS:
            if e == "sp":
                continue
            for o in reversed(self.ops):
                if o.eng == e and not o.dma:
                    o.needed = True
                    break

    def emit(self):
        ctx = self.ctx
        nc = ctx.nc
        self.analyze()
        dkey = {}
        for o in self.ops:
            if o.dma:
                k = o.key if o.key is not None else (o.writes[0] if o.writes else ("dma", o.idx))
                if k not in dkey:
                    dkey[k] = len(dkey)
                    assert len(dkey) <= len(ctx.dpool), "too many DMA semaphores"
                i = dkey[k]
                ctx.dcnt[i] += 16
                o.token = (ctx.dpool[i], ctx.dcnt[i])
            elif o.needed:
                ctx.cnt[o.eng] += 1
                o.token = (ctx.esem[o.eng], ctx.cnt[o.eng])
        per_eng = {e: [o for o in self.ops if o.eng == e] for e in ENGS}
        fin_e = dict(ctx.cnt)
        fin_d = list(ctx.dcnt)

        def run(engobj, ename):
            know = {}
            for o in per_eng[ename]:
                for d in o.deps:
                    s, v = d.token
                    if know.get(id(s), 0) < v:
                        engobj.wait_ge(s, v)
                        know[id(s)] = v
                ins = o.fn(engobj)
                if o.token is not None:
                    ins.then_inc(o.token[0], 16 if o.dma else 1)
            for e2 in ENGS:
                if e2 == "sp" or e2 == ename:
                    continue
                if fin_e[e2] > 0 and know.get(id(ctx.esem[e2]), 0) < fin_e[e2]:
                    engobj.wait_ge(ctx.esem[e2], fin_e[e2])
            for i in range(len(dkey)):
                if fin_d[i] > 0 and know.get(id(ctx.dpool[i]), 0) < fin_d[i]:
                    engobj.wait_ge(ctx.dpool[i], fin_d[i])

        with nc.Block() as block:
            @block.tensor
            def _(e):
                run(e, "pe")

            @block.scalar
            def _(e):
                run(e, "act")

            @block.vector
            def _(e):
                run(e, "dve")

            @block.gpsimd
            def _(e):
                run(e, "pool")

            @block.sync
            def _(e):
                run(e, "sp")


def build_nc():
    nc = bass.Bass("TRN2", target_bir_lowering=False)

    def din(name, shape, dt=F32):
        return nc.dram_tensor(name, list(shape), dt, kind="ExternalInput").ap()

    def dout(name, shape, dt=F32):
        return nc.dram_tensor(name, list(shape), dt, kind="ExternalOutput").ap()

    xp = din("xp", [SEQ, D])
    xs = din("xs", [ST, D])
    cmk = din("cmk", [NPOOL * 128, 512])
    cmv = din("cmv", [NPOOL * 128, 512])
    csk = din("csk", [NPOOL * 128, 512])
    csv = din("csv", [NPOOL * 128, 512])
    ptab = din("pt", [4, NPAGES], I32)
    w_in = din("w_in", [D, 5120])
    w_bm = din("w_bm", [512, D])
    w_bs = din("w_bs", [512, D])
    w_out = din("w_out", [D, D])
    w_g = din("w_g", [D, DFF])
    w_u = din("w_u", [D, DFF])
    w_d = din("w_d", [DFF, D])
    gmixT = din("gmixT", [128, 8])
    gffnT = din("gffnT", [128, 8])
    gfin = din("gfin", [1, D])
    cs_all = din("cs_all", [NTOK, 64])
    cbf = din("cbf", [128, 8, 128], BF16)
    ebk = din("ebk", [8, 8, 128], BF16)
    cm8_d = din("cm8", [8, 2, 64], BF16)
    oh4_d = din("oh4", [128, 4])
    pmod_d = din("pmod", [128, 1])
    zsel_d = din("zsel2", [128, 16, 32], BF16)
    eg_d = din("eg", [32, 16, 128], BF16)

    yp = dout("yp", [SEQ, D])
    ys = dout("ys", [ST, D])
    o_kv = {}
    for nm in ("mk", "mv", "sk", "sv"):
        o_kv[nm] = (dout(nm + "p", [SEQ, 512]), dout(nm + "s", [ST, 512]))

    def wscr(name, shape):
        return nc.dram_tensor(name, list(shape), BF16, kind="Internal").ap()

    w_in_b = wscr("w_in_b", [D, 5120])
    w_bm_b = wscr("w_bm_b", [512, D])
    w_bs_b = wscr("w_bs_b", [512, D])
    w_out_b = wscr("w_out_b", [D, D])
    w_g_b = wscr("w_g_b", [D, DFF])
    w_u_b = wscr("w_u_b", [D, DFF])
    w_d_b = wscr("w_d_b", [DFF, D])
    hT_scr = nc.dram_tensor("hT_scr", [128, 8, NTOK], BF16, kind="Internal").ap()
    oa_scr = nc.dram_tensor("oa_scr", [NTOK, 512], BF16, kind="Internal").ap()
    os_scr = nc.dram_tensor("os_scr", [NTOK, 512], BF16, kind="Internal").ap()

    def tok_rows(t):
        return (t * 128, 128) if t < NT else (SEQ, ST)

    def x_src(t):
        return xp[t * 128:(t + 1) * 128, :] if t < NT else xs[:, :]

    with ExitStack() as es:
        def sb(name, shape, dt, stack=None):
            return (stack or es).enter_context(nc.sbuf_tensor(name, list(shape), dt))

        def ps(name, shape, dt):
            return es.enter_context(nc.psum_tensor(name, list(shape), dt))

        ctx = Ctx(nc, es, n_dma_sems=48)
        pb = [ps("pb%d" % i, [128, 512], F32) for i in range(6)]
        pt16 = [ps("pt16_%d" % i, [128, 1024], BF16) for i in range(2)]

        consts = sb("consts", [128, 8, 128], BF16)
        ident = consts[:, 0, :]
        mask_incl = consts[:, 1, :]
        mask_strict = consts[:, 2, :]
        Uneg = consts[:, 3, :]
        negones = consts[:, 4, :]
        ones = consts[:, 5, :]
        negI = consts[:, 6, :]
        Usneg = consts[:, 7, :]
        eb = sb("eb", [8, 8, 128], BF16)
        gmix_sb = sb("gmix_sb", [128, 8], F32)
        gffn_sb = sb("gffn_sb", [128, 8], F32)
        gfin_sb = sb("gfin_sb", [128, D], F32)
        zeros16 = sb("zeros16", [128, 512], BF16)
        QTa_s = sb("QTa_s", [128, 4, ST], BF16)
        KTa_s = sb("KTa_s", [128, 4, ST], BF16)
        QTs_s = sb("QTs_s", [128, 4, ST], BF16)
        KTs_s = sb("KTs_s", [128, 4, ST], BF16)
        Vna = sb("Vna", [ST, 512], BF16)
        Vns = sb("Vns", [ST, 512], BF16)

        with ExitStack() as s1:
            hT = sb("hT", [128, 8, NTOK], BF16, s1)
            QTa = sb("QTa", [128, 4, SEQ], BF16, s1)
            KTa = sb("KTa", [128, 4, SEQ], BF16, s1)
            QTs = sb("QTs", [128, 4, SEQ], BF16, s1)
            KTs = sb("KTs", [128, 4, SEQ], BF16, s1)
            Va = sb("Va", [128, NT, 8, 65], BF16, s1)
            Vs = sb("Vs", [128, NT, 512], BF16, s1)

            with ExitStack() as s2:
                P = Prog(ctx)
                P.dma("sp", consts[:], cbf, writes=["consts"])
                P.dma("sp", eb[:], ebk, writes=["eb"])
                P.dma("sp", gmix_sb[:], gmixT, writes=["gmix"])
                P.dma("sp", gffn_sb[:], gffnT, writes=["gffn"])
                P.dma("sp", gfin_sb[:], gfin.partition_broadcast(128).rearrange("p a d -> p (a d)"), writes=["gfin"])
                for nm, src, dst, a in (("w_in", w_in, w_in_b, 4),):
                    P.dma("pool", dst.rearrange("k (a n) -> k a n", a=a), src.rearrange("k (a n) -> k a n", a=a),
                          writes=[("wscr", nm)], key=("wscr", nm))
                P.op("pool", lambda e: e.memset(zeros16[:], 0.0), writes=["zeros16"])
                P.op("pool", lambda e: e.memset(Va[:, :, :, 64:65], 1.0), writes=["Va_ones"])

                xt = [sb("xt%d" % i, [128, D], F32, s2) for i in range(2)]
                sq = sb("sq", [128, D], BF16, s2)
                ssq = [sb("ssq%d" % i, [128, 4], F32, s2) for i in range(2)]
                hb = [sb("hb%d" % i, [128, D], BF16, s2) for i in range(2)]
                cs_sb = sb("cs_sb", [128, NT + 1, 64], F32, s2)
                for t in range(NT + 1):
                    r0, nr = tok_rows(t)
                    P.dma("sp", cs_sb[0:nr, t, :], cs_all[r0:r0 + nr, :], writes=[("cs", t)])

                for t in range(NT + 1):
                    r0, nr = tok_rows(t)
                    b = t % 2
                    X, SS, HB = xt[b], ssq[b], hb[b]
                    P.dma("sp", X[0:nr, :], x_src(t), writes=[("xt", b)])
                    P.op("act", lambda e, X=X, SS=SS, nr=nr: e.activation(
                        out=sq[0:nr, :], in_=X[0:nr, :], func=AF.Square, accum_out=SS[0:nr, 0:1]),
                        reads=[("xt", b)], writes=["sq", ("ssq", b)])
                    P.op("act", lambda e, SS=SS, nr=nr: e.activation(
                        out=SS[0:nr, 1:2], in_=SS[0:nr, 0:1], func=AF.Ln, scale=1.0 / D, bias=EPS),
                        reads=[("ssq", b)], writes=[("ssq", b)])
                    P.op("act", lambda e, SS=SS, nr=nr: e.activation(
                        out=SS[0:nr, 2:3], in_=SS[0:nr, 1:2], func=AF.Exp, scale=-0.5),
                        reads=[("ssq", b)], writes=[("ssq", b)])
                    P.op("dve", lambda e, X=X, SS=SS, HB=HB, nr=nr: e.tensor_scalar(
                        out=HB[0:nr, :], in0=X[0:nr, :], scalar1=SS[0:nr, 2:3], scalar2=None, op0=ALU.mult),
                        reads=[("xt", b), ("ssq", b)], writes=[("hb", b)])
                    pT = pt16[b]
                    for kc in range(8):
                        P.op("pe", lambda e, pT=pT, HB=HB, kc=kc, nr=nr: e.transpose(
                            out=pT[:, kc * 128:kc * 128 + nr], in_=HB[0:nr, kc * 128:(kc + 1) * 128],
                            identity=ident[0:nr, 0:nr]),
                            reads=[("hb", b), "consts"], writes=[("pt16", b)])
                    P.op("dve", lambda e, pT=pT, r0=r0, nr=nr: e.tensor_tensor(
                        out=hT[:, :, r0:r0 + nr],
                        in0=pT[:, :].rearrange("p (k t) -> p k t", t=128)[:, :, 0:nr],
                        in1=gmix_sb[:, :].unsqueeze(2).to_broadcast([128, 8, nr]), op=ALU.mult),
                        reads=[("pt16", b), "gmix"], writes=[("hT", t)])
                    P.dma("sp", hT_scr[:, :, r0:r0 + nr], hT[:, :, r0:r0 + nr], reads=[("hT", t)], key=("o", "hTs"))

                Wb = [sb("Wb%d" % i, [128, 8, 512], BF16, s2) for i in range(2)]
                raw = [sb("raw%d" % i, [128, 512], F32, s2) for i in range(2)]
                stg = [sb("stg%d" % i, [128, 512], F32, s2) for i in range(3)]
                tmpv = sb("tmpv", [128, 256], F32, s2)
                tmpg = sb("tmpg", [128, 256], F32, s2)
                qb = [sb("qb%d" % i, [128, 512], BF16, s2) for i in range(2)]
                w_in_v = w_in_b.rearrange("(kc p) n -> p kc n", p=128)
                it = 0
                for cb in range(6):
                    wslot = cb % 2
                    W = Wb[wslot]
                    for half in range(2):
                        P.dma("sp", W[:, half * 4:(half + 1) * 4, :],
                              w_in_v[:, half * 4:(half + 1) * 4, cb * 512:(cb + 1) * 512],
                              reads=[("wscr", "w_in")], writes=[("Wb", wslot, half)])
                    kind = ("qa", "ka", "va", "qs", "ks", "vs")[cb]
                    for t in range(NT + 1):
                        r0, nr = tok_rows(t)
                        acc = pb[it % 2]
                        akey = ("pb", it % 2)
                        for kc in range(8):
                            P.op("pe", lambda e, acc=acc, W=W, kc=kc, r0=r0, nr=nr: e.matmul(
                                acc[0:nr, :], lhsT=hT[:, kc, r0:r0 + nr], rhs=W[:, kc, :],
                                start=(kc == 0), stop=(kc == 7)),
                                reads=[("hT", t), ("Wb", wslot, kc // 4)], writes=[akey])
                        rslot = it % 2
                        R = raw[rslot]
                        sslot = it % 3
                        S = stg[sslot]
                        qslot = it % 2
                        Q = qb[qslot]
                        it += 1
                        if kind in ("qa", "ka"):
                            P.op("act", lambda e, R=R, acc=acc, nr=nr: e.activation(out=R[0:nr, :], in_=acc[0:nr, :], func=AF.Copy),
                                 reads=[akey], writes=[("raw", rslot)])
                            Rv = R[:, :].rearrange("p (h t d) -> p h t d", t=2, d=32)
                            Sv = S[:, :].rearrange("p (h t d) -> p h t d", t=2, d=32)
                            cosb = cs_sb[0:nr, t, 0:32].unsqueeze(1).to_broadcast([nr, 8, 32])
                            sinb = cs_sb[0:nr, t, 32:64].unsqueeze(1).to_broadcast([nr, 8, 32])
                            tv = tmpv[:, :].rearrange("p (h d) -> p h d", d=32)
                            tg = tmpg[:, :].rearrange("p (h d) -> p h d", d=32)
                            P.op("dve", lambda e, Sv=Sv, Rv=Rv, cosb=cosb, nr=nr: e.tensor_tensor(
                                out=Sv[0:nr, :, 0, :], in0=Rv[0:nr, :, 0, :], in1=cosb, op=ALU.mult),
                                reads=[("raw", rslot), ("cs", t)], writes=[("stg", sslot, 0)])
                            P.op("dve", lambda e, tv=tv, Rv=Rv, sinb=sinb, nr=nr: e.tensor_tensor(
                                out=tv[0:nr], in0=Rv[0:nr, :, 1, :], in1=sinb, op=ALU.mult),
                                reads=[("raw", rslot), ("cs", t)], writes=["tmpv"])
                            P.op("dve", lambda e, Sv=Sv, tv=tv, nr=nr: e.tensor_tensor(
                                out=Sv[0:nr, :, 0, :], in0=Sv[0:nr, :, 0, :], in1=tv[0:nr], op=ALU.subtract),
                                reads=["tmpv", ("stg", sslot, 0)], writes=[("stg", sslot, 0)])
                            P.op("pool", lambda e, Sv=Sv, Rv=Rv, cosb=cosb, nr=nr: e.tensor_tensor(
                                out=Sv[0:nr, :, 1, :], in0=Rv[0:nr, :, 1, :], in1=cosb, op=ALU.mult),
                                reads=[("raw", rslot), ("cs", t)], writes=[("stg", sslot, 1)])
                            P.op("pool", lambda e, tg=tg, Rv=Rv, sinb=sinb, nr=nr: e.tensor_tensor(
                                out=tg[0:nr], in0=Rv[0:nr, :, 0, :], in1=sinb, op=ALU.mult),
                                reads=[("raw", rslot), ("cs", t)], writes=["tmpg"])
                            P.op("pool", lambda e, Sv=Sv, tg=tg, nr=nr: e.tensor_tensor(
                                out=Sv[0:nr, :, 1, :], in0=Sv[0:nr, :, 1, :], in1=tg[0:nr], op=ALU.add),
                                reads=["tmpg", ("stg", sslot, 1)], writes=[("stg", sslot, 1)])
                            sreads = [("stg", sslot, 0), ("stg", sslot, 1)]
                        else:
                            P.op("act", lambda e, S=S, acc=acc, nr=nr: e.activation(out=S[0:nr, :], in_=acc[0:nr, :], func=AF.Copy),
                                 reads=[akey], writes=[("stg", sslot, 0), ("stg", sslot, 1)])
                            sreads = [("stg", sslot, 0), ("stg", sslot, 1)]
                        if kind in ("ka", "va", "ks", "vs"):
                            dst = o_kv[{"ka": "mk", "va": "mv", "ks": "sk", "vs": "sv"}[kind]]
                            dap = dst[0][r0:r0 + nr, :] if t < NT else dst[1][:, :]
                            P.dma("sp", dap, S[0:nr, :], reads=sreads, key=("o", "stg", sslot))
                        if kind in ("va", "vs"):
                            if t < NT:
                                if kind == "va":
                                    P.op("pool", lambda e, S=S, t=t: e.tensor_copy(
                                        out=Va[:, t, :, 0:64], in_=S[:, :].rearrange("p (h d) -> p h d", d=64)),
                                        reads=sreads, writes=[("Va", t)])
                                else:
                                    P.op("pool", lambda e, S=S, t=t: e.tensor_copy(out=Vs[:, t, :], in_=S[:, :]),
                                         reads=sreads, writes=[("Vs", t)])
                            else:
                                Vn = Vna if kind == "va" else Vns
                                P.op("pool", lambda e, S=S, Vn=Vn: e.tensor_copy(out=Vn[:, :], in_=S[0:ST, :]),
                                     reads=sreads, writes=["Vn" + kind])
                        else:
                            scale = 0.125 if kind in ("qa", "qs") else 1.0
                            P.op("dve", lambda e, Q=Q, S=S, nr=nr, scale=scale: e.tensor_scalar(
                                out=Q[0:nr, :], in0=S[0:nr, :], scalar1=scale, scalar2=None, op0=ALU.mult),
                                reads=sreads, writes=[("qb", qslot)])
                            pT = pt16[it % 2]
                            for j in range(4):
                                P.op("pe", lambda e, pT=pT, Q=Q, j=j, nr=nr: e.transpose(
                                    out=pT[:, j * 128:j * 128 + nr], in_=Q[0:nr, j * 128:(j + 1) * 128],
                                    identity=ident[0:nr, 0:nr]),
                                    reads=[("qb", qslot), "consts"], writes=[("pt16", it % 2)])
                            if t < NT:
                                dstT = {"qa": QTa, "ka": KTa, "qs": QTs, "ks": KTs}[kind]
                                dsl = dstT[:, :, r0:r0 + nr]
                            else:
                                dstT = {"qa": QTa_s, "ka": KTa_s, "qs": QTs_s, "ks": KTs_s}[kind]
                                dsl = dstT[:, :, :]
                            P.op("act", lambda e, pT=pT, dsl=dsl, nr=nr: e.activation(
                                out=dsl, in_=pT[:, 0:512].rearrange("p (j t) -> p j t", t=128)[:, :, 0:nr], func=AF.Copy),
                                reads=[("pt16", it % 2)], writes=[(kind + "T", t)])
                P.emit()

            with ExitStack() as s2:
                P = Prog(ctx)
                otok_a = sb("otok_a", [128, NT, 512], BF16, s2)
                otok_s = sb("otok_s", [128, NT, 512], BF16, s2)
                kmf = sb("kmf", [128, 4, 8], F32, s2)
                kmT = sb("kmT", [128, 4, 8], BF16, s2)
                Gm = sb("Gm", [128, 8, 8], F32, s2)
                top8 = sb("top8", [128, 8, 8], F32, s2)
                selt = sb("selt", [128, 8, 8], F32, s2)
                Mtok = sb("Mtok", [128, 8, 64], BF16, s2)
                MTs = [sb("MTs%d" % i, [8, 512], BF16, s2) for i in range(2)]
                Pt = [sb("Pt%d" % i, [128, 512], BF16, s2) for i in range(4)]
                Ef = [sb("Ef%d" % i, [128, 512], F32, s2) for i in range(3)]
                SP = [sb("SP%d" % i, [128, 512], BF16, s2) for i in range(3)]
                Rr = [sb("Rr%d" % i, [128, 512], BF16, s2) for i in range(4)]
                rden = sb("rden", [128, 4], F32, s2)
                for nm, src, dst, a in (("w_bm", w_bm, w_bm_b, 1), ("w_bs", w_bs, w_bs_b, 1),
                                        ("w_out", w_out, w_out_b, 1), ("w_g", w_g, w_g_b, 2), ("w_u", w_u, w_u_b, 2),
                                        ("w_d", w_d, w_d_b, 1)):
                    P.dma("pool", dst.rearrange("k (a n) -> k a n", a=a), src.rearrange("k (a n) -> k a n", a=a),
                          writes=[("wscr", nm)], key=("wscr", nm))

                for hp in range(4):
                    P.op("dve", lambda e, hp=hp: e.reduce_sum(
                        out=kmf[:, hp, :], in_=KTa[:, hp, :].rearrange("p (n k) -> p n k", k=256), axis=AX.X),
                        writes=[("kmf", hp)])
                P.op("dve", lambda e: e.tensor_scalar(out=kmT[:], in0=kmf[:], scalar1=1.0 / 256, scalar2=None, op0=ALU.mult),
                     reads=[("kmf", hp) for hp in range(4)], writes=["kmT"])
                for c in range(8, NT):
                    cur = c // 2
                    G = pb[5]
                    for h in range(8):
                        hp, hb_ = h // 2, (h % 2) * 64
                        P.op("pe", lambda e, G=G, h=h, hp=hp, hb_=hb_, c=c: e.matmul(
                            G[:, h * 8:(h + 1) * 8], lhsT=QTa[hb_:hb_ + 64, hp, c * 128:(c + 1) * 128],
                            rhs=kmT[hb_:hb_ + 64, hp, :], start=True, stop=True),
                            reads=["kmT"], writes=[("pb", 5)])
                    P.op("dve", lambda e, G=G: e.tensor_copy(out=Gm[:], in_=G[:, 0:64].rearrange("p (h n) -> p h n", n=8)),
                         reads=[("pb", 5)], writes=["Gm"])
                    P.op("dve", lambda e, cur=cur: e.memset(Gm[:, :, cur:8], -1e30), reads=["Gm"], writes=["Gm"])
                    for h in range(8):
                        P.op("dve", lambda e, h=h: e.max(out=top8[:, h, :], in_=Gm[:, h, :]), reads=["Gm"], writes=[("top8", h)])
                    P.op("dve", lambda e: e.tensor_tensor(
                        out=selt[:], in0=Gm[:], in1=top8[:, :, 2:3].to_broadcast([128, 8, 8]), op=ALU.is_ge),
                        reads=["Gm"] + [("top8", h) for h in range(8)], writes=["selt"])
                    Mv = Mtok[:, c - 8, :].rearrange("p (h n) -> p h n", n=8)
                    P.op("dve", lambda e, Mv=Mv: e.tensor_scalar(
                        out=Mv, in0=selt[:], scalar1=1.0, scalar2=-NEG, op0=ALU.subtract, op1=ALU.mult),
                        reads=["selt"], writes=[("Mtok", c)])
                    P.op("dve", lambda e, Mv=Mv, cur=cur: e.memset(Mv[:, :, cur:cur + 1], 0.0),
                         reads=[("Mtok", c)], writes=[("Mtok", c)])

                sidx = [0]
                gidx = [0]

                def attn_head_group(h, qg, moba):
                    hp, hb_ = h // 2, (h % 2) * 64
                    KT, QT = (KTa, QTa) if moba else (KTs, QTs)
                    c_lo, c_hi = qg * 4, qg * 4 + 3
                    oi = 4 + gidx[0] % 2
                    mslot = rslot = gidx[0] % 2
                    gidx[0] += 1
                    O = pb[oi]
                    okey = ("pb", oi)
                    ncol = 4 * 65 if moba else 4 * 64
                    ow = 65 if moba else 64
                    P.i("pe", "matmul", ["zeros16"], [okey], O[:, 0:ncol], lhsT=zeros16[:, 0:128], rhs=zeros16[:, 0:ncol],
                        start=True, stop=False)
                    mts = None
                    if moba and qg >= 2:
                        mts = MTs[mslot]
                        pT = pt16[0]
                        for i in range(4):
                            c = c_lo + i
                            P.i("pe", "transpose", [("Mtok", c), "consts"], [("pt16", 0)], out=pT[0:8, i * 128:(i + 1) * 128],
                                in_=Mtok[:, c - 8, h * 8:(h + 1) * 8], identity=ident)
                        P.i("dve", "tensor_copy", [("pt16", 0)], [("MTs", mslot)], out=mts[:, :], in_=pT[0:8, 0:512])
                    Rpp = None
                    if not moba:
                        Rpp = (Rr[2 * rslot], Rr[2 * rslot + 1])
                        for q_ in range(2):
                            P.i("pool", "memset", [], [("Rr", 2 * rslot + q_)], Rpp[q_][:], 0.0)
                    rstep = [0]
                    kts = list(range(0, c_hi + 1)) if moba else list(range(c_hi, -1, -1))

                    def stageA(kt):
                        c0 = max(kt, c_lo)
                        N = (c_hi + 1 - c0) * 128
                        q0 = c0 * 128
                        diag = kt >= c_lo
                        n = sidx[0]
                        sidx[0] += 1
                        st = dict(kt=kt, c0=c0, N=N, coff=(c0 - c_lo) * 128, sb_i=n % 4, eslot=n % 3, pslot=n % 4)
                        S1 = pb[st["sb_i"]]
                        k1 = ("pb", st["sb_i"])
                        P.i("pe", "matmul", [], [k1], S1[:, 0:N], lhsT=KT[hb_:hb_ + 64, hp, kt * 128:(kt + 1) * 128],
                            rhs=QT[hb_:hb_ + 64, hp, q0:q0 + N], start=True, stop=False)
                        if diag:
                            P.i("pe", "matmul", ["consts"], [k1], S1[:, 0:128], lhsT=ident,
                                rhs=(mask_incl if moba else mask_strict), start=False, stop=False)
                        if moba:
                            n_blk = kt // 2
                            cm = max(c0, 2 * n_blk + 2)
                            if qg >= 2 and cm <= c_hi:
                                o1 = (cm - c0) * 128
                                m1 = (cm - c_lo) * 128
                                P.i("pe", "matmul", ["eb", ("MTs", mslot)], [k1], S1[:, o1:N], lhsT=eb[0:8, n_blk, :],
                                    rhs=mts[0:8, m1:512], start=False, stop=True)
                        return st

                    def stageB(st):
                        N = st["N"]
                        S1 = pb[st["sb_i"]]
                        k1 = ("pb", st["sb_i"])
                        if moba:
                            A = Pt[st["pslot"]]
                            P.i("act", "activation", [k1], [("Pt", st["pslot"])], out=A[:, 0:N], in_=S1[:, 0:N], func=AF.Exp)
                        else:
                            E, SPt = Ef[st["eslot"]], SP[st["eslot"]]
                            P.i("act", "activation", [k1], [("Ef", st["eslot"])], out=E[:, 0:N], in_=S1[:, 0:N], func=AF.Exp)
                            P.i("act", "activation", [("Ef", st["eslot"])], [("SP", st["eslot"])], out=SPt[:, 0:N],
                                in_=E[:, 0:N], func=AF.Ln, bias=1.0)

                    def stageC(st, first):
                        kt, c0, N, coff = st["kt"], st["c0"], st["N"], st["coff"]
                        S1 = pb[st["sb_i"]]
                        k1 = ("pb", st["sb_i"])
                        A = Pt[st["pslot"]]
                        if not moba:
                            SPt = SP[st["eslot"]]
                            P.i("pe", "matmul", ["consts", ("SP", st["eslot"])], [k1], S1[:, 0:N], lhsT=Uneg, rhs=SPt[:, 0:N],
                                start=False, stop=first)
                            ra = rstep[0] % 2
                            rstep[0] += 1
                            Rcur, Rnxt = Rpp[ra], Rpp[1 - ra]
                            kcur, knxt = ("Rr", 2 * rslot + ra), ("Rr", 2 * rslot + 1 - ra)
                            if kt > 0:
                                P.i("pool", "tensor_tensor", [("SP", st["eslot"]), kcur], [knxt],
                                    out=Rnxt[:, coff:coff + N], in0=Rcur[:, coff:coff + N], in1=SPt[:, 0:N], op=ALU.add)
                            if not first:
                                P.i("pe", "matmul", ["consts", kcur], [k1], S1[:, 0:N], lhsT=negones,
                                    rhs=Rcur[:, coff:coff + N], start=False, stop=True)
                            P.i("act", "activation", [k1], [("Pt", st["pslot"])], out=A[:, 0:N], in_=S1[:, 0:N], func=AF.Exp)
                        for c in range(c0, c_hi + 1):
                            j = c - c0
                            i = c - c_lo
                            rhs = Va[:, kt, h, :] if moba else Vs[:, kt, h * 64:(h + 1) * 64]
                            P.i("pe", "matmul", [("Pt", st["pslot"])], [okey], O[:, i * ow:(i + 1) * ow],
                                lhsT=A[:, j * 128:(j + 1) * 128], rhs=rhs, start=False, stop=False)

                    sts = []
                    for i_, kt in enumerate(kts):
                        sts.append(stageA(kt))
                        if i_ >= 1:
                            stageB(sts[i_ - 1])
                        if i_ >= 2:
                            stageC(sts[i_ - 2], i_ - 2 == 0)
                    nk = len(kts)
                    stageB(sts[nk - 1])
                    if nk >= 2:
                        stageC(sts[nk - 2], nk - 2 == 0)
                    stageC(sts[nk - 1], nk - 1 == 0)
                    if moba:
                        Ov = O[:, 0:ncol].rearrange("p (i w) -> p i w", w=65)
                        P.i("dve", "reciprocal", [okey], ["rden"], out=rden[:, :], in_=Ov[:, :, 64])
                        P.i("dve", "tensor_tensor", [okey, "rden"], [("otok_a", qg, h)],
                            out=otok_a[:, c_lo:c_hi + 1, h * 64:(h + 1) * 64], in0=Ov[:, :, 0:64],
                            in1=rden[:, :].unsqueeze(2).to_broadcast([128, 4, 64]), op=ALU.mult)
                    else:
                        Ov = O[:, 0:ncol].rearrange("p (i w) -> p i w", w=64)
                        P.i("dve", "tensor_copy", [okey], [("otok_s", qg, h)],
                            out=otok_s[:, c_lo:c_hi + 1, h * 64:(h + 1) * 64], in_=Ov)

                for h in range(8):
                    for qg in range(4):
                        attn_head_group(h, qg, True)
                for h in range(8):
                    for qg in range(4):
                        attn_head_group(h, qg, False)
                for qg in range(4):
                    for nm, ot, scr in (("otok_a", otok_a, oa_scr), ("otok_s", otok_s, os_scr)):
                        P.dma("sp", scr[qg * 512:(qg + 1) * 512, :].rearrange("(i p) f -> p i f", p=128),
                              ot[:, qg * 4:(qg + 1) * 4, :], reads=[(nm, qg, h) for h in range(8)], key=("o", nm))
                P.emit()


        with ExitStack() as s2:
            P = Prog(ctx)
            pt_sb = sb("pt_sb", [128, 4, NPAGES], I32, s2)
            ptf = sb("ptf", [128, 4, NPAGES], F32, s2)
            ptsel = sb("ptsel", [128, 64, 4], F32, s2)
            pgrp = sb("pgrp", [128, 64], F32, s2)
            idxf = sb("idxf", [128, 64], F32, s2)
            idx = sb("idx", [128, 64], I32, s2)
            oh4 = sb("oh4_sb", [128, 4], F32, s2)
            pmod = sb("pmod_sb", [128, 1], F32, s2)
            cm8 = sb("cm8_sb", [8, 2, 64], BF16, s2)
            zsel = sb("zsel_sb", [128, 16, 32], BF16, s2)
            egm = sb("eg_sb", [32, 16, 128], BF16, s2)
            Vnq = sb("Vnq", [8, 4, 2, 512], BF16, s2)
            Qbd = sb("Qbd", [128, 2, 4, 16], BF16, s2)
            Ka_seq = sb("Ka_seq", [128, NPAGES, 512], BF16, s2)
            NSL = 3
            Ksb = [sb("Ksb%d" % i, [128, 4, 512], BF16, s2) for i in range(NSL)]
            Vab = [sb("Vab%d" % i, [128, 4, 512], BF16, s2) for i in range(NSL)]
            Vsb = [sb("Vsb%d" % i, [128, 4, 512], BF16, s2) for i in range(NSL)]
            KTg = [[sb("KTg%d_%d" % (a, i), [128, 4, 4, 128], BF16, s2) for i in range(NSL)] for a in range(2)]
            KMb = sb("KMb", [32, 512], BF16, s2)
            KMT = sb("KMT", [128, 4, 32], BF16, s2)
            Gs = sb("Gs", [16, 4, 32], F32, s2)
            top8s = sb("top8s", [16, 4, 8], F32, s2)
            sels = sb("sels", [16, 4, 32], F32, s2)
            Msel = sb("Msel", [16, 4, 32], BF16, s2)
            MTs2 = sb("MTs2", [32, 64], BF16, s2)
            Pn = sb("Pn", [8, 64], BF16, s2)
            En = sb("En", [8, 64], F32, s2)
            SPn = sb("SPn", [8, 64], BF16, s2)
            An = sb("An", [8, 64], BF16, s2)
            Pm = [sb("Pm%d" % i, [128, 4, 64], BF16, s2) for i in range(NSL)]
            Eg = [sb("Eg%d" % i, [128, 256], F32, s2) for i in range(NSL)]
            SPg = [sb("SPg%d" % i, [128, 4, 64], BF16, s2) for i in range(NSL)]
            Ag = [sb("Ag%d" % i, [128, 4, 64], BF16, s2) for i in range(NSL)]
            Wc = [sb("Wc%d" % i, [128, 4, 64], BF16, s2) for i in range(NSL)]
            carry = sb("carry", [128, 64], BF16, s2)
            rdn = sb("rdn", [64, 1], F32, s2)
            oa_sb = sb("oa_sb", [64, 512], BF16, s2)
            os_sb = sb("os_sb", [64, 512], BF16, s2)

            P.dma("sp", pt_sb[:], ptab.partition_broadcast(128), writes=["pt_sb"])
            P.dma("sp", cm8[:], cm8_d, writes=["cm8"])
            P.dma("sp", oh4[:], oh4_d, writes=["oh4"])
            P.dma("sp", pmod[:], pmod_d, writes=["pmod"])
            P.dma("sp", zsel[:], zsel_d, writes=["zsel"])
            P.dma("sp", egm[:], eg_d, writes=["egm"])
            P.i("dve", "tensor_copy", ["pt_sb"], ["ptf"], out=ptf[:], in_=pt_sb[:])
            P.i("dve", "tensor_tensor", ["ptf", "oh4"], ["ptsel"], out=ptsel[:],
                in0=ptf[:, :, :].rearrange("p s (g l) -> p (s g) l", l=4),
                in1=oh4[:, :].unsqueeze(1).to_broadcast([128, 64, 4]), op=ALU.mult)
            P.i("dve", "reduce_sum", ["ptsel"], ["pgrp"], out=pgrp[:], in_=ptsel[:], axis=AX.X)
            P.i("dve", "tensor_scalar", ["pgrp", "pmod"], ["idxf"], out=idxf[:], in0=pgrp[:], scalar1=32.0,
                scalar2=pmod[:, 0:1], op0=ALU.mult, op1=ALU.add)
            P.i("dve", "tensor_copy", ["idxf"], ["idx"], out=idx[:], in_=idxf[:])
            for s in range(4):
                P.dma("sp", Vnq[0:8, s, 0, :], Vna[s * 8:(s + 1) * 8, :], writes=[("Vnq", s)])
                P.dma("sp", Vnq[0:8, s, 1, :], Vns[s * 8:(s + 1) * 8, :], writes=[("Vnq", s)])

            def gather(dst, cache, s, g, reads, writes, key):
                col = s * 16 + g
                P.op("pool", lambda e: e.indirect_dma_start(
                    out=dst.rearrange("p t f -> p (t f)"), out_offset=None,
                    in_=cache.rearrange("(r t) f -> r (t f)", t=4),
                    in_offset=bass.IndirectOffsetOnAxis(ap=idx[:, col:col + 1], axis=0)),
                    reads=reads, writes=writes, dma=True, key=key)

            gcount = [0]
            for s in range(4):
                sc = slice(s * 8, (s + 1) * 8)
                P.i("dve", "memset", [], ["Qbd"], Qbd[:], 0.0)
                for a, QTx in ((0, QTa_s), (1, QTs_s)):
                    P.i("dve", "tensor_copy", ["Qbd"], ["Qbd"], out=Qbd[0:64, a, :, 0:8], in_=QTx[0:64, :, sc])
                    P.i("dve", "tensor_copy", ["Qbd"], ["Qbd"], out=Qbd[64:128, a, :, 8:16], in_=QTx[64:128, :, sc])
                for g in range(16):
                    gather(Ka_seq[:, g * 4:(g + 1) * 4, :], cmk, s, g, ["idx"], [("Ka", g // 2)], ("Ka", g // 2))
                KM = pb[2]
                for j in range(NPAGES):
                    P.i("pe", "matmul", [("Ka", j // 8), "zsel"], [("pb", 2)], KM[0:32, :],
                        lhsT=zsel[:, j // 4, :], rhs=Ka_seq[:, j, :], start=(j == 0), stop=(j == NPAGES - 1))
                P.i("act", "activation", [("pb", 2)], ["KMb"], out=KMb[:, :], in_=KM[0:32, :], func=AF.Copy)
                pT = pt16[0]
                for hp in range(4):
                    P.i("pe", "transpose", ["KMb"], [("pt16", 0)], out=pT[:, hp * 32:(hp + 1) * 32],
                        in_=KMb[0:32, hp * 128:(hp + 1) * 128], identity=ident[0:32, 0:32])
                P.i("dve", "tensor_copy", [("pt16", 0)], ["KMT"], out=KMT[:, :, :],
                    in_=pT[:, 0:128].rearrange("p (h n) -> p h n", n=32))
                Gp = pb[5]
                for hp in range(4):
                    P.i("pe", "matmul", ["KMT", "Qbd"], [("pb", 5)], Gp[0:16, hp * 32:(hp + 1) * 32],
                        lhsT=Qbd[:, 0, hp, :], rhs=KMT[:, hp, :], start=True, stop=True)
                P.i("dve", "tensor_copy", [("pb", 5)], ["Gs"], out=Gs[:, :, :],
                    in_=Gp[0:16, 0:128].rearrange("p (h n) -> p h n", n=32))
                for hp in range(4):
                    P.i("dve", "max", ["Gs"], [("top8s", hp)], out=top8s[:, hp, :], in_=Gs[:, hp, :])
                P.i("dve", "tensor_tensor", ["Gs"] + [("top8s", hp) for hp in range(4)], ["sels"], out=sels[:],
                    in0=Gs[:], in1=top8s[:, :, 2:3].to_broadcast([16, 4, 32]), op=ALU.is_ge)
                P.i("dve", "tensor_scalar", ["sels"], ["Msel"], out=Msel[:], in0=sels[:], scalar1=1.0, scalar2=-NEG,
                    op0=ALU.subtract, op1=ALU.mult)
                pT = pt16[1]
                for hp in range(4):
                    P.i("pe", "transpose", ["Msel"], [("pt16", 1)], out=pT[0:32, hp * 16:(hp + 1) * 16],
                        in_=Msel[0:16, hp, :], identity=ident[0:16, 0:16])
                P.i("dve", "tensor_copy", [("pt16", 1)], ["MTs2"], out=MTs2[:, :], in_=pT[0:32, 0:64])
                Oa, Os, Dn = pb[3], pb[4], pb[5]
                P.i("pe", "matmul", ["zeros16"], [("pb", 3)], Oa[0:64, :], lhsT=zeros16[:, 0:64], rhs=zeros16[:, 0:512],
                    start=True, stop=False)
                P.i("pe", "matmul", ["zeros16"], [("pb", 4)], Os[0:64, :], lhsT=zeros16[:, 0:64], rhs=zeros16[:, 0:512],
                    start=True, stop=False)
                P.i("pe", "matmul", ["zeros16", "Gs"], [("pb", 5)], Dn[0:64, 0:8], lhsT=zeros16[:, 0:64], rhs=zeros16[:, 0:8],
                    start=True, stop=False)
                Sm, S1, S2 = pb[2], pb[0], pb[1]
                P.i("pe", "matmul", ["cm8"], [("pb", 2)], Sm[0:8, 0:64], lhsT=ident[0:8, 0:8], rhs=cm8[0:8, 0, :],
                    start=True, stop=False)
                for hp in range(4):
                    P.i("pe", "matmul", ["Qbd"], [("pb", 2)], Sm[0:8, hp * 16:(hp + 1) * 16],
                        lhsT=KTa_s[:, hp, sc], rhs=Qbd[:, 0, hp, :], start=False, stop=(hp == 3))
                P.i("act", "activation", [("pb", 2)], ["Pn"], out=Pn[:, :], in_=Sm[0:8, 0:64], func=AF.Exp)
                P.i("pe", "matmul", ["Pn", ("Vnq", s)], [("pb", 3)], Oa[0:64, :], lhsT=Pn[0:8, :], rhs=Vnq[0:8, s, 0, :],
                    start=False, stop=False)
                P.i("pe", "matmul", ["Pn"], [("pb", 5)], Dn[0:64, 0:1], lhsT=Pn[0:8, :], rhs=ones[0:8, 0:1],
                    start=False, stop=False)
                P.i("pe", "matmul", ["cm8"], [("pb", 0)], S1[0:8, 0:64], lhsT=ident[0:8, 0:8], rhs=cm8[0:8, 1, :],
                    start=True, stop=False)
                for hp in range(4):
                    P.i("pe", "matmul", ["Qbd"], [("pb", 0)], S1[0:8, hp * 16:(hp + 1) * 16],
                        lhsT=KTs_s[:, hp, sc], rhs=Qbd[:, 1, hp, :], start=False, stop=(hp == 3))
                P.i("act", "activation", [("pb", 0)], ["En"], out=En[:, :], in_=S1[0:8, 0:64], func=AF.Exp)
                P.i("act", "activation", ["En"], ["SPn"], out=SPn[:, :], in_=En[:, :], func=AF.Ln, bias=1.0)
                P.i("pe", "matmul", ["SPn"], [("pb", 1)], S2[0:8, 0:64], lhsT=Uneg[0:8, 0:8], rhs=SPn[0:8, :],
                    start=True, stop=False)
                P.i("pe", "matmul", ["cm8"], [("pb", 1)], S2[0:8, 0:64], lhsT=ident[0:8, 0:8], rhs=cm8[0:8, 1, :],
                    start=False, stop=False)
                for hp in range(4):
                    P.i("pe", "matmul", ["Qbd"], [("pb", 1)], S2[0:8, hp * 16:(hp + 1) * 16],
                        lhsT=KTs_s[:, hp, sc], rhs=Qbd[:, 1, hp, :], start=False, stop=(hp == 3))
                P.i("act", "activation", [("pb", 1)], ["An"], out=An[:, :], in_=S2[0:8, 0:64], func=AF.Exp)
                P.i("pe", "matmul", ["An", ("Vnq", s)], [("pb", 4)], Os[0:64, :], lhsT=An[0:8, :], rhs=Vnq[0:8, s, 1, :],
                    start=False, stop=False)
                for g in range(15, -1, -1):
                    slot = gcount[0] % NSL
                    gcount[0] += 1
                    NC_ = 256
                    gather(Ksb[slot][:, :, :], csk, s, g, ["idx"], [("Ksb", slot)], ("Ksb", slot))
                    gather(Vab[slot][:, :, :], cmv, s, g, ["idx"], [("Vab", slot)], ("Vab", slot))
                    gather(Vsb[slot][:, :, :], csv, s, g, ["idx"], [("Vsb", slot)], ("Vsb", slot))
                    ev = 0
                    for a in range(2):
                        for pp in range(0, 4, 2):
                            bank = (a * 2 + pp // 2) % 2
                            pT = pt16[bank]
                            for q in range(2):
                                pl = pp + q
                                src = Ka_seq[:, g * 4 + pl, :] if a == 0 else Ksb[slot][:, pl, :]
                                rk = ("Ka", g // 2) if a == 0 else ("Ksb", slot)
                                for hp in range(4):
                                    P.i("pe", "transpose", [rk], [("pt16", bank)],
                                        out=pT[:, q * 512 + hp * 128:q * 512 + (hp + 1) * 128],
                                        in_=src[:, hp * 128:(hp + 1) * 128], identity=ident)
                            dst = KTg[a][slot][:, pp:pp + 2, :, :].rearrange("p a h t -> p (a h t)")
                            if ev % 2 == 0:
                                P.i("act", "activation", [("pt16", bank)], [("KTg", a, slot, pp)], out=dst, in_=pT[:, :], func=AF.Copy)
                            else:
                                P.i("dve", "tensor_copy", [("pt16", bank)], [("KTg", a, slot, pp)], out=dst, in_=pT[:, :])
                            ev += 1
                    P.i("pe", "matmul", ["MTs2", "egm"], [("pb", 2)], Sm[:, 0:NC_], lhsT=egm[:, g, :],
                        rhs=MTs2[:, :].unsqueeze(1).to_broadcast([32, 4, 64]), start=True, stop=False)
                    for pl in range(4):
                        for hp in range(4):
                            P.i("pe", "matmul", [("KTg", 0, slot, (pl // 2) * 2), "Qbd"], [("pb", 2)],
                                Sm[:, pl * 64 + hp * 16:pl * 64 + (hp + 1) * 16],
                                lhsT=KTg[0][slot][:, pl, hp, :], rhs=Qbd[:, 0, hp, :], start=False, stop=False)
                    PM = Pm[slot]
                    P.i("act", "activation", [("pb", 2)], [("Pm", slot)], out=PM[:, :, :].rearrange("p a c -> p (a c)"),
                        in_=Sm[:, 0:NC_], func=AF.Exp)
                    for pl in range(4):
                        P.i("pe", "matmul", [("Pm", slot), ("Vab", slot)], [("pb", 3)], Oa[0:64, :], lhsT=PM[:, pl, :],
                            rhs=Vab[slot][:, pl, :], start=False, stop=False)
                        P.i("pe", "matmul", [("Pm", slot)], [("pb", 5)], Dn[0:64, 0:1], lhsT=PM[:, pl, :],
                            rhs=ones[:, 0:1], start=False, stop=False)
                    for pl in range(4):
                        for hp in range(4):
                            P.i("pe", "matmul", [("KTg", 1, slot, (pl // 2) * 2), "Qbd"], [("pb", 0)],
                                S1[:, pl * 64 + hp * 16:pl * 64 + (hp + 1) * 16],
                                lhsT=KTg[1][slot][:, pl, hp, :], rhs=Qbd[:, 1, hp, :], start=True, stop=True)
                    EG, SPG, AG, WC = Eg[slot], SPg[slot], Ag[slot], Wc[slot]
                    P.i("act", "activation", [("pb", 0)], [("Eg", slot)], out=EG[:, :], in_=S1[:, 0:NC_], func=AF.Exp)
                    P.i("act", "activation", [("Eg", slot)], [("SPg", slot)], out=SPG[:, :, :].rearrange("p a c -> p (a c)"),
                        in_=EG[:, :], func=AF.Ln, bias=1.0)
                    P.i("dve", "tensor_copy", [("SPg", slot)], [("Wc", slot)], out=WC[:, 3, :], in_=SPG[:, 3, :])
                    for pl in range(2, -1, -1):
                        P.i("dve", "tensor_tensor", [("Wc", slot), ("SPg", slot)], [("Wc", slot)], out=WC[:, pl, :],
                            in0=WC[:, pl + 1, :], in1=SPG[:, pl, :], op=ALU.add)
                    P.i("pe", "matmul", [("Wc", slot)], [("pb", 1)], S2[:, 0:NC_], lhsT=negI,
                        rhs=WC[:, :, :].rearrange("p a c -> p (a c)"), start=True, stop=False)
                    P.i("pe", "matmul", [("Wc", slot)], [("pb", 1)], S2[:, 0:NC_], lhsT=Usneg,
                        rhs=WC[:, 0, :].unsqueeze(1).to_broadcast([128, 4, 64]), start=False, stop=False)
                    if g < 15:
                        P.i("pe", "matmul", ["carry"], [("pb", 1)], S2[:, 0:NC_], lhsT=negones,
                            rhs=carry[:, :].unsqueeze(1).to_broadcast([128, 4, 64]), start=False, stop=False)
                    if g > 0:
                        if g == 15:
                            P.i("dve", "tensor_copy", [("Wc", slot)], ["carry"], out=carry[:, :], in_=WC[:, 0, :])
                        else:
                            P.i("dve", "tensor_tensor", [("Wc", slot), "carry"], ["carry"], out=carry[:, :],
                                in0=carry[:, :], in1=WC[:, 0, :], op=ALU.add)
                    P.i("pe", "matmul", ["SPn"], [("pb", 1)], S2[:, 0:NC_], lhsT=negones[0:8, :],
                        rhs=SPn[0:8, :].unsqueeze(1).to_broadcast([8, 4, 64]), start=False, stop=False)
                    for pl in range(4):
                        for hp in range(4):
                            P.i("pe", "matmul", [("KTg", 1, slot, (pl // 2) * 2), "Qbd"], [("pb", 1)],
                                S2[:, pl * 64 + hp * 16:pl * 64 + (hp + 1) * 16],
                                lhsT=KTg[1][slot][:, pl, hp, :], rhs=Qbd[:, 1, hp, :], start=False, stop=False)
                    P.i("act", "activation", [("pb", 1)], [("Ag", slot)], out=AG[:, :, :].rearrange("p a c -> p (a c)"),
                        in_=S2[:, 0:NC_], func=AF.Exp)
                    for pl in range(4):
                        P.i("pe", "matmul", [("Ag", slot), ("Vsb", slot)], [("pb", 4)], Os[0:64, :], lhsT=AG[:, pl, :],
                            rhs=Vsb[slot][:, pl, :], start=False, stop=False)
                P.i("dve", "reciprocal", [("pb", 5)], ["rdn"], out=rdn[:, :], in_=Dn[0:64, 0:1])
                P.i("dve", "tensor_scalar", [("pb", 3), "rdn"], ["oa_sb"], out=oa_sb[:, :], in0=Oa[0:64, :],
                    scalar1=rdn[:, 0:1], scalar2=None, op0=ALU.mult)
                P.i("act", "activation", [("pb", 4)], ["os_sb"], out=os_sb[:, :], in_=Os[0:64, :], func=AF.Copy)
                r0 = SEQ + s * 8
                for h in range(8):
                    P.dma("sp", oa_scr[r0:r0 + 8, h * 64:(h + 1) * 64], oa_sb[h * 8:(h + 1) * 8, h * 64:(h + 1) * 64],
                          reads=["oa_sb"], key=("o", "osc_a"))
                    P.dma("sp", os_scr[r0:r0 + 8, h * 64:(h + 1) * 64], os_sb[h * 8:(h + 1) * 8, h * 64:(h + 1) * 64],
                          reads=["os_sb"], key=("o", "osc_s"))
            P.emit()

        with ExitStack() as s2:
            P = Prog(ctx)
            Wout = sb("Wout", [128, 8, D], BF16, s2)
            Wd = sb("Wd", [128, NFC, D], BF16, s2)
            w_out_v = w_out_b.rearrange("(kc p) n -> p kc n", p=128)
            w_d_v = w_d_b.rearrange("(fc p) n -> p fc n", p=128)
            for kc in range(0, 8, 2):
                P.dma("sp", Wout[:, kc:kc + 2, :], w_out_v[:, kc:kc + 2, :], writes=[("Wout", kc)])
            for fc in range(0, NFC, 2):
                P.dma("sp", Wd[:, fc:fc + 2, :], w_d_v[:, fc:fc + 2, :], writes=[("Wd", fc)])
            w_bm_v = w_bm_b.rearrange("(kc p) n -> p kc n", p=128)
            w_bs_v = w_bs_b.rearrange("(kc p) n -> p kc n", p=128)
            w_in_v = w_in_b.rearrange("(kc p) n -> p kc n", p=128)
            w_g_v = w_g_b.rearrange("(kc p) n -> p kc n", p=128)
            w_u_v = w_u_b.rearrange("(kc p) n -> p kc n", p=128)
            Wbr = [sb("Wbr%d" % i, [128, 2, 4, 128], BF16, s2) for i in range(2)]
            Wgt = [sb("Wgt%d" % i, [128, 2, 8, 128], BF16, s2) for i in range(2)]
            Wgu = [sb("Wgu%d" % i, [128, 2, 8, 128], BF16, s2) for i in range(4)]
            hTg = sb("hTg", [128, 8, 512], BF16, s2)
            otg = sb("otg", [128, 2, 4, 512], BF16, s2)
            oT = sb("oT", [128, 2, 4, 512], BF16, s2)
            mergedT = sb("mergedT", [128, 8, 512], BF16, s2)
            x1 = sb("x1", [128, 4, D], F32, s2)
            xin = sb("xin", [128, D], F32, s2)
            h2 = [sb("h2_%d" % i, [128, D], BF16, s2) for i in range(2)]
            h2T = sb("h2T", [128, 8, 512], BF16, s2)
            ffT = sb("ffT", [128, NFC, 512], BF16, s2)
            ea = sb("ea", [128, 512], F32, s2)
            ebb = sb("ebb", [128, 512], F32, s2)
            ma = sb("ma", [128, 512], F32, s2)
            ssd = sb("ssd", [128, 8], F32, s2)
            sqd = sb("sqd", [128, D], BF16, s2)

            NG = 5
            for g in range(NG):
                if g < 4:
                    t0, ntile, ntok, r0 = g * 4, 4, 512, g * 512
                    tiles = [(i, 128) for i in range(4)]
                else:
                    t0, ntile, ntok, r0 = NT, 1, ST, SEQ
                    tiles = [(0, ST)]
                gk = ("g", g)
                P.dma("sp", hTg[:, :, 0:ntok], hT_scr[:, :, r0:r0 + ntok], writes=["hTg"])
                for a, scr in ((0, oa_scr), (1, os_scr)):
                    for (i, nr) in tiles:
                        P.dma("sp", otg[0:nr, a, i, :], scr[r0 + i * 128:r0 + i * 128 + nr, :], writes=[("otg", a, i)])
                for a in range(2):
                    for (i, nr) in tiles:
                        pT = pt16[(a * 4 + i) % 2]
                        pk = ("pt16", (a * 4 + i) % 2)
                        for kc in range(4):
                            P.op("pe", lambda e, pT=pT, a=a, i=i, kc=kc, nr=nr: e.transpose(
                                out=pT[:, kc * 128:kc * 128 + nr], in_=otg[0:nr, a, i, kc * 128:(kc + 1) * 128],
                                identity=ident[0:nr, 0:nr]),
                                reads=[("otg", a, i), "consts"], writes=[pk])
                        P.op("act", lambda e, pT=pT, a=a, i=i, nr=nr: e.activation(
                            out=oT[:, a, :, i * 128:i * 128 + nr],
                            in_=pT[:, 0:512].rearrange("p (k t) -> p k t", t=128)[:, :, 0:nr], func=AF.Copy),
                            reads=[pk], writes=[("oT", a, i)])
                oT_reads = [("oT", a, i) for a in range(2) for (i, _) in tiles]
                for dc in range(8):
                    ws = (g * 8 + dc) % 2
                    WB, WG = Wbr[ws], Wgt[ws]
                    P.dma("sp", WB[:, 0, :, :], w_bm_v[:, :, dc * 128:(dc + 1) * 128], writes=[("Wbr", ws, 0)])
                    P.dma("sp", WB[:, 1, :, :], w_bs_v[:, :, dc * 128:(dc + 1) * 128], writes=[("Wbr", ws, 1)])
                    P.dma("sp", WG[:, 0, :, :], w_in_v[:, :, 3072 + dc * 128:3072 + (dc + 1) * 128], writes=[("Wgt", ws, 0)])
                    P.dma("sp", WG[:, 1, :, :], w_in_v[:, :, 4096 + dc * 128:4096 + (dc + 1) * 128], writes=[("Wgt", ws, 1)])
                    Ba, Bs, Ga, Gs = pb[0], pb[1], pb[2], pb[3]
                    for a, Bx in ((0, Ba), (1, Bs)):
                        for kc in range(4):
                            P.op("pe", lambda e, Bx=Bx, WB=WB, a=a, kc=kc: e.matmul(
                                Bx[:, 0:ntok], lhsT=WB[:, a, kc, :], rhs=oT[:, a, kc, 0:ntok], start=(kc == 0), stop=(kc == 3)),
                                reads=[("Wbr", ws, a)] + oT_reads, writes=[("pb", a)])
                    for a, Gx in ((0, Ga), (1, Gs)):
                        for kc in range(8):
                            P.op("pe", lambda e, Gx=Gx, WG=WG, a=a, kc=kc: e.matmul(
                                Gx[:, 0:ntok], lhsT=WG[:, a, kc, :], rhs=hTg[:, kc, 0:ntok], start=(kc == 0), stop=(kc == 7)),
                                reads=[("Wgt", ws, a), "hTg"], writes=[("pb", 2 + a)])
                    P.op("act", lambda e, Ga=Ga: e.activation(out=ea[:, 0:ntok], in_=Ga[:, 0:ntok], func=AF.Exp, scale=-1.0),
                         reads=[("pb", 2)], writes=["ea"])
                    P.op("act", lambda e, Gs=Gs: e.activation(out=ebb[:, 0:ntok], in_=Gs[:, 0:ntok], func=AF.Exp, scale=-1.0),
                         reads=[("pb", 3)], writes=["ebb"])
                    for nm_, t_ in (("ea", ea), ("ebb", ebb)):
                        P.i("act", "activation", [nm_], [nm_], out=t_[:, 0:ntok], in_=t_[:, 0:ntok], func=AF.Ln, bias=1.0)
                        P.i("act", "activation", [nm_], [nm_], out=t_[:, 0:ntok], in_=t_[:, 0:ntok], func=AF.Exp, scale=-1.0)
                    P.op("dve", lambda e, Ba=Ba: e.tensor_tensor(out=ma[:, 0:ntok], in0=Ba[:, 0:ntok], in1=ea[:, 0:ntok], op=ALU.mult),
                         reads=[("pb", 0), "ea"], writes=["ma"])
                    P.op("dve", lambda e, Bs=Bs: e.tensor_tensor(out=ebb[:, 0:ntok], in0=Bs[:, 0:ntok], in1=ebb[:, 0:ntok], op=ALU.mult),
                         reads=[("pb", 1), "ebb"], writes=["ebb"])
                    P.op("dve", lambda e, dc=dc: e.tensor_tensor(out=mergedT[:, dc, 0:ntok], in0=ma[:, 0:ntok], in1=ebb[:, 0:ntok], op=ALU.add),
                         reads=["ma", "ebb"], writes=[("mergedT", dc)])
                mreads = [("mergedT", dc) for dc in range(8)]
                for (i, nr) in tiles:
                    P.dma("sp", xin[0:nr, :], (xp[r0 + i * 128:r0 + i * 128 + nr, :] if g < 4 else xs[:, :]), writes=["xin"])
                    for half in range(2):
                        acc = pb[4 + half]
                        for dc in range(8):
                            P.op("pe", lambda e, acc=acc, dc=dc, i=i, nr=nr, half=half: e.matmul(
                                acc[0:nr, :], lhsT=mergedT[:, dc, i * 128:i * 128 + nr], rhs=Wout[:, dc, half * 512:(half + 1) * 512],
                                start=(dc == 0), stop=(dc == 7)),
                                reads=mreads + [("Wout", (dc // 2) * 2)], writes=[("pb", 4 + half)])
                        P.op("dve", lambda e, acc=acc, i=i, nr=nr, half=half: e.tensor_tensor(
                            out=x1[0:nr, i, half * 512:(half + 1) * 512], in0=acc[0:nr, :], in1=xin[0:nr, half * 512:(half + 1) * 512], op=ALU.add),
                            reads=[("pb", 4 + half), "xin"], writes=[("x1", i, half)])
                    x1r = [("x1", i, 0), ("x1", i, 1)]
                    P.op("act", lambda e, i=i, nr=nr: e.activation(out=sqd[0:nr, :], in_=x1[0:nr, i, :], func=AF.Square, accum_out=ssd[0:nr, 0:1]),
                         reads=x1r, writes=["sqd", "ssd"])
                    P.op("act", lambda e, nr=nr: e.activation(out=ssd[0:nr, 1:2], in_=ssd[0:nr, 0:1], func=AF.Ln, scale=1.0 / D, bias=EPS),
                         reads=["ssd"], writes=["ssd"])
                    P.op("act", lambda e, nr=nr: e.activation(out=ssd[0:nr, 2:3], in_=ssd[0:nr, 1:2], func=AF.Exp, scale=-0.5),
                         reads=["ssd"], writes=["ssd"])
                    hs = i % 2
                    H2 = h2[hs]
                    P.op("dve", lambda e, H2=H2, i=i, nr=nr: e.tensor_scalar(
                        out=H2[0:nr, :], in0=x1[0:nr, i, :], scalar1=ssd[0:nr, 2:3], scalar2=None, op0=ALU.mult),
                        reads=x1r + ["ssd"], writes=[("h2", hs)])
                    pT = pt16[hs]
                    for kc in range(8):
                        P.op("pe", lambda e, pT=pT, H2=H2, kc=kc, nr=nr: e.transpose(
                            out=pT[:, kc * 128:kc * 128 + nr], in_=H2[0:nr, kc * 128:(kc + 1) * 128], identity=ident[0:nr, 0:nr]),
                            reads=[("h2", hs), "consts"], writes=[("pt16", hs)])
                    P.op("dve", lambda e, pT=pT, i=i, nr=nr: e.tensor_tensor(
                        out=h2T[:, :, i * 128:i * 128 + nr],
                        in0=pT[:, :].rearrange("p (k t) -> p k t", t=128)[:, :, 0:nr],
                        in1=gffn_sb[:, :].unsqueeze(2).to_broadcast([128, 8, nr]), op=ALU.mult),
                        reads=[("pt16", hs), "gffn"], writes=[("h2T", i)])
                h2r = [("h2T", i) for (i, _) in tiles]
                for fc in range(NFC):
                    ws = (g * NFC + fc) % 4
                    WGU = Wgu[ws]
                    P.dma("sp", WGU[:, 0, :, :], w_g_v[:, :, fc * 128:(fc + 1) * 128], writes=[("Wgu", ws, 0)])
                    P.dma("sp", WGU[:, 1, :, :], w_u_v[:, :, fc * 128:(fc + 1) * 128], writes=[("Wgu", ws, 1)])
                    pg, pu = pb[(fc % 2) * 2], pb[(fc % 2) * 2 + 1]
                    kg, ku = ("pb", (fc % 2) * 2), ("pb", (fc % 2) * 2 + 1)
                    for a, px, kx in ((0, pg, kg), (1, pu, ku)):
                        for kc in range(8):
                            P.op("pe", lambda e, px=px, WGU=WGU, a=a, kc=kc: e.matmul(
                                px[:, 0:ntok], lhsT=WGU[:, a, kc, :], rhs=h2T[:, kc, 0:ntok], start=(kc == 0), stop=(kc == 7)),
                                reads=[("Wgu", ws, a)] + h2r, writes=[kx])
                    P.op("act", lambda e, pg=pg: e.activation(out=ea[:, 0:ntok], in_=pg[:, 0:ntok], func=AF.Exp, scale=-1.0),
                         reads=[kg], writes=["ea"])
                    P.i("act", "activation", ["ea"], ["ea"], out=ea[:, 0:ntok], in_=ea[:, 0:ntok], func=AF.Ln, bias=1.0)
                    P.i("act", "activation", ["ea"], ["ea"], out=ea[:, 0:ntok], in_=ea[:, 0:ntok], func=AF.Exp, scale=-1.0)
                    P.op("dve", lambda e, pg=pg: e.tensor_tensor(out=ma[:, 0:ntok], in0=pg[:, 0:ntok], in1=ea[:, 0:ntok], op=ALU.mult),
                         reads=[kg, "ea"], writes=["ma"])
                    P.op("dve", lambda e, pu=pu, fc=fc: e.tensor_tensor(out=ffT[:, fc, 0:ntok], in0=pu[:, 0:ntok], in1=ma[:, 0:ntok], op=ALU.mult),
                         reads=[ku, "ma"], writes=[("ffT", fc)])
                ffr = [("ffT", fc) for fc in range(NFC)]
                for (i, nr) in tiles:
                    for half in range(2):
                        acc = pb[4 + half]
                        for fc in range(NFC):
                            P.op("pe", lambda e, acc=acc, fc=fc, i=i, nr=nr, half=half: e.matmul(
                                acc[0:nr, :], lhsT=ffT[:, fc, i * 128:i * 128 + nr], rhs=Wd[:, fc, half * 512:(half + 1) * 512],
                                start=(fc == 0), stop=(fc == NFC - 1)),
                                reads=ffr + [("Wd", (fc // 2) * 2)], writes=[("pb", 4 + half)])
                        P.op("dve", lambda e, acc=acc, i=i, nr=nr, half=half: e.tensor_tensor(
                            out=x1[0:nr, i, half * 512:(half + 1) * 512], in0=acc[0:nr, :], in1=x1[0:nr, i, half * 512:(half + 1) * 512], op=ALU.add),
                            reads=[("pb", 4 + half), ("x1", i, half)], writes=[("x1", i, half)])
                    x1r = [("x1", i, 0), ("x1", i, 1)]
                    P.op("act", lambda e, i=i, nr=nr: e.activation(out=sqd[0:nr, :], in_=x1[0:nr, i, :], func=AF.Square, accum_out=ssd[0:nr, 4:5]),
                         reads=x1r, writes=["sqd", "ssd2"])
                    P.op("act", lambda e, nr=nr: e.activation(out=ssd[0:nr, 5:6], in_=ssd[0:nr, 4:5], func=AF.Ln, scale=1.0 / D, bias=EPS),
                         reads=["ssd2"], writes=["ssd2"])
                    P.op("act", lambda e, nr=nr: e.activation(out=ssd[0:nr, 6:7], in_=ssd[0:nr, 5:6], func=AF.Exp, scale=-0.5),
                         reads=["ssd2"], writes=["ssd2"])
                    P.op("dve", lambda e, i=i, nr=nr: e.scalar_tensor_tensor(
                        out=x1[0:nr, i, :], in0=x1[0:nr, i, :], scalar=ssd[0:nr, 6:7], in1=gfin_sb[0:nr, :], op0=ALU.mult, op1=ALU.mult),
                        reads=x1r + ["ssd2", "gfin"], writes=x1r)
                    ydst = yp[r0 + i * 128:r0 + i * 128 + nr, :] if g < 4 else ys[:, :]
                    P.dma("pool", ydst, x1[0:nr, i, :], reads=x1r, key=("o", "y", i))
            P.emit()
    return nc


_NC_CACHE = {}


def _consts():
    k = np.arange(128)[:, None]
    q = np.arange(128)[None, :]
    c = np.zeros((128, 8, 128), np.float32)
    c[:, 0, :] = np.eye(128)
    c[:, 1, :] = np.where(k <= q, 0.0, NEG)
    c[:, 2, :] = np.where(k < q, 0.0, NEG)
    c[:, 3, :] = np.where(k >= q, -1.0, 0.0)
    c[:, 4, :] = -1.0
    c[:, 5, :] = 1.0
    c[:, 6, :] = -np.eye(128)
    c[:, 7, :] = np.where(k > q, -1.0, 0.0)
    eb = np.zeros((8, 8, 128), np.float32)
    for n in range(8):
        eb[n, n, :] = 1.0
    kk = np.arange(8)[:, None, None]
    qq = np.arange(8)[None, None, :]
    cm8 = np.zeros((8, 2, 8, 8), np.float32)
    cm8[:, 0] = np.where(kk <= qq, 0.0, NEG)
    cm8[:, 1] = np.where(kk < qq, 0.0, NEG)
    cm8 = cm8.reshape(8, 2, 64)
    pp = np.arange(128)
    oh4 = (pp[:, None] // 32 == np.arange(4)[None, :]).astype(np.float32)
    pmod = (pp % 32).astype(np.float32).reshape(128, 1)
    blk = 2 * np.arange(16)[None, :] + pp[:, None] // 64
    zsel2 = (blk[:, :, None] == np.arange(32)[None, None, :]).astype(np.float32) / 256.0
    eg = (np.arange(32)[:, None, None] == blk.T[None, :, :]).astype(np.float32)
    bf = ml_dtypes.bfloat16
    return c.astype(bf), eb.astype(bf), cm8.astype(bf), oh4, pmod, zsel2.astype(bf), eg.astype(bf)


def _rope_table(pos):
    half = 32
    inv = (10000.0 ** (-np.arange(half, dtype=np.float32) / half)).astype(np.float32)
    ang = pos.astype(np.float32)[:, None] * inv[None, :]
    return np.concatenate([np.cos(ang), np.sin(ang)], axis=1).astype(np.float32)


def kernel(x_prompt, x_sample, cache_moba_k, cache_moba_v, cache_sb_k, cache_sb_v,
           page_table, g_mix, w_in, w_branch_moba, w_branch_sb, w_out,
           g_ffn, w_ffn_gate, w_ffn_up, w_ffn_down, g_final):
    f = lambda a: np.ascontiguousarray(np.asarray(a))
    if "nc" not in _NC_CACHE:
        _NC_CACHE["nc"] = build_nc()
    nc = _NC_CACHE["nc"]
    cbf, ebk, cm8, oh4, pmod, zsel2, eg = _consts()
    past_len = page_table.shape[1] * 128
    pos = np.concatenate([np.arange(SEQ), np.tile(past_len + np.arange(8), 4)])
    cs_all = _rope_table(pos)
    caches = [f(c).reshape(NPOOL * 128, 512) for c in (cache_moba_k, cache_moba_v, cache_sb_k, cache_sb_v)]
    shared = {
        "cmk": caches[0], "cmv": caches[1], "csk": caches[2], "csv": caches[3],
        "w_in": f(w_in)[0], "w_bm": f(w_branch_moba)[0], "w_bs": f(w_branch_sb)[0], "w_out": f(w_out)[0],
        "w_g": f(w_ffn_gate)[0], "w_u": f(w_ffn_up)[0], "w_d": f(w_ffn_down)[0],
        "gmixT": f(f(g_mix)[0].reshape(8, 128).T), "gffnT": f(f(g_ffn)[0].reshape(8, 128).T),
        "gfin": f(g_final).reshape(1, D), "cs_all": cs_all, "cbf": cbf, "ebk": ebk,
        "cm8": cm8, "oh4": oh4, "pmod": pmod, "zsel2": zsel2, "eg": eg,
    }
    xpn, xsn, ptn = f(x_prompt), f(x_sample), f(page_table).astype(np.int32)
    in_maps = []
    for c in range(NCORES):
        m = dict(shared)
        m["xp"] = xpn[c]
        m["xs"] = xsn[4 * c:4 * c + 4].reshape(ST, D)
        m["pt"] = ptn[4 * c:4 * c + 4]
        in_maps.append(m)
    res = run_bass_kernel_spmd(nc, in_maps, core_ids=list(range(NCORES)))
    R = res.results
    y_prompt = np.stack([R[c]["yp"] for c in range(NCORES)]).astype(np.float32)
    y_sample = np.concatenate([R[c]["ys"].reshape(4, 8, D) for c in range(NCORES)]).astype(np.float32)
    outs = [y_prompt, y_sample]
    for nm in ("mk", "mv", "sk", "sv"):
        outs.append(np.stack([R[c][nm + "p"].reshape(SEQ, 8, 64) for c in range(NCORES)])[None].astype(np.float32))
    for nm in ("mk", "mv", "sk", "sv"):
        outs.append(np.concatenate([R[c][nm + "s"].reshape(4, 8, 8, 64) for c in range(NCORES)])[None].astype(np.float32))
    return tuple(outs)
```

```python
import numpy as np
import ml_dtypes
from contextlib import ExitStack
import concourse.bass as bass
import concourse.mybir as mybir
from concourse.bass_utils import run_bass_kernel_spmd

F32 = mybir.dt.float32
BF16 = mybir.dt.bfloat16
I32 = mybir.dt.int32
AF = mybir.ActivationFunctionType
ALU = mybir.AluOpType
AX = mybir.AxisListType

ENGS = ("pe", "act", "dve", "pool", "sp")
NCORES = 8
D = 1024
SEQ = 2048
NT = 16
ST = 32
NTOK = SEQ + ST
DFF = 2816
NFC = 22
NPOOL = 2560
NPAGES = 64
NEG = -30000.0
EPS = 1e-6


class Op:
    __slots__ = ("eng", "fn", "reads", "writes", "dma", "key", "deps", "needed", "token", "idx")

    def __init__(self, eng, fn, reads, writes, dma, key):
        self.eng, self.fn, self.reads, self.writes, self.dma, self.key = eng, fn, reads, writes, dma, key
        self.deps = []
        self.needed = False
        self.token = None


class _Rec:
    call = None

    def __getattr__(self, name):
        def f(*a, **k):
            self.call = (name, a, k)
            return None
        return f


class Ctx:
    def __init__(self, nc, es, n_dma_sems=40):
        self.nc = nc
        self.esem = {e: es.enter_context(nc.semaphore("e_" + e)) for e in ENGS if e != "sp"}
        self.cnt = {e: 0 for e in ENGS}
        self.dpool = [es.enter_context(nc.semaphore("d%d" % i)) for i in range(n_dma_sems)]
        self.dcnt = [0] * n_dma_sems
        self.out_waits = {}


class Prog:
    def __init__(self, ctx, paranoid=True):
        self.ctx = ctx
        self.ops = []
        self.paranoid = paranoid

    def op(self, eng, fn, reads=(), writes=(), dma=False, key=None):
        rec = _Rec()
        fn(rec)
        name, a, k = rec.call
        fn = lambda e: getattr(e, name)(*a, **k)
        o = Op(eng, fn, tuple(reads), tuple(writes), dma, key)
        o.idx = len(self.ops)
        self.ops.append(o)
        return o

    def i(self, eng, name, reads, writes, *args, **kwargs):
        return self.op(eng, lambda e: getattr(e, name)(*args, **kwargs), reads, writes)

    def dma(self, q, out, in_, reads=(), writes=(), key=None):
        return self.op(q, lambda e: e.dma_start(out=out, in_=in_), reads, writes, dma=True, key=key)

    def analyze(self):
        last_w, readers = {}, {}
        for o in self.ops:
            deps = {}
            for r in o.reads:
                w = last_w.get(r)
                if w is not None:
                    deps[w.idx] = (w, "raw")
            for r in o.writes:
                w = last_w.get(r)
                if w is not None and w.idx not in deps:
                    if not (w.dma and o.dma and w.key is not None and w.key == o.key):
                        deps[w.idx] = (w, "waw")
                for rd in readers.get(r, ()):
                    if rd.idx not in deps:
                        deps[rd.idx] = (rd, "war")
            for r in o.reads:
                readers.setdefault(r, []).append(o)
            for r in o.writes:
                last_w[r] = o
                readers[r] = []
            out = []
            for d, kind in deps.values():
                if d is o:
                    continue
                if (not d.dma) and d.eng == o.eng:
                    if d.eng in ("pe", "sp"):
                        continue
                    if kind == "war" or not self.paranoid:
                        continue
                out.append(d)
                d.needed = True
            o.deps = out
        for e in ENGS:
            if e == "sp":
                continue
            for o in reversed(self.ops):
                if o.eng == e and not o.dma:
                    o.needed = True
                    break

    def emit(self):
        ctx = self.ctx
        nc = ctx.nc
        self.analyze()
        dkey = {}
        for o in self.ops:
            if o.dma:
                k = o.key if o.key is not None else (o.writes[0] if o.writes else ("dma", o.idx))
                if k not in dkey:
                    dkey[k] = len(dkey)
                    assert len(dkey) <= len(ctx.dpool), "too many DMA semaphores"
                i = dkey[k]
                ctx.dcnt[i] += 16
                o.token = (ctx.dpool[i], ctx.dcnt[i])
            elif o.needed:
                ctx.cnt[o.eng] += 1
                o.token = (ctx.esem[o.eng], ctx.cnt[o.eng])
        per_eng = {e: [o for o in self.ops if o.eng == e] for e in ENGS}
        fin_e = dict(ctx.cnt)
        fin_d = list(ctx.dcnt)

        def run(engobj, ename):
            know = {}
            for o in per_eng[ename]:
                for d in o.deps:
                    s, v = d.token
                    if know.get(id(s), 0) < v:
                        engobj.wait_ge(s, v)
                        know[id(s)] = v
                ins = o.fn(engobj)
                if o.token is not None:
                    ins.then_inc(o.token[0], 16 if o.dma else 1)
            for e2 in ENGS:
                if e2 == "sp" or e2 == ename:
                    continue
                if fin_e[e2] > 0 and know.get(id(ctx.esem[e2]), 0) < fin_e[e2]:
                    engobj.wait_ge(ctx.esem[e2], fin_e[e2])
            for i in range(len(dkey)):
                if fin_d[i] > 0 and know.get(id(ctx.dpool[i]), 0) < fin_d[i]:
                    engobj.wait_ge(ctx.dpool[i], fin_d[i])

        with nc.Block() as block:
            @block.tensor
            def _(e):
                run(e, "pe")

            @block.scalar
            def _(e):
                run(e, "act")

            @block.vector
            def _(e):
                run(e, "dve")

            @block.gpsimd
            def _(e):
                run(e, "pool")

            @block.sync
            def _(e):
                run(e, "sp")


def build_nc():
    nc = bass.Bass("TRN2", target_bir_lowering=False)

    def din(name, shape, dt=F32):
        return nc.dram_tensor(name, list(shape), dt, kind="ExternalInput").ap()

    def dout(name, shape, dt=F32):
        return nc.dram_tensor(name, list(shape), dt, kind="ExternalOutput").ap()

    xp = din("xp", [SEQ, D])
    xs = din("xs", [ST, D])
    cmk = din("cmk", [NPOOL * 128, 512])
    cmv = din("cmv", [NPOOL * 128, 512])
    csk = din("csk", [NPOOL * 128, 512])
    csv = din("csv", [NPOOL * 128, 512])
    ptab = din("pt", [4, NPAGES], I32)
    w_in = din("w_in", [D, 5120])
    w_bm = din("w_bm", [512, D])
    w_bs = din("w_bs", [512, D])
    w_out = din("w_out", [D, D])
    w_g = din("w_g", [D, DFF])
    w_u = din("w_u", [D, DFF])
    w_d = din("w_d", [DFF, D])
    gmixT = din("gmixT", [128, 8])
    gffnT = din("gffnT", [128, 8])
    gfin = din("gfin", [1, D])
    cs_all = din("cs_all", [NTOK, 64])
    cbf = din("cbf", [128, 8, 128], BF16)
    ebk = din("ebk", [8, 8, 128], BF16)
    cm8_d = din("cm8", [8, 2, 64], BF16)
    oh4_d = din("oh4", [128, 4])
    pmod_d = din("pmod", [128, 1])
    zsel_d = din("zsel2", [128, 16, 32], BF16)
    eg_d = din("eg", [32, 16, 128], BF16)

    yp = dout("yp", [SEQ, D])
    ys = dout("ys", [ST, D])
    o_kv = {}
    for nm in ("mk", "mv", "sk", "sv"):
        o_kv[nm] = (dout(nm + "p", [SEQ, 512]), dout(nm + "s", [ST, 512]))

    def wscr(name, shape):
        return nc.dram_tensor(name, list(shape), BF16, kind="Internal").ap()

    w_in_b = wscr("w_in_b", [D, 5120])
    w_bm_b = wscr("w_bm_b", [512, D])
    w_bs_b = wscr("w_bs_b", [512, D])
    w_out_b = wscr("w_out_b", [D, D])
    w_g_b = wscr("w_g_b", [D, DFF])
    w_u_b = wscr("w_u_b", [D, DFF])
    w_d_b = wscr("w_d_b", [DFF, D])
    hT_scr = nc.dram_tensor("hT_scr", [128, 8, NTOK], BF16, kind="Internal").ap()
    oa_scr = nc.dram_tensor("oa_scr", [NTOK, 512], BF16, kind="Internal").ap()
    os_scr = nc.dram_tensor("os_scr", [NTOK, 512], BF16, kind="Internal").ap()

    def tok_rows(t):
        return (t * 128, 128) if t < NT else (SEQ, ST)

    def x_src(t):
        return xp[t * 128:(t + 1) * 128, :] if t < NT else xs[:, :]

    with ExitStack() as es:
        def sb(name, shape, dt, stack=None):
            return (stack or es).enter_context(nc.sbuf_tensor(name, list(shape), dt))

        def ps(name, shape, dt):
            return es.enter_context(nc.psum_tensor(name, list(shape), dt))

        ctx = Ctx(nc, es, n_dma_sems=48)
        pb = [ps("pb%d" % i, [128, 512], F32) for i in range(6)]
        pt16 = [ps("pt16_%d" % i, [128, 1024], BF16) for i in range(2)]

        consts = sb("consts", [128, 8, 128], BF16)
        ident = consts[:, 0, :]
        mask_incl = consts[:, 1, :]
        mask_strict = consts[:, 2, :]
        Uneg = consts[:, 3, :]
        negones = consts[:, 4, :]
        ones = consts[:, 5, :]
        negI = consts[:, 6, :]
        Usneg = consts[:, 7, :]
        eb = sb("eb", [8, 8, 128], BF16)
        gmix_sb = sb("gmix_sb", [128, 8], F32)
        gffn_sb = sb("gffn_sb", [128, 8], F32)
        gfin_sb = sb("gfin_sb", [128, D], F32)
        zeros16 = sb("zeros16", [128, 512], BF16)
        QTa_s = sb("QTa_s", [128, 4, ST], BF16)
        KTa_s = sb("KTa_s", [128, 4, ST], BF16)
        QTs_s = sb("QTs_s", [128, 4, ST], BF16)
        KTs_s = sb("KTs_s", [128, 4, ST], BF16)
        Vna = sb("Vna", [ST, 512], BF16)
        Vns = sb("Vns", [ST, 512], BF16)

        with ExitStack() as s1:
            hT = sb("hT", [128, 8, NTOK], BF16, s1)
            QTa = sb("QTa", [128, 4, SEQ], BF16, s1)
            KTa = sb("KTa", [128, 4, SEQ], BF16, s1)
            QTs = sb("QTs", [128, 4, SEQ], BF16, s1)
            KTs = sb("KTs", [128, 4, SEQ], BF16, s1)
            Va = sb("Va", [128, NT, 8, 65], BF16, s1)
            Vs = sb("Vs", [128, NT, 512], BF16, s1)

            with ExitStack() as s2:
                P = Prog(ctx)
                P.dma("sp", consts[:], cbf, writes=["consts"])
                P.dma("sp", eb[:], ebk, writes=["eb"])
                P.dma("sp", gmix_sb[:], gmixT, writes=["gmix"])
                P.dma("sp", gffn_sb[:], gffnT, writes=["gffn"])
                P.dma("sp", gfin_sb[:], gfin.partition_broadcast(128).rearrange("p a d -> p (a d)"), writes=["gfin"])
                for nm, src, dst, a in (("w_in", w_in, w_in_b, 4),):
                    P.dma("pool", dst.rearrange("k (a n) -> k a n", a=a), src.rearrange("k (a n) -> k a n", a=a),
                          writes=[("wscr", nm)], key=("wscr", nm))
                P.op("pool", lambda e: e.memset(zeros16[:], 0.0), writes=["zeros16"])
                P.op("pool", lambda e: e.memset(Va[:, :, :, 64:65], 1.0), writes=["Va_ones"])

                xt = [sb("xt%d" % i, [128, D], F32, s2) for i in range(2)]
                sq = sb("sq", [128, D], BF16, s2)
                ssq = [sb("ssq%d" % i, [128, 4], F32, s2) for i in range(2)]
                hb = [sb("hb%d" % i, [128, D], BF16, s2) for i in range(2)]
                cs_sb = sb("cs_sb", [128, NT + 1, 64], F32, s2)
                def load_cs():
                    P.dma("sp", cs_sb[:, 0:NT, :], cs_all[0:SEQ, :].rearrange("(t p) c -> p t c", p=128),
                          writes=[("cs", t) for t in range(NT)], key=("cs", 0))
                    P.dma("sp", cs_sb[0:ST, NT, :], cs_all[SEQ:SEQ + ST, :], writes=[("cs", NT)], key=("cs", 1))

                for t in range(NT + 1):
                    r0, nr = tok_rows(t)
                    b = t % 2
                    X, SS, HB = xt[b], ssq[b], hb[b]
                    if t == 0:
                        P.dma("sp", X[0:nr, :], x_src(t), writes=[("xt", b)])
                    if t + 1 <= NT:
                        r0n, nrn = tok_rows(t + 1)
                        P.dma("sp", xt[(t + 1) % 2][0:nrn, :], x_src(t + 1), writes=[("xt", (t + 1) % 2)])
                    if t == 0:
                        load_cs()
                    P.op("act", lambda e, X=X, SS=SS, nr=nr: e.activation(
                        out=sq[0:nr, :], in_=X[0:nr, :], func=AF.Square, accum_out=SS[0:nr, 0:1]),
                        reads=[("xt", b)], writes=["sq", ("ssq", b)])
                    P.op("act", lambda e, SS=SS, nr=nr: e.activation(
                        out=SS[0:nr, 1:2], in_=SS[0:nr, 0:1], func=AF.Ln, scale=1.0 / D, bias=EPS),
                        reads=[("ssq", b)], writes=[("ssq", b)])
                    P.op("act", lambda e, SS=SS, nr=nr: e.activation(
                        out=SS[0:nr, 2:3], in_=SS[0:nr, 1:2], func=AF.Exp, scale=-0.5),
                        reads=[("ssq", b)], writes=[("ssq", b)])
                    P.op("dve", lambda e, X=X, SS=SS, HB=HB, nr=nr: e.tensor_scalar(
                        out=HB[0:nr, :], in0=X[0:nr, :], scalar1=SS[0:nr, 2:3], scalar2=None, op0=ALU.mult),
                        reads=[("xt", b), ("ssq", b)], writes=[("hb", b)])
                    pT = pt16[b]
                    for kc in range(8):
                        P.op("pe", lambda e, pT=pT, HB=HB, kc=kc, nr=nr: e.transpose(
                            out=pT[:, kc * 128:kc * 128 + nr], in_=HB[0:nr, kc * 128:(kc + 1) * 128],
                            identity=ident[0:nr, 0:nr]),
                            reads=[("hb", b), "consts"], writes=[("pt16", b)])
                    P.op("dve", lambda e, pT=pT, r0=r0, nr=nr: e.tensor_tensor(
                        out=hT[:, :, r0:r0 + nr],
                        in0=pT[:, :].rearrange("p (k t) -> p k t", t=128)[:, :, 0:nr],
                        in1=gmix_sb[:, :].unsqueeze(2).to_broadcast([128, 8, nr]), op=ALU.mult),
                        reads=[("pt16", b), "gmix"], writes=[("hT", t)])
                    P.dma("sp", hT_scr[:, :, r0:r0 + nr], hT[:, :, r0:r0 + nr], reads=[("hT", t)], key=("o", "hTs"))

                Wb = [sb("Wb%d" % i, [128, 8, 512], BF16, s2) for i in range(2)]
                raw = [sb("raw%d" % i, [128, 512], F32, s2) for i in range(2)]
                stg = [sb("stg%d" % i, [128, 512], F32, s2) for i in range(3)]
                tmpv = sb("tmpv", [128, 256], F32, s2)
                tmpg = sb("tmpg", [128, 256], F32, s2)
                qb = [sb("qb%d" % i, [128, 512], BF16, s2) for i in range(2)]
                w_in_v = w_in_b.rearrange("(kc p) n -> p kc n", p=128)
                it = 0
                def load_wb(cb):
                    for half in range(2):
                        P.dma("sp", Wb[cb % 2][:, half * 4:(half + 1) * 4, :],
                              w_in_v[:, half * 4:(half + 1) * 4, cb * 512:(cb + 1) * 512],
                              reads=[("wscr", "w_in")], writes=[("Wb", cb % 2, half)])

                load_wb(0)
                for cb in range(6):
                    wslot = cb % 2
                    W = Wb[wslot]
                    if cb + 1 < 6:
                        load_wb(cb + 1)
                    kind = ("qa", "ka", "va", "qs", "ks", "vs")[cb]
                    for t in range(NT + 1):
                        r0, nr = tok_rows(t)
                        acc = pb[it % 2]
                        akey = ("pb", it % 2)
                        for kc in range(8):
                            P.op("pe", lambda e, acc=acc, W=W, kc=kc, r0=r0, nr=nr: e.matmul(
                                acc[0:nr, :], lhsT=hT[:, kc, r0:r0 + nr], rhs=W[:, kc, :],
                                start=(kc == 0), stop=(kc == 7)),
                                reads=[("hT", t), ("Wb", wslot, kc // 4)], writes=[akey])
                        rslot = it % 2
                        R = raw[rslot]
                        sslot = it % 3
                        S = stg[sslot]
                        qslot = it % 2
                        Q = qb[qslot]
                        it += 1
                        if kind in ("qa", "ka"):
                            P.op("act", lambda e, R=R, acc=acc, nr=nr: e.activation(out=R[0:nr, :], in_=acc[0:nr, :], func=AF.Copy),
                                 reads=[akey], writes=[("raw", rslot)])
                            Rv = R[:, :].rearrange("p (h t d) -> p h t d", t=2, d=32)
                            Sv = S[:, :].rearrange("p (h t d) -> p h t d", t=2, d=32)
                            cosb = cs_sb[0:nr, t, 0:32].unsqueeze(1).to_broadcast([nr, 8, 32])
                            sinb = cs_sb[0:nr, t, 32:64].unsqueeze(1).to_broadcast([nr, 8, 32])
                            tv = tmpv[:, :].rearrange("p (h d) -> p h d", d=32)
                            tg = tmpg[:, :].rearrange("p (h d) -> p h d", d=32)
                            P.op("dve", lambda e, Sv=Sv, Rv=Rv, cosb=cosb, nr=nr: e.tensor_tensor(
                                out=Sv[0:nr, :, 0, :], in0=Rv[0:nr, :, 0, :], in1=cosb, op=ALU.mult),
                                reads=[("raw", rslot), ("cs", t)], writes=[("stg", sslot, 0)])
                            P.op("dve", lambda e, tv=tv, Rv=Rv, sinb=sinb, nr=nr: e.tensor_tensor(
                                out=tv[0:nr], in0=Rv[0:nr, :, 1, :], in1=sinb, op=ALU.mult),
                                reads=[("raw", rslot), ("cs", t)], writes=["tmpv"])
                            P.op("dve", lambda e, Sv=Sv, tv=tv, nr=nr: e.tensor_tensor(
                                out=Sv[0:nr, :, 0, :], in0=Sv[0:nr, :, 0, :], in1=tv[0:nr], op=ALU.subtract),
                                reads=["tmpv", ("stg", sslot, 0)], writes=[("stg", sslot, 0)])
                            P.op("pool", lambda e, Sv=Sv, Rv=Rv, cosb=cosb, nr=nr: e.tensor_tensor(
                                out=Sv[0:nr, :, 1, :], in0=Rv[0:nr, :, 1, :], in1=cosb, op=ALU.mult),
                                reads=[("raw", rslot), ("cs", t)], writes=[("stg", sslot, 1)])
                            P.op("pool", lambda e, tg=tg, Rv=Rv, sinb=sinb, nr=nr: e.tensor_tensor(
                                out=tg[0:nr], in0=Rv[0:nr, :, 0, :], in1=sinb, op=ALU.mult),
                                reads=[("raw", rslot), ("cs", t)], writes=["tmpg"])
                            P.op("pool", lambda e, Sv=Sv, tg=tg, nr=nr: e.tensor_tensor(
                                out=Sv[0:nr, :, 1, :], in0=Sv[0:nr, :, 1, :], in1=tg[0:nr], op=ALU.add),
                                reads=["tmpg", ("stg", sslot, 1)], writes=[("stg", sslot, 1)])
                            sreads = [("stg", sslot, 0), ("stg", sslot, 1)]
                        else:
                            P.op("act", lambda e, S=S, acc=acc, nr=nr: e.activation(out=S[0:nr, :], in_=acc[0:nr, :], func=AF.Copy),
                                 reads=[akey], writes=[("stg", sslot, 0), ("stg", sslot, 1)])
                            sreads = [("stg", sslot, 0), ("stg", sslot, 1)]
                        if kind in ("ka", "va", "ks", "vs"):
                            dst = o_kv[{"ka": "mk", "va": "mv", "ks": "sk", "vs": "sv"}[kind]]
                            dap = dst[0][r0:r0 + nr, :] if t < NT else dst[1][:, :]
                            P.dma("sp", dap, S[0:nr, :], reads=sreads, key=("o", "stg", sslot))
                        if kind in ("va", "vs"):
                            if t < NT:
                                if kind == "va":
                                    P.op("pool", lambda e, S=S, t=t: e.tensor_copy(
                                        out=Va[:, t, :, 0:64], in_=S[:, :].rearrange("p (h d) -> p h d", d=64)),
                                        reads=sreads, writes=[("Va", t)])
                                else:
                                    P.op("pool", lambda e, S=S, t=t: e.tensor_copy(out=Vs[:, t, :], in_=S[:, :]),
                                         reads=sreads, writes=[("Vs", t)])
                            else:
                                Vn = Vna if kind == "va" else Vns
                                P.op("pool", lambda e, S=S, Vn=Vn: e.tensor_copy(out=Vn[:, :], in_=S[0:ST, :]),
                                     reads=sreads, writes=["Vn" + kind])
                        else:
                            scale = 0.125 if kind in ("qa", "qs") else 1.0
                            P.op("dve", lambda e, Q=Q, S=S, nr=nr, scale=scale: e.tensor_scalar(
                                out=Q[0:nr, :], in0=S[0:nr, :], scalar1=scale, scalar2=None, op0=ALU.mult),
                                reads=sreads, writes=[("qb", qslot)])
                            pT = pt16[it % 2]
                            for j in range(4):
                                P.op("pe", lambda e, pT=pT, Q=Q, j=j, nr=nr: e.transpose(
                                    out=pT[:, j * 128:j * 128 + nr], in_=Q[0:nr, j * 128:(j + 1) * 128],
                                    identity=ident[0:nr, 0:nr]),
                                    reads=[("qb", qslot), "consts"], writes=[("pt16", it % 2)])
                            if t < NT:
                                dstT = {"qa": QTa, "ka": KTa, "qs": QTs, "ks": KTs}[kind]
                                dsl = dstT[:, :, r0:r0 + nr]
                            else:
                                dstT = {"qa": QTa_s, "ka": KTa_s, "qs": QTs_s, "ks": KTs_s}[kind]
                                dsl = dstT[:, :, :]
                            P.op("act", lambda e, pT=pT, dsl=dsl, nr=nr: e.activation(
                                out=dsl, in_=pT[:, 0:512].rearrange("p (j t) -> p j t", t=128)[:, :, 0:nr], func=AF.Copy),
                                reads=[("pt16", it % 2)], writes=[(kind + "T", t)])
                P.emit()

            with ExitStack() as s2:
                P = Prog(ctx)
                otok_a = sb("otok_a", [128, NT, 512], BF16, s2)
                otok_s = sb("otok_s", [128, NT, 512], BF16, s2)
                kmf = sb("kmf", [128, 4, 8], F32, s2)
                kmT = sb("kmT", [128, 4, 8], BF16, s2)
                Gm = sb("Gm", [128, 8, 8], F32, s2)
                top8 = sb("top8", [128, 8, 8], F32, s2)
                selt = sb("selt", [128, 8, 8], F32, s2)
                Mtok = sb("Mtok", [128, 8, 64], BF16, s2)
                MTs = [sb("MTs%d" % i, [8, 512], BF16, s2) for i in range(2)]
                Pt = [sb("Pt%d" % i, [128, 512], BF16, s2) for i in range(4)]
                Ef = [sb("Ef%d" % i, [128, 512], F32, s2) for i in range(3)]
                SP = [sb("SP%d" % i, [128, 512], BF16, s2) for i in range(3)]
                Rr = [sb("Rr%d" % i, [128, 512], BF16, s2) for i in range(4)]
                rden = sb("rden", [128, 4], F32, s2)
                for nm, src, dst, a in (("w_bm", w_bm, w_bm_b, 1), ("w_bs", w_bs, w_bs_b, 1),
                                        ("w_out", w_out, w_out_b, 1), ("w_g", w_g, w_g_b, 2), ("w_u", w_u, w_u_b, 2),
                                        ("w_d", w_d, w_d_b, 1)):
                    P.dma("pool", dst.rearrange("k (a n) -> k a n", a=a), src.rearrange("k (a n) -> k a n", a=a),
                          writes=[("wscr", nm)], key=("wscr", nm))

                for hp in range(4):
                    P.op("dve", lambda e, hp=hp: e.reduce_sum(
                        out=kmf[:, hp, :], in_=KTa[:, hp, :].rearrange("p (n k) -> p n k", k=256), axis=AX.X),
                        writes=[("kmf", hp)])
                P.op("dve", lambda e: e.tensor_scalar(out=kmT[:], in0=kmf[:], scalar1=1.0 / 256, scalar2=None, op0=ALU.mult),
                     reads=[("kmf", hp) for hp in range(4)], writes=["kmT"])
                for c in range(8, NT):
                    cur = c // 2
                    G = pb[5]
                    for h in range(8):
                        hp, hb_ = h // 2, (h % 2) * 64
                        P.op("pe", lambda e, G=G, h=h, hp=hp, hb_=hb_, c=c: e.matmul(
                            G[:, h * 8:(h + 1) * 8], lhsT=QTa[hb_:hb_ + 64, hp, c * 128:(c + 1) * 128],
                            rhs=kmT[hb_:hb_ + 64, hp, :], start=True, stop=True),
                            reads=["kmT"], writes=[("pb", 5)])
                    P.op("dve", lambda e, G=G: e.tensor_copy(out=Gm[:], in_=G[:, 0:64].rearrange("p (h n) -> p h n", n=8)),
                         reads=[("pb", 5)], writes=["Gm"])
                    P.op("dve", lambda e, cur=cur: e.memset(Gm[:, :, cur:8], -1e30), reads=["Gm"], writes=["Gm"])
                    for h in range(8):
                        P.op("dve", lambda e, h=h: e.max(out=top8[:, h, :], in_=Gm[:, h, :]), reads=["Gm"], writes=[("top8", h)])
                    P.op("dve", lambda e: e.tensor_tensor(
                        out=selt[:], in0=Gm[:], in1=top8[:, :, 2:3].to_broadcast([128, 8, 8]), op=ALU.is_ge),
                        reads=["Gm"] + [("top8", h) for h in range(8)], writes=["selt"])
                    Mv = Mtok[:, c - 8, :].rearrange("p (h n) -> p h n", n=8)
                    P.op("dve", lambda e, Mv=Mv: e.tensor_scalar(
                        out=Mv, in0=selt[:], scalar1=1.0, scalar2=-NEG, op0=ALU.subtract, op1=ALU.mult),
                        reads=["selt"], writes=[("Mtok", c)])
                    P.op("dve", lambda e, Mv=Mv, cur=cur: e.memset(Mv[:, :, cur:cur + 1], 0.0),
                         reads=[("Mtok", c)], writes=[("Mtok", c)])

                sidx = [0]
                gidx = [0]

                def attn_head_group(h, qg, moba):
                    hp, hb_ = h // 2, (h % 2) * 64
                    KT, QT = (KTa, QTa) if moba else (KTs, QTs)
                    c_lo, c_hi = qg * 4, qg * 4 + 3
                    oi = 4 + gidx[0] % 2
                    mslot = rslot = gidx[0] % 2
                    gidx[0] += 1
                    O = pb[oi]
                    okey = ("pb", oi)
                    ncol = 4 * 65 if moba else 4 * 64
                    ow = 65 if moba else 64
                    P.i("pe", "matmul", ["zeros16"], [okey], O[:, 0:ncol], lhsT=zeros16[:, 0:128], rhs=zeros16[:, 0:ncol],
                        start=True, stop=False)
                    mts = None
                    if moba and qg >= 2:
                        mts = MTs[mslot]
                        pT = pt16[0]
                        for i in range(4):
                            c = c_lo + i
                            P.i("pe", "transpose", [("Mtok", c), "consts"], [("pt16", 0)], out=pT[0:8, i * 128:(i + 1) * 128],
                                in_=Mtok[:, c - 8, h * 8:(h + 1) * 8], identity=ident)
                        P.i("dve", "tensor_copy", [("pt16", 0)], [("MTs", mslot)], out=mts[:, :], in_=pT[0:8, 0:512])
                    Rpp = None
                    if not moba:
                        Rpp = (Rr[2 * rslot], Rr[2 * rslot + 1])
                        for q_ in range(2):
                            P.i("pool", "memset", [], [("Rr", 2 * rslot + q_)], Rpp[q_][:], 0.0)
                    rstep = [0]
                    kts = list(range(0, c_hi + 1)) if moba else list(range(c_hi, -1, -1))

                    def stageA(kt):
                        c0 = max(kt, c_lo)
                        N = (c_hi + 1 - c0) * 128
                        q0 = c0 * 128
                        diag = kt >= c_lo
                        n = sidx[0]
                        sidx[0] += 1
                        st = dict(kt=kt, c0=c0, N=N, coff=(c0 - c_lo) * 128, sb_i=n % 4, eslot=n % 3, pslot=n % 4)
                        S1 = pb[st["sb_i"]]
                        k1 = ("pb", st["sb_i"])
                        P.i("pe", "matmul", [], [k1], S1[:, 0:N], lhsT=KT[hb_:hb_ + 64, hp, kt * 128:(kt + 1) * 128],
                            rhs=QT[hb_:hb_ + 64, hp, q0:q0 + N], start=True, stop=False)
                        if diag:
                            P.i("pe", "matmul", ["consts"], [k1], S1[:, 0:128], lhsT=ident,
                                rhs=(mask_incl if moba else mask_strict), start=False, stop=False)
                        if moba:
                            n_blk = kt // 2
                            cm = max(c0, 2 * n_blk + 2)
                            if qg >= 2 and cm <= c_hi:
                                o1 = (cm - c0) * 128
                                m1 = (cm - c_lo) * 128
                                P.i("pe", "matmul", ["eb", ("MTs", mslot)], [k1], S1[:, o1:N], lhsT=eb[0:8, n_blk, :],
                                    rhs=mts[0:8, m1:512], start=False, stop=True)
                        return st

                    def stageB(st):
                        N = st["N"]
                        S1 = pb[st["sb_i"]]
                        k1 = ("pb", st["sb_i"])
                        if moba:
                            A = Pt[st["pslot"]]
                            P.i("act", "activation", [k1], [("Pt", st["pslot"])], out=A[:, 0:N], in_=S1[:, 0:N], func=AF.Exp)
                        else:
                            E, SPt = Ef[st["eslot"]], SP[st["eslot"]]
                            P.i("act", "activation", [k1], [("Ef", st["eslot"])], out=E[:, 0:N], in_=S1[:, 0:N], func=AF.Exp)
                            P.i("act", "activation", [("Ef", st["eslot"])], [("SP", st["eslot"])], out=SPt[:, 0:N],
                                in_=E[:, 0:N], func=AF.Ln, bias=1.0)

                    def stageC(st, first):
                        kt, c0, N, coff = st["kt"], st["c0"], st["N"], st["coff"]
                        S1 = pb[st["sb_i"]]
                        k1 = ("pb", st["sb_i"])
                        A = Pt[st["pslot"]]
                        if not moba:
                            SPt = SP[st["eslot"]]
                            P.i("pe", "matmul", ["consts", ("SP", st["eslot"])], [k1], S1[:, 0:N], lhsT=Uneg, rhs=SPt[:, 0:N],
                                start=False, stop=first)
                            ra = rstep[0] % 2
                            rstep[0] += 1
                            Rcur, Rnxt = Rpp[ra], Rpp[1 - ra]
                            kcur, knxt = ("Rr", 2 * rslot + ra), ("Rr", 2 * rslot + 1 - ra)
                            if kt > 0:
                                P.i("pool", "tensor_tensor", [("SP", st["eslot"]), kcur], [knxt],
                                    out=Rnxt[:, coff:coff + N], in0=Rcur[:, coff:coff + N], in1=SPt[:, 0:N], op=ALU.add)
                            if not first:
                                P.i("pe", "matmul", ["consts", kcur], [k1], S1[:, 0:N], lhsT=negones,
                                    rhs=Rcur[:, coff:coff + N], start=False, stop=True)
                            P.i("act", "activation", [k1], [("Pt", st["pslot"])], out=A[:, 0:N], in_=S1[:, 0:N], func=AF.Exp)
                        for c in range(c0, c_hi + 1):
                            j = c - c0
                            i = c - c_lo
                            rhs = Va[:, kt, h, :] if moba else Vs[:, kt, h * 64:(h + 1) * 64]
                            P.i("pe", "matmul", [("Pt", st["pslot"])], [okey], O[:, i * ow:(i + 1) * ow],
                                lhsT=A[:, j * 128:(j + 1) * 128], rhs=rhs, start=False, stop=False)

                    sts = []
                    for i_, kt in enumerate(kts):
                        sts.append(stageA(kt))
                        if i_ >= 1:
                            stageB(sts[i_ - 1])
                        if i_ >= 2:
                            stageC(sts[i_ - 2], i_ - 2 == 0)
                    nk = len(kts)
                    stageB(sts[nk - 1])
                    if nk >= 2:
                        stageC(sts[nk - 2], nk - 2 == 0)
                    stageC(sts[nk - 1], nk - 1 == 0)
                    if moba:
                        Ov = O[:, 0:ncol].rearrange("p (i w) -> p i w", w=65)
                        P.i("dve", "reciprocal", [okey], ["rden"], out=rden[:, :], in_=Ov[:, :, 64])
                        P.i("dve", "tensor_tensor", [okey, "rden"], [("otok_a", qg, h)],
                            out=otok_a[:, c_lo:c_hi + 1, h * 64:(h + 1) * 64], in0=Ov[:, :, 0:64],
                            in1=rden[:, :].unsqueeze(2).to_broadcast([128, 4, 64]), op=ALU.mult)
                    else:
                        Ov = O[:, 0:ncol].rearrange("p (i w) -> p i w", w=64)
                        P.i("dve", "tensor_copy", [okey], [("otok_s", qg, h)],
                            out=otok_s[:, c_lo:c_hi + 1, h * 64:(h + 1) * 64], in_=Ov)

                for h in range(8):
                    for qg in range(4):
                        attn_head_group(h, qg, True)
                for h in range(8):
                    for qg in range(4):
                        attn_head_group(h, qg, False)
                for qg in range(4):
                    for nm, ot, scr in (("otok_a", otok_a, oa_scr), ("otok_s", otok_s, os_scr)):
                        P.dma("sp", scr[qg * 512:(qg + 1) * 512, :].rearrange("(i p) f -> p i f", p=128),
                              ot[:, qg * 4:(qg + 1) * 4, :], reads=[(nm, qg, h) for h in range(8)], key=("o", nm))
                P.emit()


        with ExitStack() as s2:
            P = Prog(ctx)
            pt_sb = sb("pt_sb", [128, 4, NPAGES], I32, s2)
            ptf = sb("ptf", [128, 4, NPAGES], F32, s2)
            ptsel = sb("ptsel", [128, 64, 4], F32, s2)
            pgrp = sb("pgrp", [128, 64], F32, s2)
            idxf = sb("idxf", [128, 64], F32, s2)
            idx = sb("idx", [128, 64], I32, s2)
            oh4 = sb("oh4_sb", [128, 4], F32, s2)
            pmod = sb("pmod_sb", [128, 1], F32, s2)
            cm8 = sb("cm8_sb", [8, 2, 64], BF16, s2)
            zsel = sb("zsel_sb", [128, 16, 32], BF16, s2)
            egm = sb("eg_sb", [32, 16, 128], BF16, s2)
            Vnq = sb("Vnq", [8, 4, 2, 512], BF16, s2)
            Qbd = sb("Qbd", [128, 2, 4, 16], BF16, s2)
            Ka_seq = sb("Ka_seq", [128, NPAGES, 512], BF16, s2)
            NSL = 3
            Ksb = [sb("Ksb%d" % i, [128, 4, 512], BF16, s2) for i in range(NSL)]
            Vab = [sb("Vab%d" % i, [128, 4, 512], BF16, s2) for i in range(NSL)]
            Vsb = [sb("Vsb%d" % i, [128, 4, 512], BF16, s2) for i in range(NSL)]
            KTg = [[sb("KTg%d_%d" % (a, i), [128, 4, 4, 128], BF16, s2) for i in range(NSL)] for a in range(2)]
            KMb = sb("KMb", [32, 512], BF16, s2)
            KMT = sb("KMT", [128, 4, 32], BF16, s2)
            Gs = sb("Gs", [16, 4, 32], F32, s2)
            top8s = sb("top8s", [16, 4, 8], F32, s2)
            sels = sb("sels", [16, 4, 32], F32, s2)
            Msel = sb("Msel", [16, 4, 32], BF16, s2)
            MTs2 = sb("MTs2", [32, 64], BF16, s2)
            Pn = sb("Pn", [8, 64], BF16, s2)
            En = sb("En", [8, 64], F32, s2)
            SPn = sb("SPn", [8, 64], BF16, s2)
            An = sb("An", [8, 64], BF16, s2)
            Pm = [sb("Pm%d" % i, [128, 4, 64], BF16, s2) for i in range(NSL)]
            Eg = [sb("Eg%d" % i, [128, 256], F32, s2) for i in range(NSL)]
            SPg = [sb("SPg%d" % i, [128, 4, 64], BF16, s2) for i in range(NSL)]
            Ag = [sb("Ag%d" % i, [128, 4, 64], BF16, s2) for i in range(NSL)]
            Wc = [sb("Wc%d" % i, [128, 4, 64], BF16, s2) for i in range(NSL)]
            carry = sb("carry", [128, 64], BF16, s2)
            rdn = sb("rdn", [64, 1], F32, s2)
            oa_sb = sb("oa_sb", [64, 512], BF16, s2)
            os_sb = sb("os_sb", [64, 512], BF16, s2)

            P.dma("sp", pt_sb[:], ptab.partition_broadcast(128), writes=["pt_sb"])
            P.dma("sp", cm8[:], cm8_d, writes=["cm8"])
            P.dma("sp", oh4[:], oh4_d, writes=["oh4"])
            P.dma("sp", pmod[:], pmod_d, writes=["pmod"])
            P.dma("sp", zsel[:], zsel_d, writes=["zsel"])
            P.dma("sp", egm[:], eg_d, writes=["egm"])
            P.i("dve", "tensor_copy", ["pt_sb"], ["ptf"], out=ptf[:], in_=pt_sb[:])
            P.i("dve", "tensor_tensor", ["ptf", "oh4"], ["ptsel"], out=ptsel[:],
                in0=ptf[:, :, :].rearrange("p s (g l) -> p (s g) l", l=4),
                in1=oh4[:, :].unsqueeze(1).to_broadcast([128, 64, 4]), op=ALU.mult)
            P.i("dve", "reduce_sum", ["ptsel"], ["pgrp"], out=pgrp[:], in_=ptsel[:], axis=AX.X)
            P.i("dve", "tensor_scalar", ["pgrp", "pmod"], ["idxf"], out=idxf[:], in0=pgrp[:], scalar1=32.0,
                scalar2=pmod[:, 0:1], op0=ALU.mult, op1=ALU.add)
            P.i("dve", "tensor_copy", ["idxf"], ["idx"], out=idx[:], in_=idxf[:])
            for s in range(4):
                P.dma("sp", Vnq[0:8, s, 0, :], Vna[s * 8:(s + 1) * 8, :], writes=[("Vnq", s)])
                P.dma("sp", Vnq[0:8, s, 1, :], Vns[s * 8:(s + 1) * 8, :], writes=[("Vnq", s)])

            def gather(dst, cache, s, g, reads, writes, key):
                col = s * 16 + g
                P.op("pool", lambda e: e.indirect_dma_start(
                    out=dst.rearrange("p t f -> p (t f)"), out_offset=None,
                    in_=cache.rearrange("(r t) f -> r (t f)", t=4),
                    in_offset=bass.IndirectOffsetOnAxis(ap=idx[:, col:col + 1], axis=0)),
                    reads=reads, writes=writes, dma=True, key=key)

            gcount = [0]
            for s in range(4):
                sc = slice(s * 8, (s + 1) * 8)
                P.i("dve", "memset", [], ["Qbd"], Qbd[:], 0.0)
                for a, QTx in ((0, QTa_s), (1, QTs_s)):
                    P.i("dve", "tensor_copy", ["Qbd"], ["Qbd"], out=Qbd[0:64, a, :, 0:8], in_=QTx[0:64, :, sc])
                    P.i("dve", "tensor_copy", ["Qbd"], ["Qbd"], out=Qbd[64:128, a, :, 8:16], in_=QTx[64:128, :, sc])
                for g in range(16):
                    gather(Ka_seq[:, g * 4:(g + 1) * 4, :], cmk, s, g, ["idx"], [("Ka", g // 2)], ("Ka", g // 2))
                KM = pb[2]
                for j in range(NPAGES):
                    P.i("pe", "matmul", [("Ka", j // 8), "zsel"], [("pb", 2)], KM[0:32, :],
                        lhsT=zsel[:, j // 4, :], rhs=Ka_seq[:, j, :], start=(j == 0), stop=(j == NPAGES - 1))
                P.i("act", "activation", [("pb", 2)], ["KMb"], out=KMb[:, :], in_=KM[0:32, :], func=AF.Copy)
                pT = pt16[0]
                for hp in range(4):
                    P.i("pe", "transpose", ["KMb"], [("pt16", 0)], out=pT[:, hp * 32:(hp + 1) * 32],
                        in_=KMb[0:32, hp * 128:(hp + 1) * 128], identity=ident[0:32, 0:32])
                P.i("dve", "tensor_copy", [("pt16", 0)], ["KMT"], out=KMT[:, :, :],
                    in_=pT[:, 0:128].rearrange("p (h n) -> p h n", n=32))
                Gp = pb[5]
                for hp in range(4):
                    P.i("pe", "matmul", ["KMT", "Qbd"], [("pb", 5)], Gp[0:16, hp * 32:(hp + 1) * 32],
                        lhsT=Qbd[:, 0, hp, :], rhs=KMT[:, hp, :], start=True, stop=True)
                P.i("dve", "tensor_copy", [("pb", 5)], ["Gs"], out=Gs[:, :, :],
                    in_=Gp[0:16, 0:128].rearrange("p (h n) -> p h n", n=32))
                for hp in range(4):
                    P.i("dve", "max", ["Gs"], [("top8s", hp)], out=top8s[:, hp, :], in_=Gs[:, hp, :])
                P.i("dve", "tensor_tensor", ["Gs"] + [("top8s", hp) for hp in range(4)], ["sels"], out=sels[:],
                    in0=Gs[:], in1=top8s[:, :, 2:3].to_broadcast([16, 4, 32]), op=ALU.is_ge)
                P.i("dve", "tensor_scalar", ["sels"], ["Msel"], out=Msel[:], in0=sels[:], scalar1=1.0, scalar2=-NEG,
                    op0=ALU.subtract, op1=ALU.mult)
                pT = pt16[1]
                for hp in range(4):
                    P.i("pe", "transpose", ["Msel"], [("pt16", 1)], out=pT[0:32, hp * 16:(hp + 1) * 16],
                        in_=Msel[0:16, hp, :], identity=ident[0:16, 0:16])
                P.i("dve", "tensor_copy", [("pt16", 1)], ["MTs2"], out=MTs2[:, :], in_=pT[0:32, 0:64])
                Oa, Os, Dn = pb[3], pb[4], pb[5]
                P.i("pe", "matmul", ["zeros16"], [("pb", 3)], Oa[0:64, :], lhsT=zeros16[:, 0:64], rhs=zeros16[:, 0:512],
                    start=True, stop=False)
                P.i("pe", "matmul", ["zeros16"], [("pb", 4)], Os[0:64, :], lhsT=zeros16[:, 0:64], rhs=zeros16[:, 0:512],
                    start=True, stop=False)
                P.i("pe", "matmul", ["zeros16", "Gs"], [("pb", 5)], Dn[0:64, 0:8], lhsT=zeros16[:, 0:64], rhs=zeros16[:, 0:8],
                    start=True, stop=False)
                Sm, S1, S2 = pb[2], pb[0], pb[1]
                P.i("pe", "matmul", ["cm8"], [("pb", 2)], Sm[0:8, 0:64], lhsT=ident[0:8, 0:8], rhs=cm8[0:8, 0, :],
                    start=True, stop=False)
                for hp in range(4):
                    P.i("pe", "matmul", ["Qbd"], [("pb", 2)], Sm[0:8, hp * 16:(hp + 1) * 16],
                        lhsT=KTa_s[:, hp, sc], rhs=Qbd[:, 0, hp, :], start=False, stop=(hp == 3))
                P.i("act", "activation", [("pb", 2)], ["Pn"], out=Pn[:, :], in_=Sm[0:8, 0:64], func=AF.Exp)
                P.i("pe", "matmul", ["Pn", ("Vnq", s)], [("pb", 3)], Oa[0:64, :], lhsT=Pn[0:8, :], rhs=Vnq[0:8, s, 0, :],
                    start=False, stop=False)
                P.i("pe", "matmul", ["Pn"], [("pb", 5)], Dn[0:64, 0:1], lhsT=Pn[0:8, :], rhs=ones[0:8, 0:1],
                    start=False, stop=False)
                P.i("pe", "matmul", ["cm8"], [("pb", 0)], S1[0:8, 0:64], lhsT=ident[0:8, 0:8], rhs=cm8[0:8, 1, :],
                    start=True, stop=False)
                for hp in range(4):
                    P.i("pe", "matmul", ["Qbd"], [("pb", 0)], S1[0:8, hp * 16:(hp + 1) * 16],
                        lhsT=KTs_s[:, hp, sc], rhs=Qbd[:, 1, hp, :], start=False, stop=(hp == 3))
                P.i("act", "activation", [("pb", 0)], ["En"], out=En[:, :], in_=S1[0:8, 0:64], func=AF.Exp)
                P.i("act", "activation", ["En"], ["SPn"], out=SPn[:, :], in_=En[:, :], func=AF.Ln, bias=1.0)
                P.i("pe", "matmul", ["SPn"], [("pb", 1)], S2[0:8, 0:64], lhsT=Uneg[0:8, 0:8], rhs=SPn[0:8, :],
                    start=True, stop=False)
                P.i("pe", "matmul", ["cm8"], [("pb", 1)], S2[0:8, 0:64], lhsT=ident[0:8, 0:8], rhs=cm8[0:8, 1, :],
                    start=False, stop=False)
                for hp in range(4):
                    P.i("pe", "matmul", ["Qbd"], [("pb", 1)], S2[0:8, hp * 16:(hp + 1) * 16],
                        lhsT=KTs_s[:, hp, sc], rhs=Qbd[:, 1, hp, :], start=False, stop=(hp == 3))
                P.i("act", "activation", [("pb", 1)], ["An"], out=An[:, :], in_=S2[0:8, 0:64], func=AF.Exp)
                P.i("pe", "matmul", ["An", ("Vnq", s)], [("pb", 4)], Os[0:64, :], lhsT=An[0:8, :], rhs=Vnq[0:8, s, 1, :],
                    start=False, stop=False)
                for g in range(15, -1, -1):
                    slot = gcount[0] % NSL
                    gcount[0] += 1
                    NC_ = 256
                    gather(Ksb[slot][:, :, :], csk, s, g, ["idx"], [("Ksb", slot)], ("Ksb", slot))
                    gather(Vab[slot][:, :, :], cmv, s, g, ["idx"], [("Vab", slot)], ("Vab", slot))
                    gather(Vsb[slot][:, :, :], csv, s, g, ["idx"], [("Vsb", slot)], ("Vsb", slot))
                    ev = 0
                    for a in range(2):
                        for pp in range(0, 4, 2):
                            bank = (a * 2 + pp // 2) % 2
                            pT = pt16[bank]
                            for q in range(2):
                                pl = pp + q
                                src = Ka_seq[:, g * 4 + pl, :] if a == 0 else Ksb[slot][:, pl, :]
                                rk = ("Ka", g // 2) if a == 0 else ("Ksb", slot)
                                for hp in range(4):
                                    P.i("pe", "transpose", [rk], [("pt16", bank)],
                                        out=pT[:, q * 512 + hp * 128:q * 512 + (hp + 1) * 128],
                                        in_=src[:, hp * 128:(hp + 1) * 128], identity=ident)
                            dst = KTg[a][slot][:, pp:pp + 2, :, :].rearrange("p a h t -> p (a h t)")
                            if ev % 2 == 0:
                                P.i("act", "activation", [("pt16", bank)], [("KTg", a, slot, pp)], out=dst, in_=pT[:, :], func=AF.Copy)
                            else:
                                P.i("dve", "tensor_copy", [("pt16", bank)], [("KTg", a, slot, pp)], out=dst, in_=pT[:, :])
                            ev += 1
                    P.i("pe", "matmul", ["MTs2", "egm"], [("pb", 2)], Sm[:, 0:NC_], lhsT=egm[:, g, :],
                        rhs=MTs2[:, :].unsqueeze(1).to_broadcast([32, 4, 64]), start=True, stop=False)
                    for pl in range(4):
                        for hp in range(4):
                            P.i("pe", "matmul", [("KTg", 0, slot, (pl // 2) * 2), "Qbd"], [("pb", 2)],
                                Sm[:, pl * 64 + hp * 16:pl * 64 + (hp + 1) * 16],
                                lhsT=KTg[0][slot][:, pl, hp, :], rhs=Qbd[:, 0, hp, :], start=False, stop=False)
                    PM = Pm[slot]
                    P.i("act", "activation", [("pb", 2)], [("Pm", slot)], out=PM[:, :, :].rearrange("p a c -> p (a c)"),
                        in_=Sm[:, 0:NC_], func=AF.Exp)
                    for pl in range(4):
                        P.i("pe", "matmul", [("Pm", slot), ("Vab", slot)], [("pb", 3)], Oa[0:64, :], lhsT=PM[:, pl, :],
                            rhs=Vab[slot][:, pl, :], start=False, stop=False)
                        P.i("pe", "matmul", [("Pm", slot)], [("pb", 5)], Dn[0:64, 0:1], lhsT=PM[:, pl, :],
                            rhs=ones[:, 0:1], start=False, stop=False)
                    for pl in range(4):
                        for hp in range(4):
                            P.i("pe", "matmul", [("KTg", 1, slot, (pl // 2) * 2), "Qbd"], [("pb", 0)],
                                S1[:, pl * 64 + hp * 16:pl * 64 + (hp + 1) * 16],
                                lhsT=KTg[1][slot][:, pl, hp, :], rhs=Qbd[:, 1, hp, :], start=True, stop=True)
                    EG, SPG, AG, WC = Eg[slot], SPg[slot], Ag[slot], Wc[slot]
                    P.i("act", "activation", [("pb", 0)], [("Eg", slot)], out=EG[:, :], in_=S1[:, 0:NC_], func=AF.Exp)
                    P.i("act", "activation", [("Eg", slot)], [("SPg", slot)], out=SPG[:, :, :].rearrange("p a c -> p (a c)"),
                        in_=EG[:, :], func=AF.Ln, bias=1.0)
                    P.i("dve", "tensor_copy", [("SPg", slot)], [("Wc", slot)], out=WC[:, 3, :], in_=SPG[:, 3, :])
                    for pl in range(2, -1, -1):
                        P.i("dve", "tensor_tensor", [("Wc", slot), ("SPg", slot)], [("Wc", slot)], out=WC[:, pl, :],
                            in0=WC[:, pl + 1, :], in1=SPG[:, pl, :], op=ALU.add)
                    P.i("pe", "matmul", [("Wc", slot)], [("pb", 1)], S2[:, 0:NC_], lhsT=negI,
                        rhs=WC[:, :, :].rearrange("p a c -> p (a c)"), start=True, stop=False)
                    P.i("pe", "matmul", [("Wc", slot)], [("pb", 1)], S2[:, 0:NC_], lhsT=Usneg,
                        rhs=WC[:, 0, :].unsqueeze(1).to_broadcast([128, 4, 64]), start=False, stop=False)
                    if g < 15:
                        P.i("pe", "matmul", ["carry"], [("pb", 1)], S2[:, 0:NC_], lhsT=negones,
                            rhs=carry[:, :].unsqueeze(1).to_broadcast([128, 4, 64]), start=False, stop=False)
                    if g > 0:
                        if g == 15:
                            P.i("dve", "tensor_copy", [("Wc", slot)], ["carry"], out=carry[:, :], in_=WC[:, 0, :])
                        else:
                            P.i("dve", "tensor_tensor", [("Wc", slot), "carry"], ["carry"], out=carry[:, :],
                                in0=carry[:, :], in1=WC[:, 0, :], op=ALU.add)
                    P.i("pe", "matmul", ["SPn"], [("pb", 1)], S2[:, 0:NC_], lhsT=negones[0:8, :],
                        rhs=SPn[0:8, :].unsqueeze(1).to_broadcast([8, 4, 64]), start=False, stop=False)
                    for pl in range(4):
                        for hp in range(4):
                            P.i("pe", "matmul", [("KTg", 1, slot, (pl // 2) * 2), "Qbd"], [("pb", 1)],
                                S2[:, pl * 64 + hp * 16:pl * 64 + (hp + 1) * 16],
                                lhsT=KTg[1][slot][:, pl, hp, :], rhs=Qbd[:, 1, hp, :], start=False, stop=False)
                    P.i("act", "activation", [("pb", 1)], [("Ag", slot)], out=AG[:, :, :].rearrange("p a c -> p (a c)"),
                        in_=S2[:, 0:NC_], func=AF.Exp)
                    for pl in range(4):
                        P.i("pe", "matmul", [("Ag", slot), ("Vsb", slot)], [("pb", 4)], Os[0:64, :], lhsT=AG[:, pl, :],
                            rhs=Vsb[slot][:, pl, :], start=False, stop=False)
                P.i("dve", "reciprocal", [("pb", 5)], ["rdn"], out=rdn[:, :], in_=Dn[0:64, 0:1])
                P.i("dve", "tensor_scalar", [("pb", 3), "rdn"], ["oa_sb"], out=oa_sb[:, :], in0=Oa[0:64, :],
                    scalar1=rdn[:, 0:1], scalar2=None, op0=ALU.mult)
                P.i("act", "activation", [("pb", 4)], ["os_sb"], out=os_sb[:, :], in_=Os[0:64, :], func=AF.Copy)
                r0 = SEQ + s * 8
                for h in range(8):
                    P.dma("sp", oa_scr[r0:r0 + 8, h * 64:(h + 1) * 64], oa_sb[h * 8:(h + 1) * 8, h * 64:(h + 1) * 64],
                          reads=["oa_sb"], key=("o", "osc_a"))
                    P.dma("sp", os_scr[r0:r0 + 8, h * 64:(h + 1) * 64], os_sb[h * 8:(h + 1) * 8, h * 64:(h + 1) * 64],
                          reads=["os_sb"], key=("o", "osc_s"))
            P.emit()

        with ExitStack() as s2:
            P = Prog(ctx)
            Wout = sb("Wout", [128, 8, D], BF16, s2)
            Wd = sb("Wd", [128, NFC, D], BF16, s2)
            w_out_v = w_out_b.rearrange("(kc p) n -> p kc n", p=128)
            w_d_v = w_d_b.rearrange("(fc p) n -> p fc n", p=128)
            for kc in range(0, 8, 2):
                P.dma("sp", Wout[:, kc:kc + 2, :], w_out_v[:, kc:kc + 2, :], writes=[("Wout", kc)])
            for fc in range(0, NFC, 2):
                P.dma("sp", Wd[:, fc:fc + 2, :], w_d_v[:, fc:fc + 2, :], writes=[("Wd", fc)])
            w_bm_v = w_bm_b.rearrange("(kc p) n -> p kc n", p=128)
            w_bs_v = w_bs_b.rearrange("(kc p) n -> p kc n", p=128)
            w_in_v = w_in_b.rearrange("(kc p) n -> p kc n", p=128)
            w_g_v = w_g_b.rearrange("(kc p) n -> p kc n", p=128)
            w_u_v = w_u_b.rearrange("(kc p) n -> p kc n", p=128)
            Wbr = [sb("Wbr%d" % i, [128, 2, 4, 128], BF16, s2) for i in range(2)]
            Wgt = [sb("Wgt%d" % i, [128, 2, 8, 128], BF16, s2) for i in range(2)]
            Wgu = [sb("Wgu%d" % i, [128, 2, 8, 128], BF16, s2) for i in range(4)]
            hTg = sb("hTg", [128, 8, 512], BF16, s2)
            otg = sb("otg", [128, 2, 4, 512], BF16, s2)
            oT = sb("oT", [128, 2, 4, 512], BF16, s2)
            mergedT = sb("mergedT", [128, 8, 512], BF16, s2)
            x1 = sb("x1", [128, 4, D], F32, s2)
            xin = sb("xin", [128, D], F32, s2)
            h2 = [sb("h2_%d" % i, [128, D], BF16, s2) for i in range(2)]
            h2T = sb("h2T", [128, 8, 512], BF16, s2)
            ffT = sb("ffT", [128, NFC, 512], BF16, s2)
            ea = sb("ea", [128, 512], F32, s2)
            ebb = sb("ebb", [128, 512], F32, s2)
            ma = sb("ma", [128, 512], F32, s2)
            ssd = sb("ssd", [128, 8], F32, s2)
            sqd = sb("sqd", [128, D], BF16, s2)

            NG = 5
            for g in range(NG):
                if g < 4:
                    t0, ntile, ntok, r0 = g * 4, 4, 512, g * 512
                    tiles = [(i, 128) for i in range(4)]
                else:
                    t0, ntile, ntok, r0 = NT, 1, ST, SEQ
                    tiles = [(0, ST)]
                gk = ("g", g)
                P.dma("sp", hTg[:, :, 0:ntok], hT_scr[:, :, r0:r0 + ntok], writes=["hTg"])
                for a, scr in ((0, oa_scr), (1, os_scr)):
                    for (i, nr) in tiles:
                        P.dma("sp", otg[0:nr, a, i, :], scr[r0 + i * 128:r0 + i * 128 + nr, :], writes=[("otg", a, i)])
                for a in range(2):
                    for (i, nr) in tiles:
                        pT = pt16[(a * 4 + i) % 2]
                        pk = ("pt16", (a * 4 + i) % 2)
                        for kc in range(4):
                            P.op("pe", lambda e, pT=pT, a=a, i=i, kc=kc, nr=nr: e.transpose(
                                out=pT[:, kc * 128:kc * 128 + nr], in_=otg[0:nr, a, i, kc * 128:(kc + 1) * 128],
                                identity=ident[0:nr, 0:nr]),
                                reads=[("otg", a, i), "consts"], writes=[pk])
                        P.op("act", lambda e, pT=pT, a=a, i=i, nr=nr: e.activation(
                            out=oT[:, a, :, i * 128:i * 128 + nr],
                            in_=pT[:, 0:512].rearrange("p (k t) -> p k t", t=128)[:, :, 0:nr], func=AF.Copy),
                            reads=[pk], writes=[("oT", a, i)])
                oT_reads = [("oT", a, i) for a in range(2) for (i, _) in tiles]
                for dc in range(8):
                    ws = (g * 8 + dc) % 2
                    WB, WG = Wbr[ws], Wgt[ws]
                    P.dma("sp", WB[:, 0, :, :], w_bm_v[:, :, dc * 128:(dc + 1) * 128], writes=[("Wbr", ws, 0)])
                    P.dma("sp", WB[:, 1, :, :], w_bs_v[:, :, dc * 128:(dc + 1) * 128], writes=[("Wbr", ws, 1)])
                    P.dma("sp", WG[:, 0, :, :], w_in_v[:, :, 3072 + dc * 128:3072 + (dc + 1) * 128], writes=[("Wgt", ws, 0)])
                    P.dma("sp", WG[:, 1, :, :], w_in_v[:, :, 4096 + dc * 128:4096 + (dc + 1) * 128], writes=[("Wgt", ws, 1)])
                    Ba, Bs, Ga, Gs = pb[0], pb[1], pb[2], pb[3]
                    for a, Bx in ((0, Ba), (1, Bs)):
                        for kc in range(4):
                            P.op("pe", lambda e, Bx=Bx, WB=WB, a=a, kc=kc: e.matmul(
                                Bx[:, 0:ntok], lhsT=WB[:, a, kc, :], rhs=oT[:, a, kc, 0:ntok], start=(kc == 0), stop=(kc == 3)),
                                reads=[("Wbr", ws, a)] + oT_reads, writes=[("pb", a)])
                    for a, Gx in ((0, Ga), (1, Gs)):
                        for kc in range(8):
                            P.op("pe", lambda e, Gx=Gx, WG=WG, a=a, kc=kc: e.matmul(
                                Gx[:, 0:ntok], lhsT=WG[:, a, kc, :], rhs=hTg[:, kc, 0:ntok], start=(kc == 0), stop=(kc == 7)),
                                reads=[("Wgt", ws, a), "hTg"], writes=[("pb", 2 + a)])
                    P.op("act", lambda e, Ga=Ga: e.activation(out=ea[:, 0:ntok], in_=Ga[:, 0:ntok], func=AF.Exp, scale=-1.0),
                         reads=[("pb", 2)], writes=["ea"])
                    P.op("act", lambda e, Gs=Gs: e.activation(out=ebb[:, 0:ntok], in_=Gs[:, 0:ntok], func=AF.Exp, scale=-1.0),
                         reads=[("pb", 3)], writes=["ebb"])
                    for nm_, t_ in (("ea", ea), ("ebb", ebb)):
                        P.i("act", "activation", [nm_], [nm_], out=t_[:, 0:ntok], in_=t_[:, 0:ntok], func=AF.Ln, bias=1.0)
                        P.i("act", "activation", [nm_], [nm_], out=t_[:, 0:ntok], in_=t_[:, 0:ntok], func=AF.Exp, scale=-1.0)
                    P.op("dve", lambda e, Ba=Ba: e.tensor_tensor(out=ma[:, 0:ntok], in0=Ba[:, 0:ntok], in1=ea[:, 0:ntok], op=ALU.mult),
                         reads=[("pb", 0), "ea"], writes=["ma"])
                    P.op("dve", lambda e, Bs=Bs: e.tensor_tensor(out=ebb[:, 0:ntok], in0=Bs[:, 0:ntok], in1=ebb[:, 0:ntok], op=ALU.mult),
                         reads=[("pb", 1), "ebb"], writes=["ebb"])
                    P.op("dve", lambda e, dc=dc: e.tensor_tensor(out=mergedT[:, dc, 0:ntok], in0=ma[:, 0:ntok], in1=ebb[:, 0:ntok], op=ALU.add),
                         reads=["ma", "ebb"], writes=[("mergedT", dc)])
                mreads = [("mergedT", dc) for dc in range(8)]
                for (i, nr) in tiles:
                    P.dma("sp", xin[0:nr, :], (xp[r0 + i * 128:r0 + i * 128 + nr, :] if g < 4 else xs[:, :]), writes=["xin"])
                    for half in range(2):
                        acc = pb[4 + half]
                        for dc in range(8):
                            P.op("pe", lambda e, acc=acc, dc=dc, i=i, nr=nr, half=half: e.matmul(
                                acc[0:nr, :], lhsT=mergedT[:, dc, i * 128:i * 128 + nr], rhs=Wout[:, dc, half * 512:(half + 1) * 512],
                                start=(dc == 0), stop=(dc == 7)),
                                reads=mreads + [("Wout", (dc // 2) * 2)], writes=[("pb", 4 + half)])
                        P.op("dve", lambda e, acc=acc, i=i, nr=nr, half=half: e.tensor_tensor(
                            out=x1[0:nr, i, half * 512:(half + 1) * 512], in0=acc[0:nr, :], in1=xin[0:nr, half * 512:(half + 1) * 512], op=ALU.add),
                            reads=[("pb", 4 + half), "xin"], writes=[("x1", i, half)])
                    x1r = [("x1", i, 0), ("x1", i, 1)]
                    P.op("act", lambda e, i=i, nr=nr: e.activation(out=sqd[0:nr, :], in_=x1[0:nr, i, :], func=AF.Square, accum_out=ssd[0:nr, 0:1]),
                         reads=x1r, writes=["sqd", "ssd"])
                    P.op("act", lambda e, nr=nr: e.activation(out=ssd[0:nr, 1:2], in_=ssd[0:nr, 0:1], func=AF.Ln, scale=1.0 / D, bias=EPS),
                         reads=["ssd"], writes=["ssd"])
                    P.op("act", lambda e, nr=nr: e.activation(out=ssd[0:nr, 2:3], in_=ssd[0:nr, 1:2], func=AF.Exp, scale=-0.5),
                         reads=["ssd"], writes=["ssd"])
                    hs = i % 2
                    H2 = h2[hs]
                    P.op("dve", lambda e, H2=H2, i=i, nr=nr: e.tensor_scalar(
                        out=H2[0:nr, :], in0=x1[0:nr, i, :], scalar1=ssd[0:nr, 2:3], scalar2=None, op0=ALU.mult),
                        reads=x1r + ["ssd"], writes=[("h2", hs)])
                    pT = pt16[hs]
                    for kc in range(8):
                        P.op("pe", lambda e, pT=pT, H2=H2, kc=kc, nr=nr: e.transpose(
                            out=pT[:, kc * 128:kc * 128 + nr], in_=H2[0:nr, kc * 128:(kc + 1) * 128], identity=ident[0:nr, 0:nr]),
                            reads=[("h2", hs), "consts"], writes=[("pt16", hs)])
                    P.op("dve", lambda e, pT=pT, i=i, nr=nr: e.tensor_tensor(
                        out=h2T[:, :, i * 128:i * 128 + nr],
                        in0=pT[:, :].rearrange("p (k t) -> p k t", t=128)[:, :, 0:nr],
                        in1=gffn_sb[:, :].unsqueeze(2).to_broadcast([128, 8, nr]), op=ALU.mult),
                        reads=[("pt16", hs), "gffn"], writes=[("h2T", i)])
                h2r = [("h2T", i) for (i, _) in tiles]
                for fc in range(NFC):
                    ws = (g * NFC + fc) % 4
                    WGU = Wgu[ws]
                    P.dma("sp", WGU[:, 0, :, :], w_g_v[:, :, fc * 128:(fc + 1) * 128], writes=[("Wgu", ws, 0)])
                    P.dma("sp", WGU[:, 1, :, :], w_u_v[:, :, fc * 128:(fc + 1) * 128], writes=[("Wgu", ws, 1)])
                    pg, pu = pb[(fc % 2) * 2], pb[(fc % 2) * 2 + 1]
                    kg, ku = ("pb", (fc % 2) * 2), ("pb", (fc % 2) * 2 + 1)
                    for a, px, kx in ((0, pg, kg), (1, pu, ku)):
                        for kc in range(8):
                            P.op("pe", lambda e, px=px, WGU=WGU, a=a, kc=kc: e.matmul(
                                px[:, 0:ntok], lhsT=WGU[:, a, kc, :], rhs=h2T[:, kc, 0:ntok], start=(kc == 0), stop=(kc == 7)),
                                reads=[("Wgu", ws, a)] + h2r, writes=[kx])
                    P.op("act", lambda e, pg=pg: e.activation(out=ea[:, 0:ntok], in_=pg[:, 0:ntok], func=AF.Exp, scale=-1.0),
                         reads=[kg], writes=["ea"])
                    P.i("act", "activation", ["ea"], ["ea"], out=ea[:, 0:ntok], in_=ea[:, 0:ntok], func=AF.Ln, bias=1.0)
                    P.i("act", "activation", ["ea"], ["ea"], out=ea[:, 0:ntok], in_=ea[:, 0:ntok], func=AF.Exp, scale=-1.0)
                    P.op("dve", lambda e, pg=pg: e.tensor_tensor(out=ma[:, 0:ntok], in0=pg[:, 0:ntok], in1=ea[:, 0:ntok], op=ALU.mult),
                         reads=[kg, "ea"], writes=["ma"])
                    P.op("dve", lambda e, pu=pu, fc=fc: e.tensor_tensor(out=ffT[:, fc, 0:ntok], in0=pu[:, 0:ntok], in1=ma[:, 0:ntok], op=ALU.mult),
                         reads=[ku, "ma"], writes=[("ffT", fc)])
                ffr = [("ffT", fc) for fc in range(NFC)]
                for (i, nr) in tiles:
                    for half in range(2):
                        acc = pb[4 + half]
                        for fc in range(NFC):
                            P.op("pe", lambda e, acc=acc, fc=fc, i=i, nr=nr, half=half: e.matmul(
                                acc[0:nr, :], lhsT=ffT[:, fc, i * 128:i * 128 + nr], rhs=Wd[:, fc, half * 512:(half + 1) * 512],
                                start=(fc == 0), stop=(fc == NFC - 1)),
                                reads=ffr + [("Wd", (fc // 2) * 2)], writes=[("pb", 4 + half)])
                        P.op("dve", lambda e, acc=acc, i=i, nr=nr, half=half: e.tensor_tensor(
                            out=x1[0:nr, i, half * 512:(half + 1) * 512], in0=acc[0:nr, :], in1=x1[0:nr, i, half * 512:(half + 1) * 512], op=ALU.add),
                            reads=[("pb", 4 + half), ("x1", i, half)], writes=[("x1", i, half)])
                    x1r = [("x1", i, 0), ("x1", i, 1)]
                    P.op("act", lambda e, i=i, nr=nr: e.activation(out=sqd[0:nr, :], in_=x1[0:nr, i, :], func=AF.Square, accum_out=ssd[0:nr, 4:5]),
                         reads=x1r, writes=["sqd", "ssd2"])
                    P.op("act", lambda e, nr=nr: e.activation(out=ssd[0:nr, 5:6], in_=ssd[0:nr, 4:5], func=AF.Ln, scale=1.0 / D, bias=EPS),
                         reads=["ssd2"], writes=["ssd2"])
                    P.op("act", lambda e, nr=nr: e.activation(out=ssd[0:nr, 6:7], in_=ssd[0:nr, 5:6], func=AF.Exp, scale=-0.5),
                         reads=["ssd2"], writes=["ssd2"])
                    P.op("dve", lambda e, i=i, nr=nr: e.scalar_tensor_tensor(
                        out=x1[0:nr, i, :], in0=x1[0:nr, i, :], scalar=ssd[0:nr, 6:7], in1=gfin_sb[0:nr, :], op0=ALU.mult, op1=ALU.mult),
                        reads=x1r + ["ssd2", "gfin"], writes=x1r)
                    ydst = yp[r0 + i * 128:r0 + i * 128 + nr, :] if g < 4 else ys[:, :]
                    P.dma("pool", ydst, x1[0:nr, i, :], reads=x1r, key=("o", "y", i))
            P.emit()
    return nc


_NC_CACHE = {}


def _consts():
    k = np.arange(128)[:, None]
    q = np.arange(128)[None, :]
    c = np.zeros((128, 8, 128), np.float32)
    c[:, 0, :] = np.eye(128)
    c[:, 1, :] = np.where(k <= q, 0.0, NEG)
    c[:, 2, :] = np.where(k < q, 0.0, NEG)
    c[:, 3, :] = np.where(k >= q, -1.0, 0.0)
    c[:, 4, :] = -1.0
    c[:, 5, :] = 1.0
    c[:, 6, :] = -np.eye(128)
    c[:, 7, :] = np.where(k > q, -1.0, 0.0)
    eb = np.zeros((8, 8, 128), np.float32)
    for n in range(8):
        eb[n, n, :] = 1.0
    kk = np.arange(8)[:, None, None]
    qq = np.arange(8)[None, None, :]
    cm8 = np.zeros((8, 2, 8, 8), np.float32)
    cm8[:, 0] = np.where(kk <= qq, 0.0, NEG)
    cm8[:, 1] = np.where(kk < qq, 0.0, NEG)
    cm8 = cm8.reshape(8, 2, 64)
    pp = np.arange(128)
    oh4 = (pp[:, None] // 32 == np.arange(4)[None, :]).astype(np.float32)
    pmod = (pp % 32).astype(np.float32).reshape(128, 1)
    blk = 2 * np.arange(16)[None, :] + pp[:, None] // 64
    zsel2 = (blk[:, :, None] == np.arange(32)[None, None, :]).astype(np.float32) / 256.0
    eg = (np.arange(32)[:, None, None] == blk.T[None, :, :]).astype(np.float32)
    bf = ml_dtypes.bfloat16
    return c.astype(bf), eb.astype(bf), cm8.astype(bf), oh4, pmod, zsel2.astype(bf), eg.astype(bf)


def _rope_table(pos):
    half = 32
    inv = (10000.0 ** (-np.arange(half, dtype=np.float32) / half)).astype(np.float32)
    ang = pos.astype(np.float32)[:, None] * inv[None, :]
    return np.concatenate([np.cos(ang), np.sin(ang)], axis=1).astype(np.float32)


def kernel(x_prompt, x_sample, cache_moba_k, cache_moba_v, cache_sb_k, cache_sb_v,
           page_table, g_mix, w_in, w_branch_moba, w_branch_sb, w_out,
           g_ffn, w_ffn_gate, w_ffn_up, w_ffn_down, g_final):
    f = lambda a: np.ascontiguousarray(np.asarray(a))
    if "nc" not in _NC_CACHE:
        _NC_CACHE["nc"] = build_nc()
    nc = _NC_CACHE["nc"]
    cbf, ebk, cm8, oh4, pmod, zsel2, eg = _consts()
    past_len = page_table.shape[1] * 128
    pos = np.concatenate([np.arange(SEQ), np.tile(past_len + np.arange(8), 4)])
    cs_all = _rope_table(pos)
    caches = [f(c).reshape(NPOOL * 128, 512) for c in (cache_moba_k, cache_moba_v, cache_sb_k, cache_sb_v)]
    shared = {
        "cmk": caches[0], "cmv": caches[1], "csk": caches[2], "csv": caches[3],
        "w_in": f(w_in)[0], "w_bm": f(w_branch_moba)[0], "w_bs": f(w_branch_sb)[0], "w_out": f(w_out)[0],
        "w_g": f(w_ffn_gate)[0], "w_u": f(w_ffn_up)[0], "w_d": f(w_ffn_down)[0],
        "gmixT": f(f(g_mix)[0].reshape(8, 128).T), "gffnT": f(f(g_ffn)[0].reshape(8, 128).T),
        "gfin": f(g_final).reshape(1, D), "cs_all": cs_all, "cbf": cbf, "ebk": ebk,
        "cm8": cm8, "oh4": oh4, "pmod": pmod, "zsel2": zsel2, "eg": eg,
    }
    xpn, xsn, ptn = f(x_prompt), f(x_sample), f(page_table).astype(np.int32)
    in_maps = []
    for c in range(NCORES):
        m = dict(shared)
        m["xp"] = xpn[c]
        m["xs"] = xsn[4 * c:4 * c + 4].reshape(ST, D)
        m["pt"] = ptn[4 * c:4 * c + 4]
        in_maps.append(m)
    res = run_bass_kernel_spmd(nc, in_maps, core_ids=list(range(NCORES)))
    R = res.results
    y_prompt = np.stack([R[c]["yp"] for c in range(NCORES)]).astype(np.float32)
    y_sample = np.concatenate([R[c]["ys"].reshape(4, 8, D) for c in range(NCORES)]).astype(np.float32)
    outs = [y_prompt, y_sample]
    for nm in ("mk", "mv", "sk", "sv"):
        outs.append(np.stack([R[c][nm + "p"].reshape(SEQ, 8, 64) for c in range(NCORES)])[None].astype(np.float32))
    for nm in ("mk", "mv", "sk", "sv"):
        outs.append(np.concatenate([R[c][nm + "s"].reshape(4, 8, 8, 64) for c in range(NCORES)])[None].astype(np.float32))
    return tuple(outs)
```

```python
import numpy as np
import ml_dtypes
from contextlib import ExitStack
import concourse.bass as bass
import concourse.mybir as mybir
from concourse.bass_utils import run_bass_kernel_spmd

F32 = mybir.dt.float32
BF16 = mybir.dt.bfloat16
I32 = mybir.dt.int32
AF = mybir.ActivationFunctionType
ALU = mybir.AluOpType
AX = mybir.AxisListType

ENGS = ("pe", "act", "dve", "pool", "sp")
NCORES = 8
D = 1024
SEQ = 2048
NT = 16
ST = 32
NTOK = SEQ + ST
DFF = 2816
NFC = 22
NPOOL = 2560
NPAGES = 64
NEG = -30000.0
EPS = 1e-6


class Op:
    __slots__ = ("eng", "fn", "reads", "writes", "dma", "key", "deps", "needed", "token", "idx")

    def __init__(self, eng, fn, reads, writes, dma, key):
        self.eng, self.fn, self.reads, self.writes, self.dma, self.key = eng, fn, reads, writes, dma, key
        self.deps = []
        self.needed = False
        self.token = None


class _Rec:
    call = None

    def __getattr__(self, name):
        def f(*a, **k):
            self.call = (name, a, k)
            return None
        return f


class Ctx:
    def __init__(self, nc, es, n_dma_sems=40):
        self.nc = nc
        self.esem = {e: es.enter_context(nc.semaphore("e_" + e)) for e in ENGS if e != "sp"}
        self.cnt = {e: 0 for e in ENGS}
        self.dpool = [es.enter_context(nc.semaphore("d%d" % i)) for i in range(n_dma_sems)]
        self.dcnt = [0] * n_dma_sems
        self.out_waits = {}


class Prog:
    def __init__(self, ctx, paranoid=True):
        self.ctx = ctx
        self.ops = []
        self.paranoid = paranoid

    def op(self, eng, fn, reads=(), writes=(), dma=False, key=None):
        rec = _Rec()
        fn(rec)
        name, a, k = rec.call
        fn = lambda e: getattr(e, name)(*a, **k)
        o = Op(eng, fn, tuple(reads), tuple(writes), dma, key)
        o.idx = len(self.ops)
        self.ops.append(o)
        return o

    def i(self, eng, name, reads, writes, *args, **kwargs):
        return self.op(eng, lambda e: getattr(e, name)(*args, **kwargs), reads, writes)

    def dma(self, q, out, in_, reads=(), writes=(), key=None):
        return self.op(q, lambda e: e.dma_start(out=out, in_=in_), reads, writes, dma=True, key=key)

    def analyze(self):
        last_w, readers = {}, {}
        for o in self.ops:
            deps = {}
            for r in o.reads:
                w = last_w.get(r)
                if w is not None:
                    deps[w.idx] = (w, "raw")
            for r in o.writes:
                w = last_w.get(r)
                if w is not None and w.idx not in deps:
                    if not (w.dma and o.dma and w.key is not None and w.key == o.key):
                        deps[w.idx] = (w, "waw")
                for rd in readers.get(r, ()):
                    if rd.idx not in deps:
                        deps[rd.idx] = (rd, "war")
            for r in o.reads:
                readers.setdefault(r, []).append(o)
            for r in o.writes:
                last_w[r] = o
                readers[r] = []
            out = []
            for d, kind in deps.values():
                if d is o:
                    continue
                if (not d.dma) and d.eng == o.eng:
                    if d.eng in ("pe", "sp"):
                        continue
                    if kind == "war" or not self.paranoid:
                        continue
                out.append(d)
                d.needed = True
            o.deps = out
        for e in ENGS:
            if e == "sp":
                continue
            for o in reversed(self.ops):
                if o.eng == e and not o.dma:
                    o.needed = True
                    break

    def emit(self):
        ctx = self.ctx
        nc = ctx.nc
        self.analyze()
        dkey = {}
        for o in self.ops:
            if o.dma:
                k = o.key if o.key is not None else (o.writes[0] if o.writes else ("dma", o.idx))
                if k not in dkey:
                    dkey[k] = len(dkey)
                    assert len(dkey) <= len(ctx.dpool), "too many DMA semaphores"
                i = dkey[k]
                ctx.dcnt[i] += 16
                o.token = (ctx.dpool[i], ctx.dcnt[i])
            elif o.needed:
                ctx.cnt[o.eng] += 1
                o.token = (ctx.esem[o.eng], ctx.cnt[o.eng])
        per_eng = {e: [o for o in self.ops if o.eng == e] for e in ENGS}
        fin_e = dict(ctx.cnt)
        fin_d = list(ctx.dcnt)

        def run(engobj, ename):
            know = {}
            for o in per_eng[ename]:
                for d in o.deps:
                    s, v = d.token
                    if know.get(id(s), 0) < v:
                        engobj.wait_ge(s, v)
                        know[id(s)] = v
                ins = o.fn(engobj)
                if o.token is not None:
                    ins.then_inc(o.token[0], 16 if o.dma else 1)
            for e2 in ENGS:
                if e2 == "sp" or e2 == ename:
                    continue
                if fin_e[e2] > 0 and know.get(id(ctx.esem[e2]), 0) < fin_e[e2]:
                    engobj.wait_ge(ctx.esem[e2], fin_e[e2])
            for i in range(len(dkey)):
                if fin_d[i] > 0 and know.get(id(ctx.dpool[i]), 0) < fin_d[i]:
                    engobj.wait_ge(ctx.dpool[i], fin_d[i])

        with nc.Block() as block:
            @block.tensor
            def _(e):
                run(e, "pe")

            @block.scalar
            def _(e):
                run(e, "act")

            @block.vector
            def _(e):
                run(e, "dve")

            @block.gpsimd
            def _(e):
                run(e, "pool")

            @block.sync
            def _(e):
                run(e, "sp")


def build_nc():
    nc = bass.Bass("TRN2", target_bir_lowering=False)

    def din(name, shape, dt=F32):
        return nc.dram_tensor(name, list(shape), dt, kind="ExternalInput").ap()

    def dout(name, shape, dt=F32):
        return nc.dram_tensor(name, list(shape), dt, kind="ExternalOutput").ap()

    xp = din("xp", [SEQ, D])
    xs = din("xs", [ST, D])
    cmk = din("cmk", [NPOOL * 128, 512])
    cmv = din("cmv", [NPOOL * 128, 512])
    csk = din("csk", [NPOOL * 128, 512])
    csv = din("csv", [NPOOL * 128, 512])
    ptab = din("pt", [4, NPAGES], I32)
    w_in = din("w_in", [D, 5120])
    w_bm = din("w_bm", [512, D])
    w_bs = din("w_bs", [512, D])
    w_out = din("w_out", [D, D])
    w_g = din("w_g", [D, DFF])
    w_u = din("w_u", [D, DFF])
    w_d = din("w_d", [DFF, D])
    gmixT = din("gmixT", [128, 8])
    gffnT = din("gffnT", [128, 8])
    gfin = din("gfin", [1, D])
    cs_all = din("cs_all", [NTOK, 64])
    cbf = din("cbf", [128, 8, 128], BF16)
    ebk = din("ebk", [8, 8, 128], BF16)
    cm8_d = din("cm8", [8, 2, 64], BF16)
    oh4_d = din("oh4", [128, 4])
    pmod_d = din("pmod", [128, 1])
    zsel_d = din("zsel2", [128, 16, 32], BF16)
    eg_d = din("eg", [32, 16, 128], BF16)

    yp = dout("yp", [SEQ, D])
    ys = dout("ys", [ST, D])
    o_kv = {}
    for nm in ("mk", "mv", "sk", "sv"):
        o_kv[nm] = (dout(nm + "p", [SEQ, 512]), dout(nm + "s", [ST, 512]))

    def wscr(name, shape):
        return nc.dram_tensor(name, list(shape), BF16, kind="Internal").ap()

    w_in_b = wscr("w_in_b", [D, 5120])
    w_bm_b = wscr("w_bm_b", [512, D])
    w_bs_b = wscr("w_bs_b", [512, D])
    w_out_b = wscr("w_out_b", [D, D])
    w_g_b = wscr("w_g_b", [D, DFF])
    w_u_b = wscr("w_u_b", [D, DFF])
    w_d_b = wscr("w_d_b", [DFF, D])
    hT_scr = nc.dram_tensor("hT_scr", [128, 8, NTOK], BF16, kind="Internal").ap()
    oa_scr = nc.dram_tensor("oa_scr", [NTOK, 512], BF16, kind="Internal").ap()
    os_scr = nc.dram_tensor("os_scr", [NTOK, 512], BF16, kind="Internal").ap()

    def tok_rows(t):
        return (t * 128, 128) if t < NT else (SEQ, ST)

    def x_src(t):
        return xp[t * 128:(t + 1) * 128, :] if t < NT else xs[:, :]

    with ExitStack() as es:
        def sb(name, shape, dt, stack=None):
            return (stack or es).enter_context(nc.sbuf_tensor(name, list(shape), dt))

        def ps(name, shape, dt):
            return es.enter_context(nc.psum_tensor(name, list(shape), dt))

        ctx = Ctx(nc, es, n_dma_sems=48)
        pb = [ps("pb%d" % i, [128, 512], F32) for i in range(6)]
        pt16 = [ps("pt16_%d" % i, [128, 1024], BF16) for i in range(2)]

        consts = sb("consts", [128, 8, 128], BF16)
        ident = consts[:, 0, :]
        mask_incl = consts[:, 1, :]
        mask_strict = consts[:, 2, :]
        Uneg = consts[:, 3, :]
        negones = consts[:, 4, :]
        ones = consts[:, 5, :]
        negI = consts[:, 6, :]
        Usneg = consts[:, 7, :]
        eb = sb("eb", [8, 8, 128], BF16)
        gmix_sb = sb("gmix_sb", [128, 8], F32)
        gffn_sb = sb("gffn_sb", [128, 8], F32)
        gfin_sb = sb("gfin_sb", [128, D], F32)
        zeros16 = sb("zeros16", [128, 512], BF16)
        QTa_s = sb("QTa_s", [128, 4, ST], BF16)
        KTa_s = sb("KTa_s", [128, 4, ST], BF16)
        QTs_s = sb("QTs_s", [128, 4, ST], BF16)
        KTs_s = sb("KTs_s", [128, 4, ST], BF16)
        Vna = sb("Vna", [ST, 512], BF16)
        Vns = sb("Vns", [ST, 512], BF16)

        with ExitStack() as s1:
            hT = sb("hT", [128, 8, NTOK], BF16, s1)
            QTa = sb("QTa", [128, 4, SEQ], BF16, s1)
            KTa = sb("KTa", [128, 4, SEQ], BF16, s1)
            QTs = sb("QTs", [128, 4, SEQ], BF16, s1)
            KTs = sb("KTs", [128, 4, SEQ], BF16, s1)
            Va = sb("Va", [128, NT, 8, 65], BF16, s1)
            Vs = sb("Vs", [128, NT, 512], BF16, s1)

            with ExitStack() as s2:
                P = Prog(ctx)
                P.dma("sp", consts[:], cbf, writes=["consts"])
                P.dma("sp", eb[:], ebk, writes=["eb"])
                P.dma("sp", gmix_sb[:], gmixT, writes=["gmix"])
                P.dma("sp", gffn_sb[:], gffnT, writes=["gffn"])
                P.dma("sp", gfin_sb[:], gfin.partition_broadcast(128).rearrange("p a d -> p (a d)"), writes=["gfin"])
                Wb = [sb("Wb%d" % i, [128, 8, 512], BF16, s2) for i in range(2)]
                w_in_f_v = w_in.rearrange("(kc p) n -> p kc n", p=128)
                for cb0 in range(2):
                    for half in range(2):
                        P.dma("pool", Wb[cb0][:, half * 4:(half + 1) * 4, :],
                              w_in_f_v[:, half * 4:(half + 1) * 4, cb0 * 512:(cb0 + 1) * 512],
                              writes=[("Wb", cb0, half)])
                for nm, src, dst, a in (("w_in", w_in, w_in_b, 4),):
                    P.dma("pool", dst.rearrange("k (a n) -> k a n", a=a), src.rearrange("k (a n) -> k a n", a=a),
                          writes=[("wscr", nm)], key=("wscr", nm))
                P.op("pool", lambda e: e.memset(zeros16[:], 0.0), writes=["zeros16"])
                P.op("pool", lambda e: e.memset(Va[:, :, :, 64:65], 1.0), writes=["Va_ones"])

                xt = [sb("xt%d" % i, [128, D], F32, s2) for i in range(2)]
                sq = sb("sq", [128, D], BF16, s2)
                ssq = [sb("ssq%d" % i, [128, 4], F32, s2) for i in range(2)]
                hb = [sb("hb%d" % i, [128, D], BF16, s2) for i in range(2)]
                cs_sb = sb("cs_sb", [128, NT + 1, 64], F32, s2)
                def load_cs():
                    P.dma("sp", cs_sb[:, 0:NT, :], cs_all[0:SEQ, :].rearrange("(t p) c -> p t c", p=128),
                          writes=[("cs", t) for t in range(NT)], key=("cs", 0))
                    P.dma("sp", cs_sb[0:ST, NT, :], cs_all[SEQ:SEQ + ST, :], writes=[("cs", NT)], key=("cs", 1))

                for t in range(NT + 1):
                    r0, nr = tok_rows(t)
                    b = t % 2
                    X, SS, HB = xt[b], ssq[b], hb[b]
                    if t == 0:
                        P.dma("sp", X[0:nr, :], x_src(t), writes=[("xt", b)])
                    if t + 1 <= NT:
                        r0n, nrn = tok_rows(t + 1)
                        P.dma("sp", xt[(t + 1) % 2][0:nrn, :], x_src(t + 1), writes=[("xt", (t + 1) % 2)])
                    if t == 0:
                        load_cs()
                    P.op("act", lambda e, X=X, SS=SS, nr=nr: e.activation(
                        out=sq[0:nr, :], in_=X[0:nr, :], func=AF.Square, accum_out=SS[0:nr, 0:1]),
                        reads=[("xt", b)], writes=["sq", ("ssq", b)])
                    P.op("act", lambda e, SS=SS, nr=nr: e.activation(
                        out=SS[0:nr, 1:2], in_=SS[0:nr, 0:1], func=AF.Ln, scale=1.0 / D, bias=EPS),
                        reads=[("ssq", b)], writes=[("ssq", b)])
                    P.op("act", lambda e, SS=SS, nr=nr: e.activation(
                        out=SS[0:nr, 2:3], in_=SS[0:nr, 1:2], func=AF.Exp, scale=-0.5),
                        reads=[("ssq", b)], writes=[("ssq", b)])
                    P.op("dve", lambda e, X=X, SS=SS, HB=HB, nr=nr: e.tensor_scalar(
                        out=HB[0:nr, :], in0=X[0:nr, :], scalar1=SS[0:nr, 2:3], scalar2=None, op0=ALU.mult),
                        reads=[("xt", b), ("ssq", b)], writes=[("hb", b)])
                    pT = pt16[b]
                    for kc in range(8):
                        P.op("pe", lambda e, pT=pT, HB=HB, kc=kc, nr=nr: e.transpose(
                            out=pT[:, kc * 128:kc * 128 + nr], in_=HB[0:nr, kc * 128:(kc + 1) * 128],
                            identity=ident[0:nr, 0:nr]),
                            reads=[("hb", b), "consts"], writes=[("pt16", b)])
                    P.op("dve", lambda e, pT=pT, r0=r0, nr=nr: e.tensor_tensor(
                        out=hT[:, :, r0:r0 + nr],
                        in0=pT[:, :].rearrange("p (k t) -> p k t", t=128)[:, :, 0:nr],
                        in1=gmix_sb[:, :].unsqueeze(2).to_broadcast([128, 8, nr]), op=ALU.mult),
                        reads=[("pt16", b), "gmix"], writes=[("hT", t)])
                    P.dma("sp", hT_scr[:, :, r0:r0 + nr], hT[:, :, r0:r0 + nr], reads=[("hT", t)], key=("o", "hTs"))

                raw = [sb("raw%d" % i, [128, 512], F32, s2) for i in range(2)]
                stg = [sb("stg%d" % i, [128, 512], F32, s2) for i in range(3)]
                tmpv = sb("tmpv", [128, 256], F32, s2)
                tmpg = sb("tmpg", [128, 256], F32, s2)
                qb = [sb("qb%d" % i, [128, 512], BF16, s2) for i in range(2)]
                w_in_v = w_in_b.rearrange("(kc p) n -> p kc n", p=128)
                it = 0
                def load_wb(cb):
                    for half in range(2):
                        P.dma("sp", Wb[cb % 2][:, half * 4:(half + 1) * 4, :],
                              w_in_v[:, half * 4:(half + 1) * 4, cb * 512:(cb + 1) * 512],
                              reads=[("wscr", "w_in")], writes=[("Wb", cb % 2, half)])

                for cb in range(6):
                    wslot = cb % 2
                    W = Wb[wslot]
                    if 2 <= cb + 1 < 6:
                        load_wb(cb + 1)
                    kind = ("qa", "ka", "va", "qs", "ks", "vs")[cb]
                    for t in range(NT + 1):
                        r0, nr = tok_rows(t)
                        acc = pb[it % 2]
                        akey = ("pb", it % 2)
                        for kc in range(8):
                            P.op("pe", lambda e, acc=acc, W=W, kc=kc, r0=r0, nr=nr: e.matmul(
                                acc[0:nr, :], lhsT=hT[:, kc, r0:r0 + nr], rhs=W[:, kc, :],
                                start=(kc == 0), stop=(kc == 7)),
                                reads=[("hT", t), ("Wb", wslot, kc // 4)], writes=[akey])
                        rslot = it % 2
                        R = raw[rslot]
                        sslot = it % 3
                        S = stg[sslot]
                        qslot = it % 2
                        Q = qb[qslot]
                        it += 1
                        if kind in ("qa", "ka"):
                            P.op("act", lambda e, R=R, acc=acc, nr=nr: e.activation(out=R[0:nr, :], in_=acc[0:nr, :], func=AF.Copy),
                                 reads=[akey], writes=[("raw", rslot)])
                            Rv = R[:, :].rearrange("p (h t d) -> p h t d", t=2, d=32)
                            Sv = S[:, :].rearrange("p (h t d) -> p h t d", t=2, d=32)
                            cosb = cs_sb[0:nr, t, 0:32].unsqueeze(1).to_broadcast([nr, 8, 32])
                            sinb = cs_sb[0:nr, t, 32:64].unsqueeze(1).to_broadcast([nr, 8, 32])
                            tv = tmpv[:, :].rearrange("p (h d) -> p h d", d=32)
                            tg = tmpg[:, :].rearrange("p (h d) -> p h d", d=32)
                            P.op("dve", lambda e, Sv=Sv, Rv=Rv, cosb=cosb, nr=nr: e.tensor_tensor(
                                out=Sv[0:nr, :, 0, :], in0=Rv[0:nr, :, 0, :], in1=cosb, op=ALU.mult),
                                reads=[("raw", rslot), ("cs", t)], writes=[("stg", sslot, 0)])
                            P.op("dve", lambda e, tv=tv, Rv=Rv, sinb=sinb, nr=nr: e.tensor_tensor(
                                out=tv[0:nr], in0=Rv[0:nr, :, 1, :], in1=sinb, op=ALU.mult),
                                reads=[("raw", rslot), ("cs", t)], writes=["tmpv"])
                            P.op("dve", lambda e, Sv=Sv, tv=tv, nr=nr: e.tensor_tensor(
                                out=Sv[0:nr, :, 0, :], in0=Sv[0:nr, :, 0, :], in1=tv[0:nr], op=ALU.subtract),
                                reads=["tmpv", ("stg", sslot, 0)], writes=[("stg", sslot, 0)])
                            P.op("pool", lambda e, Sv=Sv, Rv=Rv, cosb=cosb, nr=nr: e.tensor_tensor(
                                out=Sv[0:nr, :, 1, :], in0=Rv[0:nr, :, 1, :], in1=cosb, op=ALU.mult),
                                reads=[("raw", rslot), ("cs", t)], writes=[("stg", sslot, 1)])
                            P.op("pool", lambda e, tg=tg, Rv=Rv, sinb=sinb, nr=nr: e.tensor_tensor(
                                out=tg[0:nr], in0=Rv[0:nr, :, 0, :], in1=sinb, op=ALU.mult),
                                reads=[("raw", rslot), ("cs", t)], writes=["tmpg"])
                            P.op("pool", lambda e, Sv=Sv, tg=tg, nr=nr: e.tensor_tensor(
                                out=Sv[0:nr, :, 1, :], in0=Sv[0:nr, :, 1, :], in1=tg[0:nr], op=ALU.add),
                                reads=["tmpg", ("stg", sslot, 1)], writes=[("stg", sslot, 1)])
                            sreads = [("stg", sslot, 0), ("stg", sslot, 1)]
                        else:
                            P.op("act", lambda e, S=S, acc=acc, nr=nr: e.activation(out=S[0:nr, :], in_=acc[0:nr, :], func=AF.Copy),
                                 reads=[akey], writes=[("stg", sslot, 0), ("stg", sslot, 1)])
                            sreads = [("stg", sslot, 0), ("stg", sslot, 1)]
                        if kind in ("ka", "va", "ks", "vs"):
                            dst = o_kv[{"ka": "mk", "va": "mv", "ks": "sk", "vs": "sv"}[kind]]
                            dap = dst[0][r0:r0 + nr, :] if t < NT else dst[1][:, :]
                            P.dma("sp", dap, S[0:nr, :], reads=sreads, key=("o", "stg", sslot))
                        if kind in ("va", "vs"):
                            if t < NT:
                                if kind == "va":
                                    P.op("pool", lambda e, S=S, t=t: e.tensor_copy(
                                        out=Va[:, t, :, 0:64], in_=S[:, :].rearrange("p (h d) -> p h d", d=64)),
                                        reads=sreads, writes=[("Va", t)])
                                else:
                                    P.op("pool", lambda e, S=S, t=t: e.tensor_copy(out=Vs[:, t, :], in_=S[:, :]),
                                         reads=sreads, writes=[("Vs", t)])
                            else:
                                Vn = Vna if kind == "va" else Vns
                                P.op("pool", lambda e, S=S, Vn=Vn: e.tensor_copy(out=Vn[:, :], in_=S[0:ST, :]),
                                     reads=sreads, writes=["Vn" + kind])
                        else:
                            scale = 0.125 if kind in ("qa", "qs") else 1.0
                            P.op("dve", lambda e, Q=Q, S=S, nr=nr, scale=scale: e.tensor_scalar(
                                out=Q[0:nr, :], in0=S[0:nr, :], scalar1=scale, scalar2=None, op0=ALU.mult),
                                reads=sreads, writes=[("qb", qslot)])
                            pT = pt16[it % 2]
                            for j in range(4):
                                P.op("pe", lambda e, pT=pT, Q=Q, j=j, nr=nr: e.transpose(
                                    out=pT[:, j * 128:j * 128 + nr], in_=Q[0:nr, j * 128:(j + 1) * 128],
                                    identity=ident[0:nr, 0:nr]),
                                    reads=[("qb", qslot), "consts"], writes=[("pt16", it % 2)])
                            if t < NT:
                                dstT = {"qa": QTa, "ka": KTa, "qs": QTs, "ks": KTs}[kind]
                                dsl = dstT[:, :, r0:r0 + nr]
                            else:
                                dstT = {"qa": QTa_s, "ka": KTa_s, "qs": QTs_s, "ks": KTs_s}[kind]
                                dsl = dstT[:, :, :]
                            P.op("act", lambda e, pT=pT, dsl=dsl, nr=nr: e.activation(
                                out=dsl, in_=pT[:, 0:512].rearrange("p (j t) -> p j t", t=128)[:, :, 0:nr], func=AF.Copy),
                                reads=[("pt16", it % 2)], writes=[(kind + "T", t)])
                P.emit()

            with ExitStack() as s2:
                P = Prog(ctx)
                otok_a = sb("otok_a", [128, NT, 512], BF16, s2)
                otok_s = sb("otok_s", [128, NT, 512], BF16, s2)
                kmf = sb("kmf", [128, 4, 8], F32, s2)
                kmT = sb("kmT", [128, 4, 8], BF16, s2)
                Gm = sb("Gm", [128, 8, 8], F32, s2)
                top8 = sb("top8", [128, 8, 8], F32, s2)
                selt = sb("selt", [128, 8, 8], F32, s2)
                Mtok = sb("Mtok", [128, 8, 64], BF16, s2)
                MTs = [sb("MTs%d" % i, [8, 512], BF16, s2) for i in range(2)]
                Pt = [sb("Pt%d" % i, [128, 512], BF16, s2) for i in range(4)]
                Ef = [sb("Ef%d" % i, [128, 512], F32, s2) for i in range(3)]
                SP = [sb("SP%d" % i, [128, 512], BF16, s2) for i in range(3)]
                Rr = [sb("Rr%d" % i, [128, 512], BF16, s2) for i in range(4)]
                rden = sb("rden", [128, 4], F32, s2)
                for nm, src, dst, a in (("w_bm", w_bm, w_bm_b, 1), ("w_bs", w_bs, w_bs_b, 1),
                                        ("w_out", w_out, w_out_b, 1), ("w_g", w_g, w_g_b, 2), ("w_u", w_u, w_u_b, 2),
                                        ("w_d", w_d, w_d_b, 1)):
                    P.dma("pool", dst.rearrange("k (a n) -> k a n", a=a), src.rearrange("k (a n) -> k a n", a=a),
                          writes=[("wscr", nm)], key=("wscr", nm))

                for hp in range(4):
                    P.op("dve", lambda e, hp=hp: e.reduce_sum(
                        out=kmf[:, hp, :], in_=KTa[:, hp, :].rearrange("p (n k) -> p n k", k=256), axis=AX.X),
                        writes=[("kmf", hp)])
                P.op("dve", lambda e: e.tensor_scalar(out=kmT[:], in0=kmf[:], scalar1=1.0 / 256, scalar2=None, op0=ALU.mult),
                     reads=[("kmf", hp) for hp in range(4)], writes=["kmT"])
                for c in range(8, NT):
                    cur = c // 2
                    G = pb[5]
                    for h in range(8):
                        hp, hb_ = h // 2, (h % 2) * 64
                        P.op("pe", lambda e, G=G, h=h, hp=hp, hb_=hb_, c=c: e.matmul(
                            G[:, h * 8:(h + 1) * 8], lhsT=QTa[hb_:hb_ + 64, hp, c * 128:(c + 1) * 128],
                            rhs=kmT[hb_:hb_ + 64, hp, :], start=True, stop=True),
                            reads=["kmT"], writes=[("pb", 5)])
                    P.op("dve", lambda e, G=G: e.tensor_copy(out=Gm[:], in_=G[:, 0:64].rearrange("p (h n) -> p h n", n=8)),
                         reads=[("pb", 5)], writes=["Gm"])
                    P.op("dve", lambda e, cur=cur: e.memset(Gm[:, :, cur:8], -1e30), reads=["Gm"], writes=["Gm"])
                    for h in range(8):
                        P.op("dve", lambda e, h=h: e.max(out=top8[:, h, :], in_=Gm[:, h, :]), reads=["Gm"], writes=[("top8", h)])
                    P.op("dve", lambda e: e.tensor_tensor(
                        out=selt[:], in0=Gm[:], in1=top8[:, :, 2:3].to_broadcast([128, 8, 8]), op=ALU.is_ge),
                        reads=["Gm"] + [("top8", h) for h in range(8)], writes=["selt"])
                    Mv = Mtok[:, c - 8, :].rearrange("p (h n) -> p h n", n=8)
                    P.op("dve", lambda e, Mv=Mv: e.tensor_scalar(
                        out=Mv, in0=selt[:], scalar1=1.0, scalar2=-NEG, op0=ALU.subtract, op1=ALU.mult),
                        reads=["selt"], writes=[("Mtok", c)])
                    P.op("dve", lambda e, Mv=Mv, cur=cur: e.memset(Mv[:, :, cur:cur + 1], 0.0),
                         reads=[("Mtok", c)], writes=[("Mtok", c)])

                sidx = [0]
                gidx = [0]

                def attn_head_group(h, qg, moba):
                    hp, hb_ = h // 2, (h % 2) * 64
                    KT, QT = (KTa, QTa) if moba else (KTs, QTs)
                    c_lo, c_hi = qg * 4, qg * 4 + 3
                    oi = 4 + gidx[0] % 2
                    mslot = rslot = gidx[0] % 2
                    gidx[0] += 1
                    O = pb[oi]
                    okey = ("pb", oi)
                    ncol = 4 * 65 if moba else 4 * 64
                    ow = 65 if moba else 64
                    P.i("pe", "matmul", ["zeros16"], [okey], O[:, 0:ncol], lhsT=zeros16[:, 0:128], rhs=zeros16[:, 0:ncol],
                        start=True, stop=False)
                    mts = None
                    if moba and qg >= 2:
                        mts = MTs[mslot]
                        pT = pt16[0]
                        for i in range(4):
                            c = c_lo + i
                            P.i("pe", "transpose", [("Mtok", c), "consts"], [("pt16", 0)], out=pT[0:8, i * 128:(i + 1) * 128],
                                in_=Mtok[:, c - 8, h * 8:(h + 1) * 8], identity=ident)
                        P.i("dve", "tensor_copy", [("pt16", 0)], [("MTs", mslot)], out=mts[:, :], in_=pT[0:8, 0:512])
                    Rpp = None
                    if not moba:
                        Rpp = (Rr[2 * rslot], Rr[2 * rslot + 1])
                        for q_ in range(2):
                            P.i("pool", "memset", [], [("Rr", 2 * rslot + q_)], Rpp[q_][:], 0.0)
                    rstep = [0]
                    kts = list(range(0, c_hi + 1)) if moba else list(range(c_hi, -1, -1))

                    def stageA(kt):
                        c0 = max(kt, c_lo)
                        N = (c_hi + 1 - c0) * 128
                        q0 = c0 * 128
                        diag = kt >= c_lo
                        n = sidx[0]
                        sidx[0] += 1
                        st = dict(kt=kt, c0=c0, N=N, coff=(c0 - c_lo) * 128, sb_i=n % 4, eslot=n % 3, pslot=n % 4)
                        S1 = pb[st["sb_i"]]
                        k1 = ("pb", st["sb_i"])
                        P.i("pe", "matmul", [], [k1], S1[:, 0:N], lhsT=KT[hb_:hb_ + 64, hp, kt * 128:(kt + 1) * 128],
                            rhs=QT[hb_:hb_ + 64, hp, q0:q0 + N], start=True, stop=False)
                        if diag:
                            P.i("pe", "matmul", ["consts"], [k1], S1[:, 0:128], lhsT=ident,
                                rhs=(mask_incl if moba else mask_strict), start=False, stop=False)
                        if moba:
                            n_blk = kt // 2
                            cm = max(c0, 2 * n_blk + 2)
                            if qg >= 2 and cm <= c_hi:
                                o1 = (cm - c0) * 128
                                m1 = (cm - c_lo) * 128
                                P.i("pe", "matmul", ["eb", ("MTs", mslot)], [k1], S1[:, o1:N], lhsT=eb[0:8, n_blk, :],
                                    rhs=mts[0:8, m1:512], start=False, stop=True)
                        return st

                    def stageB(st):
                        N = st["N"]
                        S1 = pb[st["sb_i"]]
                        k1 = ("pb", st["sb_i"])
                        if moba:
                            A = Pt[st["pslot"]]
                            P.i("act", "activation", [k1], [("Pt", st["pslot"])], out=A[:, 0:N], in_=S1[:, 0:N], func=AF.Exp)
                        else:
                            E, SPt = Ef[st["eslot"]], SP[st["eslot"]]
                            P.i("act", "activation", [k1], [("Ef", st["eslot"])], out=E[:, 0:N], in_=S1[:, 0:N], func=AF.Exp)
                            P.i("act", "activation", [("Ef", st["eslot"])], [("SP", st["eslot"])], out=SPt[:, 0:N],
                                in_=E[:, 0:N], func=AF.Ln, bias=1.0)

                    def stageC(st, first):
                        kt, c0, N, coff = st["kt"], st["c0"], st["N"], st["coff"]
                        S1 = pb[st["sb_i"]]
                        k1 = ("pb", st["sb_i"])
                        A = Pt[st["pslot"]]
                        if not moba:
                            SPt = SP[st["eslot"]]
                            P.i("pe", "matmul", ["consts", ("SP", st["eslot"])], [k1], S1[:, 0:N], lhsT=Uneg, rhs=SPt[:, 0:N],
                                start=False, stop=first)
                            ra = rstep[0] % 2
                            rstep[0] += 1
                            Rcur, Rnxt = Rpp[ra], Rpp[1 - ra]
                            kcur, knxt = ("Rr", 2 * rslot + ra), ("Rr", 2 * rslot + 1 - ra)
                            if kt > 0:
                                P.i("pool", "tensor_tensor", [("SP", st["eslot"]), kcur], [knxt],
                                    out=Rnxt[:, coff:coff + N], in0=Rcur[:, coff:coff + N], in1=SPt[:, 0:N], op=ALU.add)
                            if not first:
                                P.i("pe", "matmul", ["consts", kcur], [k1], S1[:, 0:N], lhsT=negones,
                                    rhs=Rcur[:, coff:coff + N], start=False, stop=True)
                            P.i("act", "activation", [k1], [("Pt", st["pslot"])], out=A[:, 0:N], in_=S1[:, 0:N], func=AF.Exp)
                        for c in range(c0, c_hi + 1):
                            j = c - c0
                            i = c - c_lo
                            rhs = Va[:, kt, h, :] if moba else Vs[:, kt, h * 64:(h + 1) * 64]
                            P.i("pe", "matmul", [("Pt", st["pslot"])], [okey], O[:, i * ow:(i + 1) * ow],
                                lhsT=A[:, j * 128:(j + 1) * 128], rhs=rhs, start=False, stop=False)

                    sts = []
                    for i_, kt in enumerate(kts):
                        sts.append(stageA(kt))
                        if i_ >= 1:
                            stageB(sts[i_ - 1])
                        if i_ >= 2:
                            stageC(sts[i_ - 2], i_ - 2 == 0)
                    nk = len(kts)
                    stageB(sts[nk - 1])
                    if nk >= 2:
                        stageC(sts[nk - 2], nk - 2 == 0)
                    stageC(sts[nk - 1], nk - 1 == 0)
                    if moba:
                        Ov = O[:, 0:ncol].rearrange("p (i w) -> p i w", w=65)
                        P.i("dve", "reciprocal", [okey], ["rden"], out=rden[:, :], in_=Ov[:, :, 64])
                        P.i("dve", "tensor_tensor", [okey, "rden"], [("otok_a", qg, h)],
                            out=otok_a[:, c_lo:c_hi + 1, h * 64:(h + 1) * 64], in0=Ov[:, :, 0:64],
                            in1=rden[:, :].unsqueeze(2).to_broadcast([128, 4, 64]), op=ALU.mult)
                    else:
                        Ov = O[:, 0:ncol].rearrange("p (i w) -> p i w", w=64)
                        P.i("dve", "tensor_copy", [okey], [("otok_s", qg, h)],
                            out=otok_s[:, c_lo:c_hi + 1, h * 64:(h + 1) * 64], in_=Ov)

                for h in range(8):
                    for qg in range(4):
                        attn_head_group(h, qg, True)
                for h in range(8):
                    for qg in range(4):
                        attn_head_group(h, qg, False)
                for qg in range(4):
                    for nm, ot, scr in (("otok_a", otok_a, oa_scr), ("otok_s", otok_s, os_scr)):
                        P.dma("sp", scr[qg * 512:(qg + 1) * 512, :].rearrange("(i p) f -> p i f", p=128),
                              ot[:, qg * 4:(qg + 1) * 4, :], reads=[(nm, qg, h) for h in range(8)], key=("o", nm))
                P.emit()


        with ExitStack() as s2:
            P = Prog(ctx)
            pt_sb = sb("pt_sb", [128, 4, NPAGES], I32, s2)
            ptf = sb("ptf", [128, 4, NPAGES], F32, s2)
            ptsel = sb("ptsel", [128, 64, 4], F32, s2)
            pgrp = sb("pgrp", [128, 64], F32, s2)
            idxf = sb("idxf", [128, 64], F32, s2)
            idx = sb("idx", [128, 64], I32, s2)
            oh4 = sb("oh4_sb", [128, 4], F32, s2)
            pmod = sb("pmod_sb", [128, 1], F32, s2)
            cm8 = sb("cm8_sb", [8, 2, 64], BF16, s2)
            zsel = sb("zsel_sb", [128, 16, 32], BF16, s2)
            egm = sb("eg_sb", [32, 16, 128], BF16, s2)
            Vnq = sb("Vnq", [8, 4, 2, 512], BF16, s2)
            Qbd = sb("Qbd", [128, 2, 4, 16], BF16, s2)
            Ka_seq = sb("Ka_seq", [128, NPAGES, 512], BF16, s2)
            NSL = 3
            Ksb = [sb("Ksb%d" % i, [128, 4, 512], BF16, s2) for i in range(NSL)]
            Vab = [sb("Vab%d" % i, [128, 4, 512], BF16, s2) for i in range(NSL)]
            Vsb = [sb("Vsb%d" % i, [128, 4, 512], BF16, s2) for i in range(NSL)]
            KTg = [[sb("KTg%d_%d" % (a, i), [128, 4, 4, 128], BF16, s2) for i in range(NSL)] for a in range(2)]
            KMb = sb("KMb", [32, 512], BF16, s2)
            KMT = sb("KMT", [128, 4, 32], BF16, s2)
            Gs = sb("Gs", [16, 4, 32], F32, s2)
            top8s = sb("top8s", [16, 4, 8], F32, s2)
            sels = sb("sels", [16, 4, 32], F32, s2)
            Msel = sb("Msel", [16, 4, 32], BF16, s2)
            MTs2 = sb("MTs2", [32, 64], BF16, s2)
            Pn = sb("Pn", [8, 64], BF16, s2)
            En = sb("En", [8, 64], F32, s2)
            SPn = sb("SPn", [8, 64], BF16, s2)
            An = sb("An", [8, 64], BF16, s2)
            Pm = [sb("Pm%d" % i, [128, 4, 64], BF16, s2) for i in range(NSL)]
            Eg = [sb("Eg%d" % i, [128, 256], F32, s2) for i in range(NSL)]
            SPg = [sb("SPg%d" % i, [128, 4, 64], BF16, s2) for i in range(NSL)]
            Ag = [sb("Ag%d" % i, [128, 4, 64], BF16, s2) for i in range(NSL)]
            Wc = [sb("Wc%d" % i, [128, 4, 64], BF16, s2) for i in range(NSL)]
            carry = sb("carry", [128, 64], BF16, s2)
            rdn = sb("rdn", [64, 1], F32, s2)
            oa_sb = sb("oa_sb", [64, 512], BF16, s2)
            os_sb = sb("os_sb", [64, 512], BF16, s2)

            P.dma("sp", pt_sb[:], ptab.partition_broadcast(128), writes=["pt_sb"])
            P.dma("sp", cm8[:], cm8_d, writes=["cm8"])
            P.dma("sp", oh4[:], oh4_d, writes=["oh4"])
            P.dma("sp", pmod[:], pmod_d, writes=["pmod"])
            P.dma("sp", zsel[:], zsel_d, writes=["zsel"])
            P.dma("sp", egm[:], eg_d, writes=["egm"])
            P.i("dve", "tensor_copy", ["pt_sb"], ["ptf"], out=ptf[:], in_=pt_sb[:])
            P.i("dve", "tensor_tensor", ["ptf", "oh4"], ["ptsel"], out=ptsel[:],
                in0=ptf[:, :, :].rearrange("p s (g l) -> p (s g) l", l=4),
                in1=oh4[:, :].unsqueeze(1).to_broadcast([128, 64, 4]), op=ALU.mult)
            P.i("dve", "reduce_sum", ["ptsel"], ["pgrp"], out=pgrp[:], in_=ptsel[:], axis=AX.X)
            P.i("dve", "tensor_scalar", ["pgrp", "pmod"], ["idxf"], out=idxf[:], in0=pgrp[:], scalar1=32.0,
                scalar2=pmod[:, 0:1], op0=ALU.mult, op1=ALU.add)
            P.i("dve", "tensor_copy", ["idxf"], ["idx"], out=idx[:], in_=idxf[:])
            for s in range(4):
                P.dma("sp", Vnq[0:8, s, 0, :], Vna[s * 8:(s + 1) * 8, :], writes=[("Vnq", s)])
                P.dma("sp", Vnq[0:8, s, 1, :], Vns[s * 8:(s + 1) * 8, :], writes=[("Vnq", s)])

            def gather(dst, cache, s, g, reads, writes, key):
                col = s * 16 + g
                P.op("pool", lambda e: e.indirect_dma_start(
                    out=dst.rearrange("p t f -> p (t f)"), out_offset=None,
                    in_=cache.rearrange("(r t) f -> r (t f)", t=4),
                    in_offset=bass.IndirectOffsetOnAxis(ap=idx[:, col:col + 1], axis=0)),
                    reads=reads, writes=writes, dma=True, key=key)

            gcount = [0]
            for s in range(4):
                sc = slice(s * 8, (s + 1) * 8)
                P.i("dve", "memset", [], ["Qbd"], Qbd[:], 0.0)
                for a, QTx in ((0, QTa_s), (1, QTs_s)):
                    P.i("dve", "tensor_copy", ["Qbd"], ["Qbd"], out=Qbd[0:64, a, :, 0:8], in_=QTx[0:64, :, sc])
                    P.i("dve", "tensor_copy", ["Qbd"], ["Qbd"], out=Qbd[64:128, a, :, 8:16], in_=QTx[64:128, :, sc])
                for g in range(16):
                    gather(Ka_seq[:, g * 4:(g + 1) * 4, :], cmk, s, g, ["idx"], [("Ka", g // 2)], ("Ka", g // 2))
                KM = pb[2]
                for j in range(NPAGES):
                    P.i("pe", "matmul", [("Ka", j // 8), "zsel"], [("pb", 2)], KM[0:32, :],
                        lhsT=zsel[:, j // 4, :], rhs=Ka_seq[:, j, :], start=(j == 0), stop=(j == NPAGES - 1))
                P.i("act", "activation", [("pb", 2)], ["KMb"], out=KMb[:, :], in_=KM[0:32, :], func=AF.Copy)
                pT = pt16[0]
                for hp in range(4):
                    P.i("pe", "transpose", ["KMb"], [("pt16", 0)], out=pT[:, hp * 32:(hp + 1) * 32],
                        in_=KMb[0:32, hp * 128:(hp + 1) * 128], identity=ident[0:32, 0:32])
                P.i("dve", "tensor_copy", [("pt16", 0)], ["KMT"], out=KMT[:, :, :],
                    in_=pT[:, 0:128].rearrange("p (h n) -> p h n", n=32))
                Gp = pb[5]
                for hp in range(4):
                    P.i("pe", "matmul", ["KMT", "Qbd"], [("pb", 5)], Gp[0:16, hp * 32:(hp + 1) * 32],
                        lhsT=Qbd[:, 0, hp, :], rhs=KMT[:, hp, :], start=True, stop=True)
                P.i("dve", "tensor_copy", [("pb", 5)], ["Gs"], out=Gs[:, :, :],
                    in_=Gp[0:16, 0:128].rearrange("p (h n) -> p h n", n=32))
                for hp in range(4):
                    P.i("dve", "max", ["Gs"], [("top8s", hp)], out=top8s[:, hp, :], in_=Gs[:, hp, :])
                P.i("dve", "tensor_tensor", ["Gs"] + [("top8s", hp) for hp in range(4)], ["sels"], out=sels[:],
                    in0=Gs[:], in1=top8s[:, :, 2:3].to_broadcast([16, 4, 32]), op=ALU.is_ge)
                P.i("dve", "tensor_scalar", ["sels"], ["Msel"], out=Msel[:], in0=sels[:], scalar1=1.0, scalar2=-NEG,
                    op0=ALU.subtract, op1=ALU.mult)
                pT = pt16[1]
                for hp in range(4):
                    P.i("pe", "transpose", ["Msel"], [("pt16", 1)], out=pT[0:32, hp * 16:(hp + 1) * 16],
                        in_=Msel[0:16, hp, :], identity=ident[0:16, 0:16])
                P.i("dve", "tensor_copy", [("pt16", 1)], ["MTs2"], out=MTs2[:, :], in_=pT[0:32, 0:64])
                Oa, Os, Dn = pb[3], pb[4], pb[5]
                P.i("pe", "matmul", ["zeros16"], [("pb", 3)], Oa[0:64, :], lhsT=zeros16[:, 0:64], rhs=zeros16[:, 0:512],
                    start=True, stop=False)
                P.i("pe", "matmul", ["zeros16"], [("pb", 4)], Os[0:64, :], lhsT=zeros16[:, 0:64], rhs=zeros16[:, 0:512],
                    start=True, stop=False)
                P.i("pe", "matmul", ["zeros16", "Gs"], [("pb", 5)], Dn[0:64, 0:8], lhsT=zeros16[:, 0:64], rhs=zeros16[:, 0:8],
                    start=True, stop=False)
                Sm, S1, S2 = pb[2], pb[0], pb[1]
                P.i("pe", "matmul", ["cm8"], [("pb", 2)], Sm[0:8, 0:64], lhsT=ident[0:8, 0:8], rhs=cm8[0:8, 0, :],
                    start=True, stop=False)
                for hp in range(4):
                    P.i("pe", "matmul", ["Qbd"], [("pb", 2)], Sm[0:8, hp * 16:(hp + 1) * 16],
                        lhsT=KTa_s[:, hp, sc], rhs=Qbd[:, 0, hp, :], start=False, stop=(hp == 3))
                P.i("act", "activation", [("pb", 2)], ["Pn"], out=Pn[:, :], in_=Sm[0:8, 0:64], func=AF.Exp)
                P.i("pe", "matmul", ["Pn", ("Vnq", s)], [("pb", 3)], Oa[0:64, :], lhsT=Pn[0:8, :], rhs=Vnq[0:8, s, 0, :],
                    start=False, stop=False)
                P.i("pe", "matmul", ["Pn"], [("pb", 5)], Dn[0:64, 0:1], lhsT=Pn[0:8, :], rhs=ones[0:8, 0:1],
                    start=False, stop=False)
                P.i("pe", "matmul", ["cm8"], [("pb", 0)], S1[0:8, 0:64], lhsT=ident[0:8, 0:8], rhs=cm8[0:8, 1, :],
                    start=True, stop=False)
                for hp in range(4):
                    P.i("pe", "matmul", ["Qbd"], [("pb", 0)], S1[0:8, hp * 16:(hp + 1) * 16],
                        lhsT=KTs_s[:, hp, sc], rhs=Qbd[:, 1, hp, :], start=False, stop=(hp == 3))
                P.i("act", "activation", [("pb", 0)], ["En"], out=En[:, :], in_=S1[0:8, 0:64], func=AF.Exp)
                P.i("act", "activation", ["En"], ["SPn"], out=SPn[:, :], in_=En[:, :], func=AF.Ln, bias=1.0)
                P.i("pe", "matmul", ["SPn"], [("pb", 1)], S2[0:8, 0:64], lhsT=Uneg[0:8, 0:8], rhs=SPn[0:8, :],
                    start=True, stop=False)
                P.i("pe", "matmul", ["cm8"], [("pb", 1)], S2[0:8, 0:64], lhsT=ident[0:8, 0:8], rhs=cm8[0:8, 1, :],
                    start=False, stop=False)
                for hp in range(4):
                    P.i("pe", "matmul", ["Qbd"], [("pb", 1)], S2[0:8, hp * 16:(hp + 1) * 16],
                        lhsT=KTs_s[:, hp, sc], rhs=Qbd[:, 1, hp, :], start=False, stop=(hp == 3))
                P.i("act", "activation", [("pb", 1)], ["An"], out=An[:, :], in_=S2[0:8, 0:64], func=AF.Exp)
                P.i("pe", "matmul", ["An", ("Vnq", s)], [("pb", 4)], Os[0:64, :], lhsT=An[0:8, :], rhs=Vnq[0:8, s, 1, :],
                    start=False, stop=False)
                for g in range(15, -1, -1):
                    slot = gcount[0] % NSL
                    gcount[0] += 1
                    NC_ = 256
                    gather(Ksb[slot][:, :, :], csk, s, g, ["idx"], [("Ksb", slot)], ("Ksb", slot))
                    gather(Vab[slot][:, :, :], cmv, s, g, ["idx"], [("Vab", slot)], ("Vab", slot))
                    gather(Vsb[slot][:, :, :], csv, s, g, ["idx"], [("Vsb", slot)], ("Vsb", slot))
                    ev = 0
                    for a in range(2):
                        for pp in range(0, 4, 2):
                            bank = (a * 2 + pp // 2) % 2
                            pT = pt16[bank]
                            for q in range(2):
                                pl = pp + q
                                src = Ka_seq[:, g * 4 + pl, :] if a == 0 else Ksb[slot][:, pl, :]
                                rk = ("Ka", g // 2) if a == 0 else ("Ksb", slot)
                                for hp in range(4):
                                    P.i("pe", "transpose", [rk], [("pt16", bank)],
                                        out=pT[:, q * 512 + hp * 128:q * 512 + (hp + 1) * 128],
                                        in_=src[:, hp * 128:(hp + 1) * 128], identity=ident)
                            dst = KTg[a][slot][:, pp:pp + 2, :, :].rearrange("p a h t -> p (a h t)")
                            if ev % 2 == 0:
                                P.i("act", "activation", [("pt16", bank)], [("KTg", a, slot, pp)], out=dst, in_=pT[:, :], func=AF.Copy)
                            else:
                                P.i("dve", "tensor_copy", [("pt16", bank)], [("KTg", a, slot, pp)], out=dst, in_=pT[:, :])
                            ev += 1
                    P.i("pe", "matmul", ["MTs2", "egm"], [("pb", 2)], Sm[:, 0:NC_], lhsT=egm[:, g, :],
                        rhs=MTs2[:, :].unsqueeze(1).to_broadcast([32, 4, 64]), start=True, stop=False)
                    for pl in range(4):
                        for hp in range(4):
                            P.i("pe", "matmul", [("KTg", 0, slot, (pl // 2) * 2), "Qbd"], [("pb", 2)],
                                Sm[:, pl * 64 + hp * 16:pl * 64 + (hp + 1) * 16],
                                lhsT=KTg[0][slot][:, pl, hp, :], rhs=Qbd[:, 0, hp, :], start=False, stop=False)
                    PM = Pm[slot]
                    P.i("act", "activation", [("pb", 2)], [("Pm", slot)], out=PM[:, :, :].rearrange("p a c -> p (a c)"),
                        in_=Sm[:, 0:NC_], func=AF.Exp)
                    for pl in range(4):
                        P.i("pe", "matmul", [("Pm", slot), ("Vab", slot)], [("pb", 3)], Oa[0:64, :], lhsT=PM[:, pl, :],
                            rhs=Vab[slot][:, pl, :], start=False, stop=False)
                        P.i("pe", "matmul", [("Pm", slot)], [("pb", 5)], Dn[0:64, 0:1], lhsT=PM[:, pl, :],
                            rhs=ones[:, 0:1], start=False, stop=False)
                    for pl in range(4):
                        for hp in range(4):
                            P.i("pe", "matmul", [("KTg", 1, slot, (pl // 2) * 2), "Qbd"], [("pb", 0)],
                                S1[:, pl * 64 + hp * 16:pl * 64 + (hp + 1) * 16],
                                lhsT=KTg[1][slot][:, pl, hp, :], rhs=Qbd[:, 1, hp, :], start=True, stop=True)
                    EG, SPG, AG, WC = Eg[slot], SPg[slot], Ag[slot], Wc[slot]
                    P.i("act", "activation", [("pb", 0)], [("Eg", slot)], out=EG[:, :], in_=S1[:, 0:NC_], func=AF.Exp)
                    P.i("act", "activation", [("Eg", slot)], [("SPg", slot)], out=SPG[:, :, :].rearrange("p a c -> p (a c)"),
                        in_=EG[:, :], func=AF.Ln, bias=1.0)
                    P.i("dve", "tensor_copy", [("SPg", slot)], [("Wc", slot)], out=WC[:, 3, :], in_=SPG[:, 3, :])
                    for pl in range(2, -1, -1):
                        P.i("dve", "tensor_tensor", [("Wc", slot), ("SPg", slot)], [("Wc", slot)], out=WC[:, pl, :],
                            in0=WC[:, pl + 1, :], in1=SPG[:, pl, :], op=ALU.add)
                    P.i("pe", "matmul", [("Wc", slot)], [("pb", 1)], S2[:, 0:NC_], lhsT=negI,
                        rhs=WC[:, :, :].rearrange("p a c -> p (a c)"), start=True, stop=False)
                    P.i("pe", "matmul", [("Wc", slot)], [("pb", 1)], S2[:, 0:NC_], lhsT=Usneg,
                        rhs=WC[:, 0, :].unsqueeze(1).to_broadcast([128, 4, 64]), start=False, stop=False)
                    if g < 15:
                        P.i("pe", "matmul", ["carry"], [("pb", 1)], S2[:, 0:NC_], lhsT=negones,
                            rhs=carry[:, :].unsqueeze(1).to_broadcast([128, 4, 64]), start=False, stop=False)
                    if g > 0:
                        if g == 15:
                            P.i("dve", "tensor_copy", [("Wc", slot)], ["carry"], out=carry[:, :], in_=WC[:, 0, :])
                        else:
                            P.i("dve", "tensor_tensor", [("Wc", slot), "carry"], ["carry"], out=carry[:, :],
                                in0=carry[:, :], in1=WC[:, 0, :], op=ALU.add)
                    P.i("pe", "matmul", ["SPn"], [("pb", 1)], S2[:, 0:NC_], lhsT=negones[0:8, :],
                        rhs=SPn[0:8, :].unsqueeze(1).to_broadcast([8, 4, 64]), start=False, stop=False)
                    for pl in range(4):
                        for hp in range(4):
                            P.i("pe", "matmul", [("KTg", 1, slot, (pl // 2) * 2), "Qbd"], [("pb", 1)],
                                S2[:, pl * 64 + hp * 16:pl * 64 + (hp + 1) * 16],
                                lhsT=KTg[1][slot][:, pl, hp, :], rhs=Qbd[:, 1, hp, :], start=False, stop=False)
                    P.i("act", "activation", [("pb", 1)], [("Ag", slot)], out=AG[:, :, :].rearrange("p a c -> p (a c)"),
                        in_=S2[:, 0:NC_], func=AF.Exp)
                    for pl in range(4):
                        P.i("pe", "matmul", [("Ag", slot), ("Vsb", slot)], [("pb", 4)], Os[0:64, :], lhsT=AG[:, pl, :],
                            rhs=Vsb[slot][:, pl, :], start=False, stop=False)
                P.i("dve", "reciprocal", [("pb", 5)], ["rdn"], out=rdn[:, :], in_=Dn[0:64, 0:1])
                P.i("dve", "tensor_scalar", [("pb", 3), "rdn"], ["oa_sb"], out=oa_sb[:, :], in0=Oa[0:64, :],
                    scalar1=rdn[:, 0:1], scalar2=None, op0=ALU.mult)
                P.i("act", "activation", [("pb", 4)], ["os_sb"], out=os_sb[:, :], in_=Os[0:64, :], func=AF.Copy)
                r0 = SEQ + s * 8
                for h in range(8):
                    P.dma("sp", oa_scr[r0:r0 + 8, h * 64:(h + 1) * 64], oa_sb[h * 8:(h + 1) * 8, h * 64:(h + 1) * 64],
                          reads=["oa_sb"], key=("o", "osc_a"))
                    P.dma("sp", os_scr[r0:r0 + 8, h * 64:(h + 1) * 64], os_sb[h * 8:(h + 1) * 8, h * 64:(h + 1) * 64],
                          reads=["os_sb"], key=("o", "osc_s"))
            P.emit()

        with ExitStack() as s2:
            P = Prog(ctx)
            Wout = sb("Wout", [128, 8, D], BF16, s2)
            Wd = sb("Wd", [128, NFC, D], BF16, s2)
            w_out_v = w_out_b.rearrange("(kc p) n -> p kc n", p=128)
            w_d_v = w_d_b.rearrange("(fc p) n -> p fc n", p=128)
            for kc in range(0, 8, 2):
                P.dma("sp", Wout[:, kc:kc + 2, :], w_out_v[:, kc:kc + 2, :], writes=[("Wout", kc)])
            for fc in range(0, NFC, 2):
                P.dma("sp", Wd[:, fc:fc + 2, :], w_d_v[:, fc:fc + 2, :], writes=[("Wd", fc)])
            w_bm_v = w_bm_b.rearrange("(kc p) n -> p kc n", p=128)
            w_bs_v = w_bs_b.rearrange("(kc p) n -> p kc n", p=128)
            w_in_v = w_in_b.rearrange("(kc p) n -> p kc n", p=128)
            w_g_v = w_g_b.rearrange("(kc p) n -> p kc n", p=128)
            w_u_v = w_u_b.rearrange("(kc p) n -> p kc n", p=128)
            Wbr = [sb("Wbr%d" % i, [128, 2, 4, 128], BF16, s2) for i in range(2)]
            Wgt = [sb("Wgt%d" % i, [128, 2, 8, 128], BF16, s2) for i in range(2)]
            Wgu = [sb("Wgu%d" % i, [128, 2, 8, 128], BF16, s2) for i in range(4)]
            hTg = sb("hTg", [128, 8, 512], BF16, s2)
            otg = sb("otg", [128, 2, 4, 512], BF16, s2)
            oT = sb("oT", [128, 2, 4, 512], BF16, s2)
            mergedT = sb("mergedT", [128, 8, 512], BF16, s2)
            x1 = sb("x1", [128, 4, D], F32, s2)
            xin = sb("xin", [128, D], F32, s2)
            h2 = [sb("h2_%d" % i, [128, D], BF16, s2) for i in range(2)]
            h2T = sb("h2T", [128, 8, 512], BF16, s2)
            ffT = sb("ffT", [128, NFC, 512], BF16, s2)
            ea = sb("ea", [128, 512], F32, s2)
            ebb = sb("ebb", [128, 512], F32, s2)
            ma = sb("ma", [128, 512], F32, s2)
            ssd = sb("ssd", [128, 8], F32, s2)
            sqd = sb("sqd", [128, D], BF16, s2)

            NG = 5
            for g in range(NG):
                if g < 4:
                    t0, ntile, ntok, r0 = g * 4, 4, 512, g * 512
                    tiles = [(i, 128) for i in range(4)]
                else:
                    t0, ntile, ntok, r0 = NT, 1, ST, SEQ
                    tiles = [(0, ST)]
                gk = ("g", g)
                P.dma("sp", hTg[:, :, 0:ntok], hT_scr[:, :, r0:r0 + ntok], writes=["hTg"])
                for a, scr in ((0, oa_scr), (1, os_scr)):
                    for (i, nr) in tiles:
                        P.dma("sp", otg[0:nr, a, i, :], scr[r0 + i * 128:r0 + i * 128 + nr, :], writes=[("otg", a, i)])
                for a in range(2):
                    for (i, nr) in tiles:
                        pT = pt16[(a * 4 + i) % 2]
                        pk = ("pt16", (a * 4 + i) % 2)
                        for kc in range(4):
                            P.op("pe", lambda e, pT=pT, a=a, i=i, kc=kc, nr=nr: e.transpose(
                                out=pT[:, kc * 128:kc * 128 + nr], in_=otg[0:nr, a, i, kc * 128:(kc + 1) * 128],
                                identity=ident[0:nr, 0:nr]),
                                reads=[("otg", a, i), "consts"], writes=[pk])
                        P.op("act", lambda e, pT=pT, a=a, i=i, nr=nr: e.activation(
                            out=oT[:, a, :, i * 128:i * 128 + nr],
                            in_=pT[:, 0:512].rearrange("p (k t) -> p k t", t=128)[:, :, 0:nr], func=AF.Copy),
                            reads=[pk], writes=[("oT", a, i)])
                oT_reads = [("oT", a, i) for a in range(2) for (i, _) in tiles]
                for dc in range(8):
                    ws = (g * 8 + dc) % 2
                    WB, WG = Wbr[ws], Wgt[ws]
                    P.dma("sp", WB[:, 0, :, :], w_bm_v[:, :, dc * 128:(dc + 1) * 128], writes=[("Wbr", ws, 0)])
                    P.dma("sp", WB[:, 1, :, :], w_bs_v[:, :, dc * 128:(dc + 1) * 128], writes=[("Wbr", ws, 1)])
                    P.dma("sp", WG[:, 0, :, :], w_in_v[:, :, 3072 + dc * 128:3072 + (dc + 1) * 128], writes=[("Wgt", ws, 0)])
                    P.dma("sp", WG[:, 1, :, :], w_in_v[:, :, 4096 + dc * 128:4096 + (dc + 1) * 128], writes=[("Wgt", ws, 1)])
                    Ba, Bs, Ga, Gs = pb[0], pb[1], pb[2], pb[3]
                    for a, Bx in ((0, Ba), (1, Bs)):
                        for kc in range(4):
                            P.op("pe", lambda e, Bx=Bx, WB=WB, a=a, kc=kc: e.matmul(
                                Bx[:, 0:ntok], lhsT=WB[:, a, kc, :], rhs=oT[:, a, kc, 0:ntok], start=(kc == 0), stop=(kc == 3)),
                                reads=[("Wbr", ws, a)] + oT_reads, writes=[("pb", a)])
                    for a, Gx in ((0, Ga), (1, Gs)):
                        for kc in range(8):
                            P.op("pe", lambda e, Gx=Gx, WG=WG, a=a, kc=kc: e.matmul(
                                Gx[:, 0:ntok], lhsT=WG[:, a, kc, :], rhs=hTg[:, kc, 0:ntok], start=(kc == 0), stop=(kc == 7)),
                                reads=[("Wgt", ws, a), "hTg"], writes=[("pb", 2 + a)])
                    P.op("act", lambda e, Ga=Ga: e.activation(out=ea[:, 0:ntok], in_=Ga[:, 0:ntok], func=AF.Exp, scale=-1.0),
                         reads=[("pb", 2)], writes=["ea"])
                    P.op("act", lambda e, Gs=Gs: e.activation(out=ebb[:, 0:ntok], in_=Gs[:, 0:ntok], func=AF.Exp, scale=-1.0),
                         reads=[("pb", 3)], writes=["ebb"])
                    for nm_, t_ in (("ea", ea), ("ebb", ebb)):
                        P.i("act", "activation", [nm_], [nm_], out=t_[:, 0:ntok], in_=t_[:, 0:ntok], func=AF.Ln, bias=1.0)
                        P.i("act", "activation", [nm_], [nm_], out=t_[:, 0:ntok], in_=t_[:, 0:ntok], func=AF.Exp, scale=-1.0)
                    P.op("dve", lambda e, Ba=Ba: e.tensor_tensor(out=ma[:, 0:ntok], in0=Ba[:, 0:ntok], in1=ea[:, 0:ntok], op=ALU.mult),
                         reads=[("pb", 0), "ea"], writes=["ma"])
                    P.op("dve", lambda e, Bs=Bs: e.tensor_tensor(out=ebb[:, 0:ntok], in0=Bs[:, 0:ntok], in1=ebb[:, 0:ntok], op=ALU.mult),
                         reads=[("pb", 1), "ebb"], writes=["ebb"])
                    P.op("dve", lambda e, dc=dc: e.tensor_tensor(out=mergedT[:, dc, 0:ntok], in0=ma[:, 0:ntok], in1=ebb[:, 0:ntok], op=ALU.add),
                         reads=["ma", "ebb"], writes=[("mergedT", dc)])
                mreads = [("mergedT", dc) for dc in range(8)]
                for (i, nr) in tiles:
                    P.dma("sp", xin[0:nr, :], (xp[r0 + i * 128:r0 + i * 128 + nr, :] if g < 4 else xs[:, :]), writes=["xin"])
                    for half in range(2):
                        acc = pb[4 + half]
                        for dc in range(8):
                            P.op("pe", lambda e, acc=acc, dc=dc, i=i, nr=nr, half=half: e.matmul(
                                acc[0:nr, :], lhsT=mergedT[:, dc, i * 128:i * 128 + nr], rhs=Wout[:, dc, half * 512:(half + 1) * 512],
                                start=(dc == 0), stop=(dc == 7)),
                                reads=mreads + [("Wout", (dc // 2) * 2)], writes=[("pb", 4 + half)])
                        P.op("dve", lambda e, acc=acc, i=i, nr=nr, half=half: e.tensor_tensor(
                            out=x1[0:nr, i, half * 512:(half + 1) * 512], in0=acc[0:nr, :], in1=xin[0:nr, half * 512:(half + 1) * 512], op=ALU.add),
                            reads=[("pb", 4 + half), "xin"], writes=[("x1", i, half)])
                    x1r = [("x1", i, 0), ("x1", i, 1)]
                    P.op("act", lambda e, i=i, nr=nr: e.activation(out=sqd[0:nr, :], in_=x1[0:nr, i, :], func=AF.Square, accum_out=ssd[0:nr, 0:1]),
                         reads=x1r, writes=["sqd", "ssd"])
                    P.op("act", lambda e, nr=nr: e.activation(out=ssd[0:nr, 1:2], in_=ssd[0:nr, 0:1], func=AF.Ln, scale=1.0 / D, bias=EPS),
                         reads=["ssd"], writes=["ssd"])
                    P.op("act", lambda e, nr=nr: e.activation(out=ssd[0:nr, 2:3], in_=ssd[0:nr, 1:2], func=AF.Exp, scale=-0.5),
                         reads=["ssd"], writes=["ssd"])
                    hs = i % 2
                    H2 = h2[hs]
                    P.op("dve", lambda e, H2=H2, i=i, nr=nr: e.tensor_scalar(
                        out=H2[0:nr, :], in0=x1[0:nr, i, :], scalar1=ssd[0:nr, 2:3], scalar2=None, op0=ALU.mult),
                        reads=x1r + ["ssd"], writes=[("h2", hs)])
                    pT = pt16[hs]
                    for kc in range(8):
                        P.op("pe", lambda e, pT=pT, H2=H2, kc=kc, nr=nr: e.transpose(
                            out=pT[:, kc * 128:kc * 128 + nr], in_=H2[0:nr, kc * 128:(kc + 1) * 128], identity=ident[0:nr, 0:nr]),
                            reads=[("h2", hs), "consts"], writes=[("pt16", hs)])
                    P.op("dve", lambda e, pT=pT, i=i, nr=nr: e.tensor_tensor(
                        out=h2T[:, :, i * 128:i * 128 + nr],
                        in0=pT[:, :].rearrange("p (k t) -> p k t", t=128)[:, :, 0:nr],
                        in1=gffn_sb[:, :].unsqueeze(2).to_broadcast([128, 8, nr]), op=ALU.mult),
                        reads=[("pt16", hs), "gffn"], writes=[("h2T", i)])
                h2r = [("h2T", i) for (i, _) in tiles]
                for fc in range(NFC):
                    ws = (g * NFC + fc) % 4
                    WGU = Wgu[ws]
                    P.dma("sp", WGU[:, 0, :, :], w_g_v[:, :, fc * 128:(fc + 1) * 128], writes=[("Wgu", ws, 0)])
                    P.dma("sp", WGU[:, 1, :, :], w_u_v[:, :, fc * 128:(fc + 1) * 128], writes=[("Wgu", ws, 1)])
                    pg, pu = pb[(fc % 2) * 2], pb[(fc % 2) * 2 + 1]
                    kg, ku = ("pb", (fc % 2) * 2), ("pb", (fc % 2) * 2 + 1)
                    for a, px, kx in ((0, pg, kg), (1, pu, ku)):
                        for kc in range(8):
                            P.op("pe", lambda e, px=px, WGU=WGU, a=a, kc=kc: e.matmul(
                                px[:, 0:ntok], lhsT=WGU[:, a, kc, :], rhs=h2T[:, kc, 0:ntok], start=(kc == 0), stop=(kc == 7)),
                                reads=[("Wgu", ws, a)] + h2r, writes=[kx])
                    P.op("act", lambda e, pg=pg: e.activation(out=ea[:, 0:ntok], in_=pg[:, 0:ntok], func=AF.Exp, scale=-1.0),
                         reads=[kg], writes=["ea"])
                    P.i("act", "activation", ["ea"], ["ea"], out=ea[:, 0:ntok], in_=ea[:, 0:ntok], func=AF.Ln, bias=1.0)
                    P.i("act", "activation", ["ea"], ["ea"], out=ea[:, 0:ntok], in_=ea[:, 0:ntok], func=AF.Exp, scale=-1.0)
                    P.op("dve", lambda e, pg=pg: e.tensor_tensor(out=ma[:, 0:ntok], in0=pg[:, 0:ntok], in1=ea[:, 0:ntok], op=ALU.mult),
                         reads=[kg, "ea"], writes=["ma"])
                    P.op("dve", lambda e, pu=pu, fc=fc: e.tensor_tensor(out=ffT[:, fc, 0:ntok], in0=pu[:, 0:ntok], in1=ma[:, 0:ntok], op=ALU.mult),
                         reads=[ku, "ma"], writes=[("ffT", fc)])
                ffr = [("ffT", fc) for fc in range(NFC)]
                for (i, nr) in tiles:
                    for half in range(2):
                        acc = pb[4 + half]
                        for fc in range(NFC):
                            P.op("pe", lambda e, acc=acc, fc=fc, i=i, nr=nr, half=half: e.matmul(
                                acc[0:nr, :], lhsT=ffT[:, fc, i * 128:i * 128 + nr], rhs=Wd[:, fc, half * 512:(half + 1) * 512],
                                start=(fc == 0), stop=(fc == NFC - 1)),
                                reads=ffr + [("Wd", (fc // 2) * 2)], writes=[("pb", 4 + half)])
                        P.op("dve", lambda e, acc=acc, i=i, nr=nr, half=half: e.tensor_tensor(
                            out=x1[0:nr, i, half * 512:(half + 1) * 512], in0=acc[0:nr, :], in1=x1[0:nr, i, half * 512:(half + 1) * 512], op=ALU.add),
                            reads=[("pb", 4 + half), ("x1", i, half)], writes=[("x1", i, half)])
                    x1r = [("x1", i, 0), ("x1", i, 1)]
                    P.op("act", lambda e, i=i, nr=nr: e.activation(out=sqd[0:nr, :], in_=x1[0:nr, i, :], func=AF.Square, accum_out=ssd[0:nr, 4:5]),
                         reads=x1r, writes=["sqd", "ssd2"])
                    P.op("act", lambda e, nr=nr: e.activation(out=ssd[0:nr, 5:6], in_=ssd[0:nr, 4:5], func=AF.Ln, scale=1.0 / D, bias=EPS),
                         reads=["ssd2"], writes=["ssd2"])
                    P.op("act", lambda e, nr=nr: e.activation(out=ssd[0:nr, 6:7], in_=ssd[0:nr, 5:6], func=AF.Exp, scale=-0.5),
                         reads=["ssd2"], writes=["ssd2"])
                    P.op("dve", lambda e, i=i, nr=nr: e.scalar_tensor_tensor(
                        out=x1[0:nr, i, :], in0=x1[0:nr, i, :], scalar=ssd[0:nr, 6:7], in1=gfin_sb[0:nr, :], op0=ALU.mult, op1=ALU.mult),
                        reads=x1r + ["ssd2", "gfin"], writes=x1r)
                    ydst = yp[r0 + i * 128:r0 + i * 128 + nr, :] if g < 4 else ys[:, :]
                    P.dma("pool", ydst, x1[0:nr, i, :], reads=x1r, key=("o", "y", i))
            P.emit()
    return nc


_NC_CACHE = {}


def _consts():
    k = np.arange(128)[:, None]
    q = np.arange(128)[None, :]
    c = np.zeros((128, 8, 128), np.float32)
    c[:, 0, :] = np.eye(128)
    c[:, 1, :] = np.where(k <= q, 0.0, NEG)
    c[:, 2, :] = np.where(k < q, 0.0, NEG)
    c[:, 3, :] = np.where(k >= q, -1.0, 0.0)
    c[:, 4, :] = -1.0
    c[:, 5, :] = 1.0
    c[:, 6, :] = -np.eye(128)
    c[:, 7, :] = np.where(k > q, -1.0, 0.0)
    eb = np.zeros((8, 8, 128), np.float32)
    for n in range(8):
        eb[n, n, :] = 1.0
    kk = np.arange(8)[:, None, None]
    qq = np.arange(8)[None, None, :]
    cm8 = np.zeros((8, 2, 8, 8), np.float32)
    cm8[:, 0] = np.where(kk <= qq, 0.0, NEG)
    cm8[:, 1] = np.where(kk < qq, 0.0, NEG)
    cm8 = cm8.reshape(8, 2, 64)
    pp = np.arange(128)
    oh4 = (pp[:, None] // 32 == np.arange(4)[None, :]).astype(np.float32)
    pmod = (pp % 32).astype(np.float32).reshape(128, 1)
    blk = 2 * np.arange(16)[None, :] + pp[:, None] // 64
    zsel2 = (blk[:, :, None] == np.arange(32)[None, None, :]).astype(np.float32) / 256.0
    eg = (np.arange(32)[:, None, None] == blk.T[None, :, :]).astype(np.float32)
    bf = ml_dtypes.bfloat16
    return c.astype(bf), eb.astype(bf), cm8.astype(bf), oh4, pmod, zsel2.astype(bf), eg.astype(bf)


def _rope_table(pos):
    half = 32
    inv = (10000.0 ** (-np.arange(half, dtype=np.float32) / half)).astype(np.float32)
    ang = pos.astype(np.float32)[:, None] * inv[None, :]
    return np.concatenate([np.cos(ang), np.sin(ang)], axis=1).astype(np.float32)


def kernel(x_prompt, x_sample, cache_moba_k, cache_moba_v, cache_sb_k, cache_sb_v,
           page_table, g_mix, w_in, w_branch_moba, w_branch_sb, w_out,
           g_ffn, w_ffn_gate, w_ffn_up, w_ffn_down, g_final):
    f = lambda a: np.ascontiguousarray(np.asarray(a))
    if "nc" not in _NC_CACHE:
        _NC_CACHE["nc"] = build_nc()
    nc = _NC_CACHE["nc"]
    cbf, ebk, cm8, oh4, pmod, zsel2, eg = _consts()
    past_len = page_table.shape[1] * 128
    pos = np.concatenate([np.arange(SEQ), np.tile(past_len + np.arange(8), 4)])
    cs_all = _rope_table(pos)
    caches = [f(c).reshape(NPOOL * 128, 512) for c in (cache_moba_k, cache_moba_v, cache_sb_k, cache_sb_v)]
    shared = {
        "cmk": caches[0], "cmv": caches[1], "csk": caches[2], "csv": caches[3],
        "w_in": f(w_in)[0], "w_bm": f(w_branch_moba)[0], "w_bs": f(w_branch_sb)[0], "w_out": f(w_out)[0],
        "w_g": f(w_ffn_gate)[0], "w_u": f(w_ffn_up)[0], "w_d": f(w_ffn_down)[0],
        "gmixT": f(f(g_mix)[0].reshape(8, 128).T), "gffnT": f(f(g_ffn)[0].reshape(8, 128).T),
        "gfin": f(g_final).reshape(1, D), "cs_all": cs_all, "cbf": cbf, "ebk": ebk,
        "cm8": cm8, "oh4": oh4, "pmod": pmod, "zsel2": zsel2, "eg": eg,
    }
    xpn, xsn, ptn = f(x_prompt), f(x_sample), f(page_table).astype(np.int32)
    in_maps = []
    for c in range(NCORES):
        m = dict(shared)
        m["xp"] = xpn[c]
        m["xs"] = xsn[4 * c:4 * c + 4].reshape(ST, D)
        m["pt"] = ptn[4 * c:4 * c + 4]
        in_maps.append(m)
    res = run_bass_kernel_spmd(nc, in_maps, core_ids=list(range(NCORES)))
    R = res.results
    y_prompt = np.stack([R[c]["yp"] for c in range(NCORES)]).astype(np.float32)
    y_sample = np.concatenate([R[c]["ys"].reshape(4, 8, D) for c in range(NCORES)]).astype(np.float32)
    outs = [y_prompt, y_sample]
    for nm in ("mk", "mv", "sk", "sv"):
        outs.append(np.stack([R[c][nm + "p"].reshape(SEQ, 8, 64) for c in range(NCORES)])[None].astype(np.float32))
    for nm in ("mk", "mv", "sk", "sv"):
        outs.append(np.concatenate([R[c][nm + "s"].reshape(4, 8, 8, 64) for c in range(NCORES)])[None].astype(np.float32))
    return tuple(outs)
```
